# Optimizing a Trainium2 kernel written in Bass

```python
import math
import jax, jax.numpy as jnp
from jax import lax
import numpy as np

D_MODEL = 4096
BATCH = 4
SEQ = 4096
DEPTH = 1

N_MEM = 256
EPS = 1e-6
ROPE_THETA = 500000.0
MIX_WIDTH = D_MODEL
HEAD_DIM = 128
ROT_DIM = HEAD_DIM // 4
GDN_DK = 128
GDN_DV = 128
GDN_HEADS = (MIX_WIDTH // 2) // GDN_DV
GDN_CONV = 4
GDN_CHUNK = 64
GDN_QK = GDN_HEADS * GDN_DK
GDN_V = GDN_HEADS * GDN_DV
ATT_HEADS = (MIX_WIDTH // 2) // HEAD_DIM
ATT_KV_HEADS = 4
IDX_HEADS = 32
IDX_DIM = 128
IDX_ROT = IDX_DIM // 4
TOPK_MAX = 256
Q_BLOCK = 128
MEM_HEADS = 4
MEM_HEAD_DIM = 128
MEM_WIDTH = MEM_HEADS * MEM_HEAD_DIM
FFN_HIDDEN = -(-8 * D_MODEL // (3 * 256)) * 256
IN_SIZES = (GDN_QK, GDN_QK, GDN_V, GDN_V, GDN_HEADS, GDN_HEADS,
            ATT_HEADS * HEAD_DIM, ATT_KV_HEADS * HEAD_DIM, ATT_KV_HEADS * HEAD_DIM,
            IDX_HEADS * IDX_DIM, IDX_DIM, IDX_HEADS)
D_IN = sum(IN_SIZES)

kernel_name = "hybrid_gdn_dsa_parallel_heads"


def rms_norm(x, g):
    xf = x.astype(jnp.float32)
    y = xf * lax.rsqrt(jnp.mean(xf * xf, axis=-1, keepdims=True) + EPS)
    return (y * g.astype(jnp.float32)).astype(x.dtype)


def l2_norm(x):
    xf = x.astype(jnp.float32)
    return xf * lax.rsqrt(jnp.sum(xf * xf, axis=-1, keepdims=True) + EPS)


def partial_rotary(x, pos, rot_dim):
    half = rot_dim // 2
    inv_freq = ROPE_THETA ** (-jnp.arange(half, dtype=jnp.float32) * 2.0 / rot_dim)
    ang = pos.astype(jnp.float32)[:, None] * inv_freq[None, :]
    cos = jnp.cos(ang)[:, None, :]
    sin = jnp.sin(ang)[:, None, :]
    xf = x.astype(jnp.float32)
    x1, x2 = xf[..., :half], xf[..., half:rot_dim]
    out = jnp.concatenate([x1 * cos - x2 * sin, x2 * cos + x1 * sin, xf[..., rot_dim:]], axis=-1)
    return out.astype(x.dtype)


def causal_depthwise_conv(x, w):
    K = w.shape[0]
    return lax.conv_general_dilated(x, w[:, None, :], window_strides=(1,), padding=[(K - 1, 0)],
                                    dimension_numbers=('NWC', 'WIO', 'NWC'),
                                    feature_group_count=x.shape[-1])


def gated_delta_rule_chunked(q, k, v, g, beta):
    B, S, H, dk = q.shape
    dv = v.shape[-1]
    C = GDN_CHUNK
    N = S // C

    def chunks(t):
        t = t.astype(jnp.float32).reshape((B, N, C) + t.shape[2:])
        return jnp.moveaxis(t, 2, 3)

    q, k, v, g, beta = chunks(q), chunks(k), chunks(v), chunks(g), chunks(beta)
    g = jnp.cumsum(g, axis=-1)
    tri_incl = jnp.tril(jnp.ones((C, C), dtype=bool))
    tri_strict = jnp.tril(jnp.ones((C, C), dtype=bool), -1)
    decay = jnp.exp(jnp.where(tri_incl, g[..., :, None] - g[..., None, :], -jnp.inf))
    k_beta = k * beta[..., None]
    v_beta = v * beta[..., None]
    lower = jnp.where(tri_strict, jnp.einsum('bnhcd,bnhed->bnhce', k_beta, k) * decay, 0.0)
    a_mat = lower + jnp.eye(C, dtype=jnp.float32)
    rhs = jnp.concatenate([v_beta, k_beta * jnp.exp(g)[..., None]], axis=-1)
    sol = lax.linalg.triangular_solve(a_mat, rhs, left_side=True, lower=True, unit_diagonal=True)
    u, w = sol[..., :dv], sol[..., dv:]
    qk = jnp.einsum('bnhcd,bnhed->bnhce', q, k) * decay
    q_dec = q * jnp.exp(g)[..., None]
    g_last = g[..., -1]
    k_dec = k * jnp.exp(g_last[..., None] - g)[..., None]

    def step(state, xs):
        u_c, w_c, qk_c, qd_c, kd_c, gl_c = xs
        v_new = u_c - jnp.einsum('bhcd,bhdv->bhcv', w_c, state)
        o = jnp.einsum('bhcd,bhdv->bhcv', qd_c, state) + jnp.einsum('bhce,bhev->bhcv', qk_c, v_new)
        state = state * jnp.exp(gl_c)[..., None, None] + jnp.einsum('bhcd,bhcv->bhdv', kd_c, v_new)
        return state, o

    xs = tuple(jnp.moveaxis(t, 1, 0) for t in (u, w, qk, q_dec, k_dec, g_last))
    state0 = jnp.zeros((B, H, dk, dv), jnp.float32)
    _, o = lax.scan(step, state0, xs)
    return o.transpose(1, 0, 3, 2, 4).reshape(B, S, H, dv)


def dsa_attention(q, k, v, q_idx, k_idx, w_idx, topk):
    B, S, H, d = q.shape
    G = H // ATT_KV_HEADS
    nblk = S // Q_BLOCK
    scale = d ** -0.5
    idx_scale = IDX_DIM ** -0.5
    key_pos = jnp.arange(S)
    k_idx_f = k_idx.astype(jnp.float32)
    gather = jax.vmap(lambda src, ix: src[ix])

    def blocks(t):
        return jnp.moveaxis(t.reshape((B, nblk, Q_BLOCK) + t.shape[2:]), 1, 0)

    def one_block(args):
        blk, q_blk, qi_blk, wi_blk = args
        t = blk * Q_BLOCK + jnp.arange(Q_BLOCK)
        logits = jnp.einsum('bqhd,bsd->bqhs', qi_blk.astype(jnp.float32), k_idx_f) * idx_scale
        score = jnp.einsum('bqhs,bqh->bqs', jax.nn.relu(logits), wi_blk.astype(jnp.float32))
        causal = key_pos[None, :] <= t[:, None]
        score = jnp.where(causal[None], score, -jnp.inf)
        _, sel = lax.top_k(score, topk)
        valid = sel <= t[None, :, None]
        k_sel = gather(k, sel).astype(jnp.float32)
        v_sel = gather(v, sel).astype(jnp.float32)
        s = jnp.einsum('bqhgd,bqkhd->bqhgk', q_blk.astype(jnp.float32), k_sel) * scale
        s = jnp.where(valid[:, :, None, None, :], s, -jnp.inf)
        p = jax.nn.softmax(s, axis=-1)
        return jnp.einsum('bqhgk,bqkhd->bqhgd', p, v_sel).astype(q.dtype)

    qg = q.reshape(B, S, ATT_KV_HEADS, G, d)
    out = lax.map(one_block, (jnp.arange(nblk), blocks(qg), blocks(q_idx), blocks(w_idx)))
    return jnp.moveaxis(out, 0, 1).reshape(B, S, H * d)


def setup_inputs(seed: int = 0) -> dict:
    key = jax.random.key(seed)
    ks = jax.random.split(key, 24)
    f32 = jnp.float32
    L = DEPTH

    def dense(k, fan_in, shape):
        return jax.random.normal(k, shape, f32) * fan_in ** -0.5

    def gain(k, shape):
        return 1.0 + 0.02 * jax.random.normal(k, shape, f32)

    dt = jnp.exp(jax.random.uniform(ks[6], (L, GDN_HEADS), f32, math.log(1e-3), math.log(1e-1)))
    return {
        "x": jax.random.normal(ks[0], (BATCH, SEQ, D_MODEL), f32),
        "mem": jax.random.normal(ks[1], (BATCH, N_MEM, D_MODEL), f32),
        "norm_mix": gain(ks[2], (L, D_MODEL)),
        "w_in": dense(ks[3], D_MODEL, (L, D_MODEL, D_IN)),
        "conv_w": dense(ks[4], GDN_CONV, (L, GDN_CONV, 2 * GDN_QK + GDN_V)),
        "a_log": jnp.log(jax.random.uniform(ks[5], (L, GDN_HEADS), f32, 1.0, 16.0)),
        "dt_bias": dt + jnp.log(-jnp.expm1(-dt)),
        "gdn_norm": gain(ks[7], (L, GDN_DV)),
        "att_q_norm": gain(ks[8], (L, HEAD_DIM)),
        "att_k_norm": gain(ks[9], (L, HEAD_DIM)),
        "w_out": dense(ks[10], MIX_WIDTH, (L, MIX_WIDTH, D_MODEL)),
        "norm_mem_q": gain(ks[11], (L, D_MODEL)),
        "norm_mem_kv": gain(ks[12], (L, D_MODEL)),
        "w_mem_q": dense(ks[13], D_MODEL, (L, D_MODEL, MEM_WIDTH)),
        "w_mem_kv": dense(ks[14], D_MODEL, (L, D_MODEL, 2 * MEM_WIDTH)),
        "mem_q_norm": gain(ks[15], (L, MEM_HEAD_DIM)),
        "mem_k_norm": gain(ks[16], (L, MEM_HEAD_DIM)),
        "w_mem_o": dense(ks[17], MEM_WIDTH, (L, MEM_WIDTH, D_MODEL)),
        "norm_ffn": gain(ks[18], (L, D_MODEL)),
        "w_gate": dense(ks[19], D_MODEL, (L, D_MODEL, FFN_HIDDEN)),
        "w_up": dense(ks[20], D_MODEL, (L, D_MODEL, FFN_HIDDEN)),
        "w_down": dense(ks[21], FFN_HIDDEN, (L, FFN_HIDDEN, D_MODEL)),
    }


def reference(x, mem, norm_mix, w_in, conv_w, a_log, dt_bias, gdn_norm, att_q_norm, att_k_norm,
              w_out, norm_mem_q, norm_mem_kv, w_mem_q, w_mem_kv, mem_q_norm, mem_k_norm, w_mem_o,
              norm_ffn, w_gate, w_up, w_down):
    B, S, _ = x.shape
    M = mem.shape[1]
    topk = min(TOPK_MAX, S // 4)
    pos = jnp.arange(S)
    split_at = [int(i) for i in np.cumsum(IN_SIZES)[:-1]]
    for l in range(DEPTH):
        h = rms_norm(x, norm_mix[l])
        proj = h @ w_in[l]
        gq, gk, gv, gz, ga, gb, aq, ak, av, iq, ik, iw = jnp.split(proj, split_at, axis=-1)

        qkv = jax.nn.silu(causal_depthwise_conv(jnp.concatenate([gq, gk, gv], axis=-1), conv_w[l]))
        gq, gk, gv = jnp.split(qkv, [GDN_QK, 2 * GDN_QK], axis=-1)
        gq = l2_norm(gq.reshape(B, S, GDN_HEADS, GDN_DK)) * GDN_DK ** -0.5
        gk = l2_norm(gk.reshape(B, S, GDN_HEADS, GDN_DK))
        gv = gv.reshape(B, S, GDN_HEADS, GDN_DV)
        beta = jax.nn.sigmoid(gb.astype(jnp.float32))
        g = -jnp.exp(a_log[l].astype(jnp.float32)) * jax.nn.softplus(
            ga.astype(jnp.float32) + dt_bias[l].astype(jnp.float32))
        o_a = gated_delta_rule_chunked(gq, gk, gv, g, beta)
        o_a = rms_norm(o_a, gdn_norm[l]) * jax.nn.silu(
            gz.reshape(B, S, GDN_HEADS, GDN_DV).astype(jnp.float32))
        o_a = o_a.reshape(B, S, GDN_V).astype(x.dtype)

        aq = partial_rotary(rms_norm(aq.reshape(B, S, ATT_HEADS, HEAD_DIM), att_q_norm[l]), pos, ROT_DIM)
        ak = partial_rotary(rms_norm(ak.reshape(B, S, ATT_KV_HEADS, HEAD_DIM), att_k_norm[l]), pos, ROT_DIM)
        av = av.reshape(B, S, ATT_KV_HEADS, HEAD_DIM)
        iq = partial_rotary(iq.reshape(B, S, IDX_HEADS, IDX_DIM), pos, IDX_ROT)
        ik = partial_rotary(ik[:, :, None, :], pos, IDX_ROT)[:, :, 0]
        iw = iw * IDX_HEADS ** -0.5
        o_b = dsa_attention(aq, ak, av, iq, ik, iw, topk)

        x = x + jnp.concatenate([o_a, o_b], axis=-1) @ w_out[l]

        hq = rms_norm(x, norm_mem_q[l])
        hm = rms_norm(mem, norm_mem_kv[l])
        mq = rms_norm((hq @ w_mem_q[l]).reshape(B, S, MEM_HEADS, MEM_HEAD_DIM), mem_q_norm[l])
        mk, mv = jnp.split(hm @ w_mem_kv[l], 2, axis=-1)
        mk = rms_norm(mk.reshape(B, M, MEM_HEADS, MEM_HEAD_DIM), mem_k_norm[l])
        mv = mv.reshape(B, M, MEM_HEADS, MEM_HEAD_DIM)
        s = jnp.einsum('bshd,bmhd->bhsm', mq.astype(jnp.float32), mk.astype(jnp.float32)) * MEM_HEAD_DIM ** -0.5
        p = jax.nn.softmax(s, axis=-1)
        mo = jnp.einsum('bhsm,bmhd->bshd', p, mv.astype(jnp.float32)).reshape(B, S, MEM_WIDTH).astype(x.dtype)
        x = x + mo @ w_mem_o[l]

        hf = rms_norm(x, norm_ffn[l])
        x = x + (jax.nn.silu(hf @ w_gate[l]) * (hf @ w_up[l])) @ w_down[l]
    return x
```

```python
import numpy as np
import concourse.bass as bass
import concourse.mybir as mybir

F32 = mybir.dt.float32
BF16 = mybir.dt.bfloat16
AF = mybir.ActivationFunctionType
ALU = mybir.AluOpType
AX = mybir.AxisListType


class Buf:
    __slots__ = ("t", "w", "r", "dsem", "dcnt", "name", "dram", "psum")

    def __init__(self, t, name=""):
        self.t = t
        self.w = None
        self.r = {}
        self.dsem = None
        self.dcnt = 0
        self.name = name
        self.dram = False
        self.psum = False

    def __getitem__(self, key):
        return self.t[key]


class K:
    def __init__(self, nc):
        self.nc = nc
        self.engs = {"pe": nc.tensor, "act": nc.scalar, "dve": nc.vector,
                     "pool": nc.gpsimd, "sp": nc.sync}
        self.sem = {e: nc.alloc_semaphore(name="s_" + e) for e in self.engs}
        self.cnt = {e: 0 for e in self.engs}
        self.pending = {e: False for e in self.engs}
        self.waited = {e: {} for e in self.engs}
        self.dsems = []
        self.nwait = 0
        self.nins = 0

    def sb(self, es, name, shape, dtype):
        self.uid = getattr(self, "uid", 0) + 1
        t = es.enter_context(self.nc.sbuf_tensor(f"sb{self.uid}_{name}", list(shape), dtype))
        return Buf(t, name)

    def ps(self, es, name, shape=(128, 512), dtype=F32):
        t = es.enter_context(self.nc.psum_tensor(name, list(shape), dtype))
        b = Buf(t, name)
        b.psum = True
        return b

    def dram(self, name, shape, dtype, kind="Internal"):
        t = self.nc.dram_tensor(name, list(shape), dtype, kind=kind)
        b = Buf(t.ap(), name)
        b.dram = True
        return b

    def _deps(self, eng, reads, writes, skip=None, strict=False):
        need = {}
        own = self.sem[eng]

        def add(p, raw):
            if p is None:
                return
            s, c = p
            if (s is own) and (not raw) and (not strict):
                return
            if need.get(s, 0) < c:
                need[s] = c
        for b in reads:
            add(b.w, True)
            if b.psum:
                for s, c in b.r.items():
                    add((s, c), False)
        for b in writes:
            add(b.w, False)
            for s, c in b.r.items():
                add((s, c), False)
        e = self.engs[eng]
        for s, c in need.items():
            if eng == "pe" and s is own:
                continue
            if skip is not None and s is skip:
                continue
            if self.waited[eng].get(s, 0) < c:
                e.wait_ge(s, c)
                self.nwait += 1
                self.waited[eng][s] = c

    def op(self, eng, fn, reads=(), writes=(), sig=True):
        self._deps(eng, reads, writes)
        ins = fn(self.engs[eng])
        self.nins += 1
        if sig:
            self.cnt[eng] += 1
            ins.then_inc(self.sem[eng], 1)
            p = (self.sem[eng], self.cnt[eng])
            self.pending[eng] = False
        else:
            p = (self.sem[eng], self.cnt[eng] + 1)
            self.pending[eng] = True
        s, c = p
        for b in reads:
            if b.r.get(s, 0) < c:
                b.r[s] = c
        for b in writes:
            b.w = p
            b.r = {}
        return ins

    def dma(self, q, out_ap, in_ap, reads, writes, semb, **kw):
        if getattr(semb, "dram", False):
            srcs = [b for b in reads if not getattr(b, "dram", False)]
            assert srcs
            semb = srcs[0]
        if semb.dsem is None:
            semb.dsem = self.nc.alloc_semaphore(name=f"d{len(self.dsems)}_" + semb.name)
            self.dsems.append(semb)
        self._deps(q, reads, writes, skip=semb.dsem, strict=True)
        ins = self.engs[q].dma_start(out=out_ap, in_=in_ap, **kw)
        self.nins += 1
        semb.dcnt += 16
        ins.then_inc(semb.dsem, 16)
        s, c = semb.dsem, semb.dcnt
        for b in reads:
            if b.r.get(s, 0) < c:
                b.r[s] = c
        for b in writes:
            b.w = (s, c)
            b.r = {}
        return ins

    def barrier(self):
        for e, eng in self.engs.items():
            for f in self.engs:
                if f == e:
                    continue
                c = self.cnt[f]
                if c > 0 and self.waited[e].get(self.sem[f], 0) < c:
                    eng.wait_ge(self.sem[f], c)
                    self.waited[e][self.sem[f]] = c
            for b in self.dsems:
                if b.dcnt > 0 and self.waited[e].get(b.dsem, 0) < b.dcnt:
                    eng.wait_ge(b.dsem, b.dcnt)
                    self.waited[e][b.dsem] = b.dcnt

    def finish(self, out_bufs):
        for e in self.engs:
            assert not self.pending[e], e
        self.barrier()

from contextlib import ExitStack
from concourse.bass_utils import run_bass_kernel_spmd

S = 4096; D = 4096; NOWN = 2048; DIN = 15552; FFN = 11008
TT = 512
C_GQ, C_GK, C_GV, C_GZ, C_GA, C_GB = 0, 2048, 4096, 6144, 8192, 8208
C_AQ, C_AK, C_AV, C_IQ, C_IK, C_IW = 8224, 10272, 10784, 11296, 15392, 15520


EPS = 1e-6
import os as _os
GDN_STAGE = int(_os.environ.get('GDN_STAGE', '99'))
CH_POOL = 'dve'


class WRing:
    def __init__(self, k, es, npieces=6, nst=2):
        self.k = k
        self.st = [k.sb(es, f"wst{i}", [128, 8, 512], F32) for i in range(nst)]
        self.pc = [k.sb(es, f"wpc{i}", [128, 8, 512], BF16) for i in range(npieces)]
        self.si = self.pi = self.ci = 0

    def load(self, wbuf, wv, kc0, nk, c0, ncols, castengs):
        k = self.k
        st = self.st[self.si % len(self.st)]; self.si += 1
        pc = self.pc[self.pi % len(self.pc)]; self.pi += 1
        k.dma("sp", st[:, 0:nk, 0:ncols], wv[:, kc0:kc0 + nk, c0:c0 + ncols], [wbuf], [st], st)
        eng = castengs[self.ci % len(castengs)]; self.ci += 1
        if eng == "act":
            k.op("act", lambda e: e.copy(out=pc[:, 0:nk, 0:ncols], in_=st[:, 0:nk, 0:ncols]), [st], [pc])
        else:
            k.op(eng, lambda e: e.tensor_copy(pc[:, 0:nk, 0:ncols], st[:, 0:nk, 0:ncols]), [st], [pc])
        return pc


def linear(k, ring, jobs, P, prep, castengs=("dve", "act"), LA=3):
    items = []
    for ji, jb in enumerate(jobs):
        for pi in range((jb["KC"] + 7) // 8):
            items.append((ji, pi))
    pcs = {}

    def issue(idx):
        ji, pi = items[idx]
        jb = jobs[ji]
        kc0 = pi * 8
        nk = min(8, jb["KC"] - kc0)
        pcs[idx] = ring.load(jb["wbuf"], jb["wv"], kc0, nk, jb["c0"], jb["ncols"], castengs)
    for idx in range(min(LA, len(items))):
        issue(idx)
    cur_pass = None
    bank = 0
    for idx, (ji, pi) in enumerate(items):
        jb = jobs[ji]
        if pi == 0:
            if jb["pass_id"] != cur_pass:
                cur_pass = jb["pass_id"]
                prep(cur_pass)
            jb["_banks"] = P[bank * 4:(bank + 1) * 4]
            bank = (bank + 1) % (len(P) // 4)
        if idx + LA < len(items):
            issue(idx + LA)
        pc = pcs.pop(idx)
        KC = jb["KC"]; ncols = jb["ncols"]; kc0 = pi * 8; nk = min(8, KC - kc0)
        hT, hTb = jb["hT"]
        banks = jb["_banks"]
        nsub = jb.get("nsub", 4)
        ntok = nsub * 128
        nq = nsub if jb["mode"] == "N" else ncols // 128
        for q in range(nq):
            ps = banks[q]
            for cc in range(nk):
                c = kc0 + cc
                if jb["mode"] == "N":
                    k.op("pe", lambda e: e.matmul(ps[:, 0:ncols], lhsT=hT[:, c, q * 128:(q + 1) * 128],
                                                  rhs=pc[:, cc, 0:ncols], start=(c == 0), stop=(c == KC - 1)),
                         [hTb, pc], [ps], sig=(cc == nk - 1))
                else:
                    k.op("pe", lambda e: e.matmul(ps[:, 0:ntok], lhsT=pc[:, cc, q * 128:(q + 1) * 128],
                                                  rhs=hT[:, c, 0:ntok], start=(c == 0), stop=(c == KC - 1)),
                         [hTb, pc], [ps], sig=(cc == nk - 1))
        if kc0 + nk == KC:
            for q in range(nq):
                jb["evac"](q, banks[q], ncols)


def build_hT(k, x_dram, row0, xt, hT, hTb, ssq, rstd, junk, gam, ident, cstb, pst, n_sub=4, col0=0):
    for sub in range(n_sub):
        xb = xt[sub % len(xt)]
        r0 = row0 + sub * 128
        k.dma("sp", xb[:, :], x_dram[r0:r0 + 128, col0:col0 + D], [x_dram], [xb], xb)
        k.op("dve", lambda e: e.memset(ssq[:, 0:1], 0.0), [], [ssq])
        k.op("act", lambda e: e.activation(out=junk[:, :], in_=xb[:, :], func=AF.Square,
                                           accum_out=ssq[:, 0:1]), [xb, ssq], [junk, ssq])
        k.op("act", lambda e: e.activation(out=rstd[:, 0:1], in_=ssq[:, 0:1], func=AF.Sqrt, scale=1.0 / D, bias=EPS),
             [ssq], [rstd])
        k.op("dve", lambda e: e.reciprocal(rstd[:, 0:1], rstd[:, 0:1]), [rstd], [rstd])
        k.op("dve", lambda e: e.tensor_scalar(xb[:, :], xb[:, :], rstd[:, 0:1], None, ALU.mult),
             [xb, rstd], [xb])
        for g in range(8):
            ps = pst[g % len(pst)]
            for j in range(4):
                c = g * 4 + j
                k.op("pe", lambda e: e.transpose(out=ps[:, j * 128:(j + 1) * 128],
                                                 in_=xb[:, c * 128:(c + 1) * 128], identity=ident),
                     [xb, cstb], [ps], sig=(j == 3))
            k.op("dve", lambda e: e.tensor_tensor(
                out=hT[:, g * 4:(g + 1) * 4, sub * 128:(sub + 1) * 128],
                in0=ps[:, :].rearrange("p (a b) -> p a b", a=4),
                in1=gam[:, g * 4:(g + 1) * 4].unsqueeze(2).to_broadcast([128, 4, 128]),
                op=ALU.mult), [ps, gam], [hTb])


def mk_jobs(jobs, pass_id, mode, wbuf, wv, KC, c0, n, hT, evac_factory, nsub=4):
    off = 0
    while off < n:
        nc_ = min(512, n - off)
        jobs.append(dict(pass_id=pass_id, mode=mode, wbuf=wbuf, wv=wv, KC=KC, c0=c0 + off, ncols=nc_,
                         hT=hT, evac=evac_factory(off), nsub=nsub))
        off += nc_


def phase1(k, X, P, C):
    with ExitStack() as es:
        xt = [k.sb(es, f"xt{i}", [128, D], F32) for i in range(2)]
        junk = k.sb(es, "junk", [128, D], BF16)
        hTt = es.enter_context(k.nc.sbuf_tensor("hT", [128, 32, 512], BF16))
        hTb = Buf(hTt, "hT")
        ssq = k.sb(es, "ssq", [128, 1], F32)
        rstd = k.sb(es, "rstd", [128, 1], F32)
        gam = k.sb(es, "gam", [128, 32], F32)
        ot = [k.sb(es, f"ot{i}", [128, 512], F32) for i in range(4)]
        oi = [0]
        ring = WRing(k, es)
        k.dma("sp", gam[:, :], X["norm_mix"][:, :], [X["norm_mix"]], [gam], gam)
        wv = X["w_in"].t.rearrange("(c p) n -> p c n", p=128)
        jobs = []
        passes = {}

        def fac(mode, dst, row0, dcol0):
            def f(off):
                def evac(q, ps, ncols):
                    o = ot[oi[0] % 4]; oi[0] += 1
                    w = ncols if mode == "N" else 512
                    k.op("act", lambda e: e.copy(out=o[:, 0:w], in_=ps[:, 0:w]), [ps], [o])
                    if mode == "N":
                        r = row0 + q * 128
                        dc = dcol0 + off
                        k.dma("sp", dst[r:r + 128, dc:dc + ncols], o[:, 0:ncols], [o], [dst], dst)
                    else:
                        r = dcol0 + off + q * 128
                        k.dma("sp", dst[r:r + 128, row0:row0 + 512], o[:, 0:512], [o], [dst], dst)
                return evac
            return f

        for t in range(C["n_all_tiles"]):
            pid = ("all", t)
            passes[pid] = (X["x_all"], t * 512)
            r0 = t * 512
            mk_jobs(jobs, pid, "T", X["w_in"], wv, 32, C_GQ, 6144, (hTt, hTb), fac("T", X["qkvT"], r0, 0))
            mk_jobs(jobs, pid, "N", X["w_in"], wv, 32, C_GA, 32, (hTt, hTb), fac("N", X["gab"], r0, 0))
            mk_jobs(jobs, pid, "N", X["w_in"], wv, 32, C_AK, 1024, (hTt, hTb), fac("N", X["akv"], r0, 0))
            mk_jobs(jobs, pid, "N", X["w_in"], wv, 32, C_IK, 128, (hTt, hTb), fac("N", X["ikr"], r0, 0))
        for t in range(C["n_own_tiles"]):
            pid = ("own", t)
            passes[pid] = (X["x_own"], t * 512)
            r0 = t * 512
            mk_jobs(jobs, pid, "N", X["w_in"], wv, 32, C_GZ, 2048, (hTt, hTb), fac("N", X["gz"], r0, 0))
            mk_jobs(jobs, pid, "N", X["w_in"], wv, 32, C_AQ, 2048, (hTt, hTb), fac("N", X["aq"], r0, 0))
            mk_jobs(jobs, pid, "N", X["w_in"], wv, 32, C_IQ, 4096, (hTt, hTb), fac("N", X["iq"], r0, 0))
            mk_jobs(jobs, pid, "N", X["w_in"], wv, 32, C_IW, 32, (hTt, hTb), fac("N", X["iw"], r0, 0))

        def prep(pid):
            xd, r0 = passes[pid]
            build_hT(k, xd, r0, xt, hTt, hTb, ssq, rstd, junk, gam, C["ident"], C["cst"], P[6:8])

        linear(k, ring, jobs, P, prep)
    k.barrier()


def phase2(k, X, P, C):
    cst = C["cst"]
    ident = C["ident"]
    tri = cst.t[:, 128:256]
    tris = cst.t[:, 256:384]
    ones = cst.t[:, 384:512]
    NCH = C["n_chunks"]
    NH = C["n_gdn_heads"]
    with ExitStack() as es:
        raw = [k.sb(es, f"raw{i}", [128, 1027], F32) for i in range(3)]
        y = [k.sb(es, f"y{i}", [128, 1024], F32) for i in range(3)]
        sq = k.sb(es, "sq", [128, 1024], F32)
        rn = k.sb(es, "rn", [128, 1024], F32)
        cw = k.sb(es, "cw", [128, 48, 4], F32)
        oh = k.sb(es, "oh", [16, 2048], F32)
        alog = k.sb(es, "alog", [128, 16], F32)
        dtb = k.sb(es, "dtb", [128, 16], F32)
        nea = k.sb(es, "nea", [128, 16], F32)
        G = {n: k.sb(es, "g_" + n, [128, 32, 16], F32) for n in ("gcum", "eg_unused", "edec", "egl", "nbeta", "beta")}
        gcumT = k.sb(es, "gcumT", [16, 32, 128], F32)
        gtmp = [k.sb(es, f"gtmp{i}", [128, 32], F32) for i in range(4)]
        St = k.sb(es, "S", [128, 128], F32)
        W = {}
        for n in ("Dm", "DT", "Eg", "DTs", "DTi", "QKm", "M0", "kdec", "vtok", "kegT", "qdT", "r", "vnew", "osb", "ZT"):
            W[n] = k.sb(es, "w_" + n, [128, 128], F32)
        OSB = [k.sb(es, f"osb{i}", [128, 128], F32) for i in range(2)]
        NY = [k.sb(es, f"NY{i}", [128, 256], F32) for i in range(2)]
        Mb = [k.sb(es, f"Mb{i}", [128, 128], F32) for i in range(2)]

        k.dma("sp", cw[:, :, :], X["cw"][:, :, :], [X["cw"]], [cw], cw)
        k.dma("sp", oh[:, :], X["consts2"][:, :], [X["consts2"]], [oh], oh)
        k.dma("sp", alog[:, :], X["a_log"].t.partition_broadcast(128), [X["a_log"]], [alog], alog)
        k.dma("sp", dtb[:, :], X["dt_bias"].t.partition_broadcast(128), [X["dt_bias"]], [dtb], dtb)
        k.op("act", lambda e: e.activation(out=nea[:, :], in_=alog[:, :], func=AF.Exp), [alog], [nea])
        k.op("dve", lambda e: e.tensor_scalar(nea[:, :], nea[:, :], -1.0, None, ALU.mult), [nea], [nea])

        for ch in range(NCH):
            t0 = ch * 128
            ga = gtmp[0]; t1 = gtmp[1]; t2 = gtmp[2]; g = gtmp[3]
            k.dma("sp", ga[:, :], X["gab"][t0:t0 + 128, :], [X["gab"]], [ga], ga)
            k.op("dve", lambda e: e.tensor_tensor(out=t1[:, 0:16], in0=ga[:, 0:16], in1=dtb[:, :], op=ALU.add),
                 [ga, dtb], [t1])
            k.op("act", lambda e: e.activation(out=t1[:, 0:16], in_=t1[:, 0:16], func=AF.Exp), [t1], [t1])
            k.op("act", lambda e: e.activation(out=t1[:, 0:16], in_=t1[:, 0:16], func=AF.Ln, bias=1.0), [t1], [t1])
            k.op("dve", lambda e: e.tensor_tensor(out=g[:, 0:16], in0=t1[:, 0:16], in1=nea[:, :], op=ALU.mult),
                 [t1, nea], [g])
            k.op("act", lambda e: e.activation(out=t2[:, 0:16], in_=ga[:, 16:32], func=AF.Exp, scale=-1.0), [ga], [t2])
            k.op("dve", lambda e: e.tensor_scalar(t2[:, 0:16], t2[:, 0:16], 1.0, None, ALU.add), [t2], [t2])
            k.op("dve", lambda e: e.reciprocal(G["beta"][:, ch, :], t2[:, 0:16]), [t2], [G["beta"]])
            k.op("dve", lambda e: e.tensor_scalar(G["nbeta"][:, ch, :], G["beta"][:, ch, :], -1.0, None, ALU.mult),
                 [G["beta"]], [G["nbeta"]])
            ps = P[ch % 2]
            k.op("pe", lambda e: e.matmul(ps[:, 0:16], lhsT=tri, rhs=g[:, 0:16], start=True, stop=True), [cst, g], [ps], sig=False)
            k.op("pe", lambda e: e.matmul(ps[:, 16:32], lhsT=ones, rhs=g[:, 0:16], start=True, stop=True), [cst, g], [ps])
            k.op("act", lambda e: e.copy(out=G["gcum"][:, ch, :], in_=ps[:, 0:16]), [ps], [G["gcum"]])
            k.op("act", lambda e: e.activation(out=G["egl"][:, ch, :], in_=ps[:, 16:32], func=AF.Exp), [ps], [G["egl"]])
            k.op("dve", lambda e: e.tensor_tensor(out=t2[:, 16:32], in0=ps[:, 16:32], in1=G["gcum"][:, ch, :], op=ALU.subtract),
                 [ps, G["gcum"]], [t2])
            k.op("act", lambda e: e.activation(out=G["edec"][:, ch, :], in_=t2[:, 16:32], func=AF.Exp), [t2], [G["edec"]])
            ps2 = P[2 + ch % 2]
            k.op("pe", lambda e: e.transpose(out=ps2[0:16, 0:128], in_=G["gcum"][:, ch, :], identity=ident),
                 [G["gcum"], cst], [ps2])
            k.op("act", lambda e: e.copy(out=gcumT[:, ch, :], in_=ps2[0:16, 0:128]), [ps2], [gcumT])

        pi = [0]

        def nps():
            p = P[pi[0] % 8]; pi[0] += 1
            return p

        for h in range(NH):
            k.op("dve", lambda e: e.memset(St[:, :], 0.0), [], [St])
            for tl in range((NCH + 7) // 8):
                tok0 = tl * 1024
                nch_t = min(8, NCH - tl * 8)
                ntk = nch_t * 128
                for gi in range(3):
                    row0 = gi * 2048 + h * 128
                    if tl == 0:
                        k.op("dve", lambda e: e.memset(raw[gi][:, 0:3], 0.0), [], [raw[gi]])
                        k.dma("sp", raw[gi][:, 3:3 + ntk], X["qkvT"][row0:row0 + 128, 0:ntk], [X["qkvT"]], [raw[gi]], raw[gi])
                    else:
                        k.dma("sp", raw[gi][:, 0:3 + ntk], X["qkvT"][row0:row0 + 128, tok0 - 3:tok0 + ntk],
                              [X["qkvT"]], [raw[gi]], raw[gi])
                    ce = "dve"
                    wi = gi * 16 + h
                    k.op(ce, lambda e: e.tensor_scalar(y[gi][:, 0:ntk], raw[gi][:, 0:ntk], cw[:, wi, 0:1], None, ALU.mult),
                         [raw[gi], cw], [y[gi]])
                    for kk in range(1, 4):
                        k.op(ce, lambda e: e.scalar_tensor_tensor(out=y[gi][:, 0:ntk], in0=raw[gi][:, kk:kk + ntk],
                                                                   scalar=cw[:, wi, kk:kk + 1], in1=y[gi][:, 0:ntk],
                                                                   op0=ALU.mult, op1=ALU.add),
                             [raw[gi], cw, y[gi]], [y[gi]])
                    k.op("act", lambda e: e.activation(out=y[gi][:, 0:ntk], in_=y[gi][:, 0:ntk], func=AF.Silu),
                         [y[gi]], [y[gi]])
                for gi in range(2):
                    k.op("dve", lambda e: e.tensor_tensor(out=sq[:, 0:ntk], in0=y[gi][:, 0:ntk], in1=y[gi][:, 0:ntk], op=ALU.mult),
                         [y[gi]], [sq])
                    for hf in range((ntk + 511) // 512):
                        w_ = min(512, ntk - hf * 512)
                        ps = nps()
                        k.op("pe", lambda e: e.matmul(ps[:, 0:w_], lhsT=ones, rhs=sq[:, hf * 512:hf * 512 + w_], start=True, stop=True),
                             [cst, sq], [ps])
                        sc_ = 128.0 if gi == 0 else 1.0
                        k.op("act", lambda e: e.activation(out=rn[:, hf * 512:hf * 512 + w_], in_=ps[:, 0:w_], func=AF.Sqrt,
                                                           scale=sc_, bias=sc_ * EPS), [ps], [rn])
                    k.op("dve", lambda e: e.reciprocal(rn[:, 0:ntk], rn[:, 0:ntk]), [rn], [rn])
                    k.op("dve", lambda e: e.tensor_tensor(out=y[gi][:, 0:ntk], in0=y[gi][:, 0:ntk], in1=rn[:, 0:ntk], op=ALU.mult),
                         [y[gi], rn], [y[gi]])
                for cl in range(nch_t):
                    ch = tl * 8 + cl
                    cs_ = slice(cl * 128, (cl + 1) * 128)
                    qn = y[0].t[:, cs_]; kn = y[1].t[:, cs_]; vv = y[2].t[:, cs_]
                    gc = G["gcum"].t[:, ch, h:h + 1]
                    psB = nps()
                    k.op("pe", lambda e: e.matmul(psB[:, 0:128], lhsT=oh[0:16, h * 128:(h + 1) * 128], rhs=gcumT[0:16, ch, :],
                                                  start=True, stop=True), [oh, gcumT], [psB])
                    k.op("dve", lambda e: e.tensor_scalar(W["Dm"][:, :], psB[:, 0:128], gc, 0.0, ALU.subtract, ALU.min),
                         [psB, G["gcum"]], [W["Dm"]])
                    k.op("act", lambda e: e.activation(out=W["DT"][:, :], in_=W["Dm"][:, :], func=AF.Exp), [W["Dm"]], [W["DT"]])
                    k.op("act", lambda e: e.activation(out=W["Eg"][:, :], in_=psB[:, 0:128], func=AF.Exp), [psB], [W["Eg"]])
                    if GDN_STAGE < 1:
                        continue
                    psK = nps()
                    k.op("pe", lambda e: e.matmul(psK[:, 0:128], lhsT=kn, rhs=kn, start=True, stop=True), [y[1]], [psK], sig=False)
                    k.op("pe", lambda e: e.matmul(psK[:, 128:256], lhsT=kn, rhs=qn, start=True, stop=True), [y[1], y[0]], [psK])
                    k.op(CH_POOL, lambda e: e.tensor_tensor(out=W["DTs"][:, :], in0=W["DT"][:, :], in1=tris, op=ALU.mult),
                         [W["DT"], cst], [W["DTs"]])
                    k.op(CH_POOL, lambda e: e.tensor_tensor(out=W["DTi"][:, :], in0=W["DT"][:, :], in1=tri, op=ALU.mult),
                         [W["DT"], cst], [W["DTi"]])
                    if GDN_STAGE < 2:
                        continue
                    N0 = NY[0]
                    k.op("dve", lambda e: e.scalar_tensor_tensor(out=N0[:, 0:128], in0=psK[:, 0:128],
                                                                 scalar=G["nbeta"].t[:, ch, h:h + 1], in1=W["DTs"][:, :],
                                                                 op0=ALU.mult, op1=ALU.mult),
                         [psK, G["nbeta"], W["DTs"]], [N0])
                    k.op("dve", lambda e: e.tensor_tensor(out=W["QKm"][:, :], in0=psK[:, 128:256], in1=W["DTi"][:, :], op=ALU.mult),
                         [psK, W["DTi"]], [W["QKm"]])
                    if GDN_STAGE < 3:
                        continue
                    psT = nps()
                    k.op("pe", lambda e: e.matmul(psT[:, 0:128], lhsT=N0[:, 0:128], rhs=ident, start=True, stop=True), [N0, cst], [psT])
                    if not _os.environ.get("SKIP_M0COPY"):
                        k.op(_os.environ.get("M0ENG", "act"), (lambda e: e.copy(out=Mb[0][:, :], in_=psT[:, 0:128])) if _os.environ.get("M0ENG", "act") == "act" else (lambda e: e.tensor_copy(Mb[0][:, :], psT[:, 0:128])), [psT], [Mb[0]])
                    if not _os.environ.get("GDN_SKIPY1"):
                        k.op(CH_POOL, lambda e: e.tensor_tensor(out=NY[1][:, 128:256], in0=N0[:, 0:128], in1=ident, op=ALU.add),
                             [N0, cst], [NY[1]])
                    if GDN_STAGE < 4:
                        continue
                    ps1 = nps()
                    LV0 = int(_os.environ.get("LV0", "9"))
                    k.op("pe", lambda e: e.matmul(ps1[:, 0:128], lhsT=Mb[0][:, :], rhs=N0[:, 0:128], start=True, stop=True),
                         [Mb[0], N0], [ps1], sig=(LV0 < 2))
                    if LV0 >= 2:
                        k.op("pe", lambda e: e.matmul(ps1[:, 128:256], lhsT=N0[:, 0:128], rhs=Mb[0][:, :], start=True, stop=True),
                             [Mb[0], N0], [ps1])
                    if LV0 >= 3:
                        k.op("act", lambda e: e.copy(out=NY[1][:, 0:128], in_=ps1[:, 0:128]), [ps1], [NY[1]])
                    if LV0 >= 4:
                        k.op("act", lambda e: e.copy(out=Mb[1][:, :], in_=ps1[:, 128:256]), [ps1], [Mb[1]])
                    if GDN_STAGE < 5:
                        continue
                    cur = 1
                    for lv in range(1, 6):
                        nyc = NY[cur]; nyn = NY[1 - cur]; mc = Mb[cur]; mn = Mb[1 - cur]
                        ps2 = nps()
                        k.op("pe", lambda e: e.matmul(ps2[:, 0:256], lhsT=mc[:, :], rhs=nyc[:, 0:256], start=True, stop=True),
                             [mc, nyc], [ps2], sig=False)
                        k.op("pe", lambda e: e.matmul(ps2[:, 256:384], lhsT=nyc[:, 0:128], rhs=mc[:, :], start=True, stop=True),
                             [mc, nyc], [ps2])
                        k.op("act", lambda e: e.copy(out=nyn[:, 0:128], in_=ps2[:, 0:128]), [ps2], [nyn])
                        k.op("dve", lambda e: e.tensor_tensor(out=nyn[:, 128:256], in0=ps2[:, 128:256], in1=nyc[:, 128:256], op=ALU.add),
                             [ps2, nyc], [nyn])
                        k.op("act", lambda e: e.copy(out=mn[:, :], in_=ps2[:, 256:384]), [ps2], [mn])
                        cur = 1 - cur
                    if GDN_STAGE < 6:
                        continue
                    ps3 = nps()
                    k.op("pe", lambda e: e.matmul(ps3[:, 0:128], lhsT=Mb[cur][:, :], rhs=NY[cur][:, 128:256], start=True, stop=True),
                         [Mb[cur], NY[cur]], [ps3])
                    k.op("dve", lambda e: e.tensor_tensor(out=W["ZT"][:, :], in0=ps3[:, 0:128], in1=NY[cur][:, 128:256], op=ALU.add),
                         [ps3, NY[cur]], [W["ZT"]])
                    if GDN_STAGE < 7:
                        continue
                    ps4 = nps()
                    k.op("pe", lambda e: e.transpose(out=ps4[:, 0:128], in_=kn, identity=ident), [y[1], cst], [ps4], sig=False)
                    k.op("pe", lambda e: e.transpose(out=ps4[:, 128:256], in_=vv, identity=ident), [y[2], cst], [ps4])
                    k.op("act", lambda e: e.activation(out=W["kdec"][:, :], in_=ps4[:, 0:128], func=AF.Copy,
                                                       scale=G["edec"].t[:, ch, h:h + 1]), [ps4, G["edec"]], [W["kdec"]])
                    k.op("act", lambda e: e.copy(out=W["vtok"][:, :], in_=ps4[:, 128:256]), [ps4], [W["vtok"]])
                    k.op(CH_POOL, lambda e: e.tensor_tensor(out=W["kegT"][:, :], in0=kn, in1=W["Eg"][:, :], op=ALU.mult),
                         [y[1], W["Eg"]], [W["kegT"]])
                    k.op(CH_POOL, lambda e: e.tensor_tensor(out=W["qdT"][:, :], in0=qn, in1=W["Eg"][:, :], op=ALU.mult),
                         [y[0], W["Eg"]], [W["qdT"]])
                    if GDN_STAGE < 8:
                        continue
                    ps5 = nps()
                    k.op("pe", lambda e: e.matmul(ps5[:, 0:128], lhsT=W["kegT"][:, :], rhs=St[:, :], start=True, stop=True),
                         [W["kegT"], St], [ps5])
                    k.op("dve", lambda e: e.scalar_tensor_tensor(out=W["r"][:, :], in0=ps5[:, 0:128], scalar=-1.0, in1=W["vtok"][:, :],
                                                                 op0=ALU.mult, op1=ALU.add),
                         [W["vtok"], ps5], [W["r"]])
                    k.op("pe", lambda e: e.matmul(ps5[:, 128:256], lhsT=W["ZT"][:, :], rhs=W["r"][:, :], start=True, stop=True),
                         [W["ZT"], W["r"]], [ps5])
                    k.op("act", lambda e: e.activation(out=W["vnew"][:, :], in_=ps5[:, 128:256], func=AF.Copy,
                                                       scale=G["beta"].t[:, ch, h:h + 1]), [ps5, G["beta"]], [W["vnew"]])
                    ps6 = nps()
                    k.op("pe", lambda e: e.matmul(ps6[:, 0:128], lhsT=W["qdT"][:, :], rhs=St[:, :], start=True, stop=False),
                         [W["qdT"], St], [ps6], sig=False)
                    k.op("pe", lambda e: e.matmul(ps6[:, 0:128], lhsT=W["QKm"][:, :], rhs=W["vnew"][:, :], start=False, stop=True),
                         [W["QKm"], W["vnew"]], [ps6], sig=False)
                    k.op("pe", lambda e: e.matmul(ps6[:, 128:256], lhsT=W["kdec"][:, :], rhs=W["vnew"][:, :], start=True, stop=True),
                         [W["kdec"], W["vnew"]], [ps6])
                    osb_ = OSB[(h * NCH + ch) % 2]
                    k.op("act", lambda e: e.copy(out=osb_[:, :], in_=ps6[:, 0:128]), [ps6], [osb_])
                    k.dma(_os.environ.get("OQ", "sp"), X["oscr"][ch * 128:(ch + 1) * 128, h * 128:(h + 1) * 128], osb_[:, :],
                          [osb_], [X["oscr"]], X["oscr"])
                    k.op("dve", lambda e: e.tensor_scalar(St[:, :], St[:, :], G["egl"].t[:, ch, h:h + 1], None, ALU.mult),
                         [St, G["egl"]], [St])
                    k.op("dve", lambda e: e.tensor_tensor(out=St[:, :], in0=ps6[:, 128:256], in1=St[:, :], op=ALU.add),
                         [St, ps6], [St])
    k.barrier()


def rot_norm(k, src, dst, nh, ssq, rstd, tmp, gain, cs, scale, do_norm, eng="dve", rot=True):
    s3 = src.t[:, 0:nh * 128].rearrange("p (h d) -> p h d", h=nh)
    t3 = tmp.t[:, 0:nh * 128].rearrange("p (h d) -> p h d", h=nh)
    d3 = dst.t[:, 0:nh * 128].rearrange("p (h d) -> p h d", h=nh)
    if do_norm:
        k.op(eng, lambda e: e.tensor_tensor(out=t3, in0=s3, in1=s3, op=ALU.mult), [src], [tmp])
        k.op("dve", lambda e: e.tensor_reduce(out=ssq[:, 0:nh], in_=t3, axis=AX.X, op=ALU.add), [tmp], [ssq])
        k.op("act", lambda e: e.activation(out=rstd[:, 0:nh], in_=ssq[:, 0:nh], func=AF.Sqrt, scale=1.0 / 128, bias=EPS),
             [ssq], [rstd])
        k.op("dve", lambda e: e.reciprocal(rstd[:, 0:nh], rstd[:, 0:nh]), [rstd], [rstd])
        k.op(eng, lambda e: e.tensor_tensor(out=t3, in0=s3, in1=rstd.t[:, 0:nh].unsqueeze(2).to_broadcast([128, nh, 128]),
                                            op=ALU.mult), [src, rstd], [tmp])
        k.op(eng, lambda e: e.scalar_tensor_tensor(out=t3, in0=t3, scalar=scale,
                                                   in1=gain.t[:, 0:128].unsqueeze(1).to_broadcast([128, nh, 128]),
                                                   op0=ALU.mult, op1=ALU.mult), [tmp, gain], [tmp])
        base = tmp
        b3 = t3
    else:
        k.op(eng, lambda e: e.tensor_scalar(t3, s3, scale, None, ALU.mult), [src], [tmp])
        base = tmp
        b3 = t3
    if not rot:
        k.op("act", lambda e: e.copy(out=d3, in_=b3), [base], [dst])
        return
    cosb = cs.t[:, 0:16].unsqueeze(1).to_broadcast([128, nh, 16])
    sinb = cs.t[:, 16:32].unsqueeze(1).to_broadcast([128, nh, 16])
    k.op("act", lambda e: e.copy(out=d3[:, :, 32:128], in_=b3[:, :, 32:128]), [base], [dst])
    ra = src.t[:, 0:nh * 128].rearrange("p (h d) -> p h d", h=nh)
    k.op(eng, lambda e: e.tensor_tensor(out=ra[:, :, 32:48], in0=b3[:, :, 0:16], in1=cosb, op=ALU.mult), [base, cs], [src])
    k.op(eng, lambda e: e.tensor_tensor(out=ra[:, :, 48:64], in0=b3[:, :, 16:32], in1=sinb, op=ALU.mult), [base, cs], [src])
    k.op(eng, lambda e: e.tensor_tensor(out=ra[:, :, 64:80], in0=b3[:, :, 16:32], in1=cosb, op=ALU.mult), [base, cs], [src])
    k.op(eng, lambda e: e.tensor_tensor(out=ra[:, :, 80:96], in0=b3[:, :, 0:16], in1=sinb, op=ALU.mult), [base, cs], [src])
    k.op(eng, lambda e: e.tensor_tensor(out=d3[:, :, 0:16], in0=ra[:, :, 32:48], in1=ra[:, :, 48:64], op=ALU.subtract), [src], [dst])
    k.op(eng, lambda e: e.tensor_tensor(out=d3[:, :, 16:32], in0=ra[:, :, 64:80], in1=ra[:, :, 80:96], op=ALU.add), [src], [dst])


def phase3(k, X, P, C):
    cst = C["cst"]
    ident = C["ident"]
    NB = C["n_own_blocks"]
    NKB = C["n_key_blocks"]
    with ExitStack() as es:
        identb = k.sb(es, "identb", [128, 128], BF16)
        onesb = k.sb(es, "onesb", [128, 128], BF16)
        k.op("dve", lambda e: e.tensor_copy(identb[:, :], ident), [cst], [identb])
        k.op("dve", lambda e: e.tensor_copy(onesb[:, :], cst.t[:, 384:512]), [cst], [onesb])
        KT = k.sb(es, "KT", [128, 4, S], BF16)
        IKT = k.sb(es, "IKT", [128, S], BF16)
        V = k.sb(es, "V", [128, 32, 512], BF16)
        kpos = k.sb(es, "kpos", [128, 256], F32)
        posr = k.sb(es, "posr", [128, 1], F32)
        gq_n = k.sb(es, "gq_n", [128, 128], F32)
        gk_n = k.sb(es, "gk_n", [128, 128], F32)
        gd_n = k.sb(es, "gd_n", [128, 128], F32)
        jsel = k.sb(es, "jsel", [128, 1], F32)
        for b_, nm in ((gq_n, "att_q_norm"), (gk_n, "att_k_norm"), (gd_n, "gdn_norm")):
            k.dma("sp", b_[:, :], X[nm].t.partition_broadcast(128), [X[nm]], [b_], b_)
        k.dma("sp", kpos[:, :], X["kpos"].t.partition_broadcast(128), [X["kpos"]], [kpos], kpos)
        k.dma("sp", jsel[:, :], X["jsel"][:, :], [X["jsel"]], [jsel], jsel)
        med = [k.sb(es, f"med{i}", [128, 2048], F32) for i in range(3)]
        bfb = k.sb(es, "bfb", [128, 2048], BF16)
        cs = k.sb(es, "cs", [128, 32], F32)
        pos = k.sb(es, "pos", [128, 1], F32)
        iw = k.sb(es, "iw", [128, 32], F32)
        ssq = k.sb(es, "ssq3", [128, 16], F32)
        rstd = k.sb(es, "rstd3", [128, 16], F32)
        QT = k.sb(es, "QT", [128, 16, 128], BF16)
        IQT = k.sb(es, "IQT", [128, 32, 128], BF16)
        sc = k.sb(es, "sc", [128, S], F32)
        wk = k.sb(es, "wk", [128, S], F32)
        m8 = k.sb(es, "m8", [128, 8], F32)
        tau = k.sb(es, "tau", [128, 1], F32)
        mk = k.sb(es, "mk", [128, S], BF16)
        mkT = k.sb(es, "mkT", [128, 32, 128], BF16)
        rl = [k.sb(es, f"rl{i}", [128, 512], F32) for i in range(3)]
        pe_ = [k.sb(es, f"pe{i}", [128, 512], BF16) for i in range(3)]
        pm = [k.sb(es, f"pm{i}", [128, 512], BF16) for i in range(3)]
        rinv = k.sb(es, "rinv", [128, 512], F32)
        obT = [k.sb(es, f"obT{i}", [128, 512], BF16) for i in range(2)]
        mT = [k.sb(es, f"mT{i}", [128, 512], BF16) for i in range(2)]
        pi = [0]

        def nps():
            p = P[pi[0] % 8]; pi[0] += 1
            return p

        for sb_ in range(NKB):
            t0 = sb_ * 128
            a = med[0]; tm = med[1]
            k.dma("sp", a[:, 0:1024], X["akv"][t0:t0 + 128, :], [X["akv"]], [a], a)
            k.dma("sp", a[:, 1024:1152], X["ikr"][t0:t0 + 128, :], [X["ikr"]], [a], a)
            k.dma("sp", cs[:, :], X["cs_all"][t0:t0 + 128, :], [X["cs_all"]], [cs], cs)
            k.op("act", lambda e: e.copy(out=V[:, sb_, :], in_=a[:, 512:1024]), [a], [V])
            rot_norm(k, _view(a, 1024, 128), _view(bfb, 1024, 128), 1, ssq, rstd, _view(tm, 1024, 128), gk_n, cs, 1.0, False)
            rot_norm(k, _view(a, 0, 512), _view(bfb, 0, 512), 4, ssq, rstd, _view(tm, 0, 512), gk_n, cs, 1.0, True)
            ps = nps()
            psb = ps.t[:, :].bitcast(BF16)
            for g in range(4):
                k.op("pe", lambda e: e.transpose(out=psb[:, g * 128:(g + 1) * 128], in_=bfb[:, g * 128:(g + 1) * 128], identity=identb[:, :]),
                     [bfb, identb], [ps], sig=False)
            k.op("pe", lambda e: e.transpose(out=psb[:, 512:640], in_=bfb[:, 1024:1152], identity=identb[:, :]),
                 [bfb, identb], [ps])
            k.op("act", lambda e: e.copy(out=KT[:, :, t0:t0 + 128], in_=psb[:, 0:512].rearrange("p (g s) -> p g s", g=4)),
                 [ps], [KT])
            k.op("act", lambda e: e.copy(out=IKT[:, t0:t0 + 128], in_=psb[:, 512:640]), [ps], [IKT])

        for n in range(NB):
            r0 = n * 128
            NK = min(2 * n + 2, NKB)
            NKc = NK * 128
            aq = med[0]; tm = med[1]
            k.dma("sp", aq[:, :], X["aq"][r0:r0 + 128, :], [X["aq"]], [aq], aq)
            k.dma("sp", iw[:, :], X["iw"][r0:r0 + 128, :], [X["iw"]], [iw], iw)
            k.dma("sp", cs[:, :], X["cs_own"][r0:r0 + 128, :], [X["cs_own"]], [cs], cs)
            k.dma("sp", pos[:, :], X["pos_own"][r0:r0 + 128, :], [X["pos_own"]], [pos], pos)
            k.op("dve", lambda e: e.tensor_scalar(iw[:, :], iw[:, :], 32 ** -0.5, None, ALU.mult), [iw], [iw])
            rot_norm(k, aq, bfb, 16, ssq, rstd, tm, gq_n, cs, 128 ** -0.5, True)
            for g4 in range(4):
                ps = nps(); psb = ps.t[:, :].bitcast(BF16)
                for j in range(4):
                    hq = g4 * 4 + j
                    k.op("pe", lambda e: e.transpose(out=psb[:, j * 128:(j + 1) * 128], in_=bfb[:, hq * 128:(hq + 1) * 128],
                                                     identity=identb[:, :]), [bfb, identb], [ps], sig=(j == 3))
                k.op("act", lambda e: e.copy(out=QT[:, g4 * 4:(g4 + 1) * 4, :], in_=psb[:, 0:512].rearrange("p (g s) -> p g s", g=4)),
                     [ps], [QT])
            for half in range(2):
                iqr = med[0]; itm = med[1]
                k.dma("sp", iqr[:, :], X["iq"][r0:r0 + 128, half * 2048:(half + 1) * 2048], [X["iq"]], [iqr], iqr)
                rot_norm(k, iqr, bfb, 16, ssq, rstd, itm, gq_n, cs, 128 ** -0.5, False, eng="dve")
                for g4 in range(4):
                    ps = nps(); psb = ps.t[:, :].bitcast(BF16)
                    for j in range(4):
                        hq = g4 * 4 + j
                        k.op("pe", lambda e: e.transpose(out=psb[:, j * 128:(j + 1) * 128], in_=bfb[:, hq * 128:(hq + 1) * 128],
                                                         identity=identb[:, :]), [bfb, identb], [ps], sig=(j == 3))
                    k.op("act", lambda e: e.copy(out=IQT[:, half * 16 + g4 * 4:half * 16 + (g4 + 1) * 4, :],
                                                 in_=psb[:, 0:512].rearrange("p (g s) -> p g s", g=4)), [ps], [IQT])
            nkt = (NKc + 511) // 512
            scv = [Buf(sc.t[:, kt * 512:kt * 512 + min(512, NKc - kt * 512)], f"scv{kt}") for kt in range(nkt)]
            ri = 0
            for kt in range(nkt):
                w_ = min(512, NKc - kt * 512)
                acc_eng = "dve"
                for hi in range(32):
                    ps = nps()
                    k.op("pe", lambda e: e.matmul(ps[:, 0:w_], lhsT=IQT[:, hi, :], rhs=IKT[:, kt * 512:kt * 512 + w_],
                                                  start=True, stop=True), [IQT, IKT], [ps])
                    r_ = rl[ri % 3]; ri += 1
                    k.op("act", lambda e: e.activation(out=r_[:, 0:w_], in_=ps[:, 0:w_], func=AF.Relu), [ps], [r_])
                    if hi == 0:
                        k.op(acc_eng, lambda e: e.tensor_scalar(sc[:, kt * 512:kt * 512 + w_], r_[:, 0:w_], iw[:, 0:1], None, ALU.mult),
                             [r_, iw], [scv[kt], sc])
                    else:
                        k.op(acc_eng, lambda e: e.scalar_tensor_tensor(out=sc[:, kt * 512:kt * 512 + w_], in0=r_[:, 0:w_],
                                                                       scalar=iw[:, hi:hi + 1], in1=sc[:, kt * 512:kt * 512 + w_],
                                                                       op0=ALU.mult, op1=ALU.add), [r_, iw, scv[kt]], [scv[kt]])
            c0 = NKc - 256
            k.op("dve", lambda e: e.tensor_scalar(posr[:, :], pos[:, :], float(-c0), None, ALU.add), [pos], [posr])
            k.op("dve", lambda e: e.tensor_scalar(wk[:, c0:NKc], kpos[:, 0:256], posr[:, 0:1], -1e30, ALU.is_gt, ALU.mult),
                 [kpos, posr], [wk])
            k.op("dve", lambda e: e.tensor_tensor(out=sc[:, c0:NKc], in0=sc[:, c0:NKc], in1=wk[:, c0:NKc], op=ALU.add),
                 [wk] + scv, [sc] + scv)
            if NKc > 256:
                src = sc
                for rnd in range(32):
                    k.op("dve", lambda e: e.max(out=m8[:, :], in_=src[:, 0:NKc]), [src], [m8])
                    if rnd < 31:
                        k.op("dve", lambda e: e.match_replace(out=wk[:, 0:NKc], in_to_replace=m8[:, :], in_values=src[:, 0:NKc],
                                                              imm_value=-3e38), [src, m8], [wk])
                        src = wk
                k.op("dve", lambda e: e.tensor_scalar(tau[:, :], m8[:, 7:8], -1e29, None, ALU.max), [m8], [tau])
            else:
                k.op("dve", lambda e: e.memset(tau[:, :], -1e29), [], [tau])
            k.op("dve", lambda e: e.tensor_scalar(mk[:, 0:NKc], sc[:, 0:NKc], tau[:, 0:1], None, ALU.is_ge), [sc, tau], [mk])
            for kb4 in range((NK + 3) // 4):
                ps = nps(); psb = ps.t[:, :].bitcast(BF16)
                nb_ = min(4, NK - kb4 * 4)
                for j in range(nb_):
                    kb = kb4 * 4 + j
                    k.op("pe", lambda e: e.transpose(out=psb[:, j * 128:(j + 1) * 128], in_=mk[:, kb * 128:(kb + 1) * 128],
                                                     identity=identb[:, :]), [mk, identb], [ps], sig=(j == nb_ - 1))
                k.op("act", lambda e: e.copy(out=mkT[:, kb4 * 4:kb4 * 4 + nb_, :],
                                             in_=psb[:, 0:nb_ * 128].rearrange("p (g s) -> p g s", g=nb_)), [ps], [mkT])
            for g in range(4):
                psO = P[0]; psR = P[1]
                xi = 0
                for kb in range(NK):
                    psS = P[2 + (pi[0] % 6)]; pi[0] += 1
                    k.op("pe", lambda e: e.matmul(psS[:, 0:512], lhsT=KT[:, g, kb * 128:(kb + 1) * 128],
                                                  rhs=QT[:, g * 4:(g + 1) * 4, :].rearrange("p g s -> p (g s)"), start=True, stop=True),
                         [KT, QT], [psS])
                    pe1 = pe_[xi % 3]; pm1 = pm[xi % 3]; xi += 1
                    k.op("act", lambda e: e.activation(out=pe1[:, :], in_=psS[:, 0:512], func=AF.Exp), [psS], [pe1])
                    me = "dve"
                    k.op(me, lambda e: e.tensor_tensor(out=pm1[:, :].rearrange("p (g s) -> p g s", g=4),
                                                       in0=pe1[:, :].rearrange("p (g s) -> p g s", g=4),
                                                       in1=mkT.t[:, kb, :].unsqueeze(1).to_broadcast([128, 4, 128]), op=ALU.mult),
                         [pe1, mkT], [pm1])
                    k.op("pe", lambda e: e.matmul(psO[:, 0:512], lhsT=V[:, kb, g * 128:(g + 1) * 128], rhs=pm1[:, :],
                                                  start=(kb == 0), stop=(kb == NK - 1)), [V, pm1], [psO], sig=False)
                    k.op("pe", lambda e: e.matmul(psR[:, 0:512], lhsT=onesb[:, :], rhs=pm1[:, :],
                                                  start=(kb == 0), stop=(kb == NK - 1)), [onesb, pm1], [psR])
                k.op("act", lambda e: e.copy(out=rinv[:, :], in_=psR[:, 0:512]), [psR], [rinv])
                k.op("dve", lambda e: e.reciprocal(rinv[:, :], rinv[:, :]), [rinv], [rinv])
                ob = obT[g % 2]
                k.op("dve", lambda e: e.tensor_tensor(out=ob[:, :], in0=psO[:, 0:512], in1=rinv[:, :], op=ALU.mult), [psO, rinv], [ob])
                for hh in range(4):
                    rr = 2048 + (g * 4 + hh) * 128
                    k.dma("sp", X["mixT"][rr:rr + 128, r0:r0 + 128], ob[:, hh * 128:(hh + 1) * 128], [ob], [X["mixT"]], X["mixT"])
            A = med[0]; Bt = med[1]; gzt = med[2]
            k.dma("sp", A[:, :], X["oscr"][(2 * n) * 128:(2 * n + 1) * 128, :], [X["oscr"]], [A], A)
            k.dma("sp", Bt[:, :], X["oscr"][(2 * n + 1) * 128:(2 * n + 2) * 128, :], [X["oscr"]], [Bt], Bt)
            k.dma("sp", gzt[:, 0:2048], X["gz"][r0:r0 + 128, :], [X["gz"]], [gzt], gzt)
            k.op("dve", lambda e: e.tensor_tensor(out=Bt[:, :], in0=Bt[:, :], in1=A[:, :], op=ALU.subtract), [A, Bt], [Bt])
            k.op("dve", lambda e: e.scalar_tensor_tensor(out=A[:, :], in0=Bt[:, :], scalar=jsel[:, 0:1], in1=A[:, :],
                                                         op0=ALU.mult, op1=ALU.add), [A, Bt, jsel], [A])
            A3 = A.t[:, :].rearrange("p (h d) -> p h d", h=16)
            B3 = Bt.t[:, :].rearrange("p (h d) -> p h d", h=16)
            k.op("dve", lambda e: e.tensor_tensor(out=B3, in0=A3, in1=A3, op=ALU.mult), [A], [Bt])
            k.op("dve", lambda e: e.tensor_reduce(out=ssq[:, 0:16], in_=B3, axis=AX.X, op=ALU.add), [Bt], [ssq])
            k.op("act", lambda e: e.activation(out=rstd[:, 0:16], in_=ssq[:, 0:16], func=AF.Sqrt, scale=1.0 / 128, bias=EPS), [ssq], [rstd])
            k.op("dve", lambda e: e.reciprocal(rstd[:, 0:16], rstd[:, 0:16]), [rstd], [rstd])
            k.op("dve", lambda e: e.tensor_tensor(out=A3, in0=A3, in1=rstd.t[:, 0:16].unsqueeze(2).to_broadcast([128, 16, 128]), op=ALU.mult),
                 [A, rstd], [A])
            k.op("dve", lambda e: e.tensor_tensor(out=A3, in0=A3, in1=gd_n.t[:, :].unsqueeze(1).to_broadcast([128, 16, 128]), op=ALU.mult),
                 [A, gd_n], [A])
            k.op("act", lambda e: e.activation(out=gzt[:, 0:2048], in_=gzt[:, 0:2048], func=AF.Silu), [gzt], [gzt])
            k.op("dve", lambda e: e.tensor_tensor(out=A[:, :], in0=A[:, :], in1=gzt[:, 0:2048], op=ALU.mult), [A, gzt], [A])
            for g4 in range(4):
                ps = nps()
                for j in range(4):
                    hh = g4 * 4 + j
                    k.op("pe", lambda e: e.transpose(out=ps[:, j * 128:(j + 1) * 128], in_=A[:, hh * 128:(hh + 1) * 128], identity=ident),
                         [A, cst], [ps], sig=(j == 3))
                m_ = mT[g4 % 2]
                k.op("act", lambda e: e.copy(out=m_[:, :], in_=ps[:, 0:512]), [ps], [m_])
                for j in range(4):
                    rr = (g4 * 4 + j) * 128
                    k.dma("sp", X["mixT"][rr:rr + 128, r0:r0 + 128], m_[:, j * 128:(j + 1) * 128], [m_], [X["mixT"]], X["mixT"])
    k.barrier()


class _View:
    def __init__(self, parent, c0, n):
        self.p = parent
        self.t = parent.t[:, c0:c0 + n]
        self.name = parent.name + "_v"
        self.psum = False
        self.dram = False

    def __getitem__(self, key):
        return self.t[key]
    w = property(lambda self: self.p.w, lambda self, v: setattr(self.p, "w", v))
    r = property(lambda self: self.p.r, lambda self, v: setattr(self.p, "r", v))
    dsem = property(lambda self: self.p.dsem, lambda self, v: setattr(self.p, "dsem", v))
    dcnt = property(lambda self: self.p.dcnt, lambda self, v: setattr(self.p, "dcnt", v))


def _view(parent, c0, n):
    return _View(parent, c0, n)


def resid_evac_factory(k, X, src, row0, xr, ot, ctr):
    def f(off):
        def evac(q, ps, ncols):
            i = ctr[0] % len(xr); ctr[0] += 1
            xb = xr[i]; o = ot[i]
            r = row0 + q * 128
            k.dma("sp", xb[:, 0:ncols], src[r:r + 128, off:off + ncols], [src], [xb], xb)
            k.op("dve", lambda e: e.tensor_tensor(out=o[:, 0:ncols], in0=ps[:, 0:ncols], in1=xb[:, 0:ncols], op=ALU.add),
                 [ps, xb], [o])
            k.dma("sp", X["out"][r:r + 128, off:off + ncols], o[:, 0:ncols], [o], [X["out"]], X["out"])
        return evac
    return f


def phase4(k, X, P, C):
    with ExitStack() as es:
        hTt = es.enter_context(k.nc.sbuf_tensor("hT4", [128, 32, 512], BF16))
        hTb = Buf(hTt, "hT4")
        xr = [k.sb(es, f"xr{i}", [128, 512], F32) for i in range(4)]
        ot = [k.sb(es, f"ot4{i}", [128, 512], F32) for i in range(4)]
        ring = WRing(k, es)
        wv = X["w_out"].t.rearrange("(c p) n -> p c n", p=128)
        mv = X["mixT"].t.rearrange("(c p) t -> p c t", p=128)
        jobs = []
        ctr = [0]
        for t in range(C["n_own_tiles"]):
            mk_jobs(jobs, t, "N", X["w_out"], wv, 32, 0, D, (hTt, hTb),
                    resid_evac_factory(k, X, X["x_own"], t * 512, xr, ot, ctr))

        def prep(t):
            for c4 in range(4):
                k.dma("sp", hTt[:, c4 * 8:(c4 + 1) * 8, :], mv[:, c4 * 8:(c4 + 1) * 8, t * 512:(t + 1) * 512],
                      [X["mixT"]], [hTb], hTb)
        linear(k, ring, jobs, P, prep)
    k.barrier()


def phase5(k, X, P, C):
    cst = C["cst"]; ident = C["ident"]
    with ExitStack() as es:
        xt = [k.sb(es, f"xt5{i}", [128, D], F32) for i in range(2)]
        junk = k.sb(es, "junk5", [128, D], BF16)
        hTt = es.enter_context(k.nc.sbuf_tensor("hT5", [128, 32, 512], BF16))
        hTb = Buf(hTt, "hT5")
        ssq = k.sb(es, "ssq5", [128, 16], F32)
        rstd = k.sb(es, "rstd5", [128, 16], F32)
        gamq = k.sb(es, "gamq", [128, 32], F32)
        gamkv = k.sb(es, "gamkv", [128, 32], F32)
        qn_g = k.sb(es, "qn_g", [128, 128], F32)
        kn_g = k.sb(es, "kn_g", [128, 128], F32)
        identb = k.sb(es, "identb5", [128, 128], BF16)
        onesb = k.sb(es, "onesb5", [128, 128], BF16)
        k.op("dve", lambda e: e.tensor_copy(identb[:, :], ident), [cst], [identb])
        k.op("dve", lambda e: e.tensor_copy(onesb[:, :], cst.t[:, 384:512]), [cst], [onesb])
        k.dma("sp", gamq[:, :], X["norm_mem_q"][:, :], [X["norm_mem_q"]], [gamq], gamq)
        k.dma("sp", gamkv[:, :], X["norm_mem_kv"][:, :], [X["norm_mem_kv"]], [gamkv], gamkv)
        k.dma("sp", qn_g[:, :], X["mem_q_norm"].t.partition_broadcast(128), [X["mem_q_norm"]], [qn_g], qn_g)
        k.dma("sp", kn_g[:, :], X["mem_k_norm"].t.partition_broadcast(128), [X["mem_k_norm"]], [kn_g], kn_g)
        MKT = k.sb(es, "MKT", [128, 4, 256], BF16)
        MV = k.sb(es, "MV", [128, 2, 512], BF16)
        MQT = k.sb(es, "MQT", [128, 4, 512], BF16)
        moTt = es.enter_context(k.nc.sbuf_tensor("moT", [128, 4, 512], BF16))
        moTb = Buf(moTt, "moT")
        o32 = [k.sb(es, f"o32_{i}", [128, 512], F32) for i in range(2)]
        t32 = k.sb(es, "t32", [128, 512], F32)
        obf = k.sb(es, "obf5", [128, 512], BF16)
        pe_ = [k.sb(es, f"pe5{i}", [128, 512], BF16) for i in range(2)]
        rinv = k.sb(es, "rinv5", [128, 512], F32)
        xr = [k.sb(es, f"xr5{i}", [128, 512], F32) for i in range(4)]
        ot = [k.sb(es, f"ot5{i}", [128, 512], F32) for i in range(4)]
        ring = WRing(k, es, npieces=4)
        ctr = [0]
        oc = [0]
        wkv = X["w_mem_kv"].t.rearrange("(c p) n -> p c n", p=128)

        def kv_fac(off):
            def evac(q, ps, ncols):
                o = o32[oc[0] % 2]; oc[0] += 1
                k.op("act", lambda e: e.copy(out=o[:, :], in_=ps[:, 0:512]), [ps], [o])
                if off == 0:
                    rot_norm(k, o, obf, 4, ssq, rstd, t32, kn_g, None, 1.0, True, rot=False)
                    p2 = P[6 + q % 2]; pb = p2.t[:, :].bitcast(BF16)
                    for j in range(4):
                        k.op("pe", lambda e: e.transpose(out=pb[:, j * 128:(j + 1) * 128], in_=obf[:, j * 128:(j + 1) * 128],
                                                         identity=identb[:, :]), [obf, identb], [p2], sig=(j == 3))
                    k.op("act", lambda e: e.copy(out=MKT[:, :, q * 128:(q + 1) * 128],
                                                 in_=pb[:, 0:512].rearrange("p (g s) -> p g s", g=4)), [p2], [MKT])
                else:
                    k.op("dve", lambda e: e.tensor_copy(MV[:, q, :], o[:, :]), [o], [MV])
            return evac
        jobs = []
        mk_jobs(jobs, "kv", "N", X["w_mem_kv"], wkv, 32, 0, 1024, (hTt, hTb), kv_fac, nsub=2)
        linear(k, ring, jobs, P[0:4], lambda pid: build_hT(k, X["mem"], 0, xt, hTt, hTb, ssq, rstd, junk, gamkv, ident, cst, P[6:8], n_sub=2))
        wq = X["w_mem_q"].t.rearrange("(c p) n -> p c n", p=128)
        wo = X["w_mem_o"].t.rearrange("(c p) n -> p c n", p=128)
        for t in range(C["n_own_tiles"]):
            def q_fac(off):
                def evac(q, ps, ncols):
                    o = o32[oc[0] % 2]; oc[0] += 1
                    k.op("act", lambda e: e.copy(out=o[:, :], in_=ps[:, 0:512]), [ps], [o])
                    rot_norm(k, o, obf, 4, ssq, rstd, t32, qn_g, None, 128 ** -0.5, True, rot=False)
                    p2 = P[6 + q % 2]; pb = p2.t[:, :].bitcast(BF16)
                    for j in range(4):
                        k.op("pe", lambda e: e.transpose(out=pb[:, j * 128:(j + 1) * 128], in_=obf[:, j * 128:(j + 1) * 128],
                                                         identity=identb[:, :]), [obf, identb], [p2], sig=(j == 3))
                    k.op("act", lambda e: e.copy(out=MQT[:, :, q * 128:(q + 1) * 128],
                                                 in_=pb[:, 0:512].rearrange("p (g s) -> p g s", g=4)), [p2], [MQT])
                return evac
            jobs = []
            mk_jobs(jobs, t, "N", X["w_mem_q"], wq, 32, 0, 512, (hTt, hTb), q_fac)
            linear(k, ring, jobs, P[0:4], lambda pid: build_hT(k, X["out"], pid * 512, xt, hTt, hTb, ssq, rstd, junk, gamq, ident, cst, P[6:8]))
            for hd in range(4):
                psO = P[0]; psR = P[1]
                for mb in range(2):
                    psS = P[2 + mb]
                    k.op("pe", lambda e: e.matmul(psS[:, 0:512], lhsT=MKT[:, hd, mb * 128:(mb + 1) * 128], rhs=MQT[:, hd, :],
                                                  start=True, stop=True), [MKT, MQT], [psS])
                    pe1 = pe_[mb]
                    k.op("act", lambda e: e.activation(out=pe1[:, :], in_=psS[:, 0:512], func=AF.Exp), [psS], [pe1])
                    k.op("pe", lambda e: e.matmul(psO[:, 0:512], lhsT=MV[:, mb, hd * 128:(hd + 1) * 128], rhs=pe1[:, :],
                                                  start=(mb == 0), stop=(mb == 1)), [MV, pe1], [psO], sig=False)
                    k.op("pe", lambda e: e.matmul(psR[:, 0:512], lhsT=onesb[:, :], rhs=pe1[:, :],
                                                  start=(mb == 0), stop=(mb == 1)), [onesb, pe1], [psR])
                k.op("act", lambda e: e.copy(out=rinv[:, :], in_=psR[:, 0:512]), [psR], [rinv])
                k.op("dve", lambda e: e.reciprocal(rinv[:, :], rinv[:, :]), [rinv], [rinv])
                k.op("dve", lambda e: e.tensor_tensor(out=moTt[:, hd, :], in0=psO[:, 0:512], in1=rinv[:, :], op=ALU.mult),
                     [psO, rinv], [moTb])
            jobs = []
            mk_jobs(jobs, t, "N", X["w_mem_o"], wo, 4, 0, D, (moTt, moTb),
                    resid_evac_factory(k, X, X["out"], t * 512, xr, ot, ctr))
            linear(k, ring, jobs, P[0:4], lambda pid: None)
    k.barrier()


def phase6a(k, X, P, C):
    cst = C["cst"]; ident = C["ident"]
    with ExitStack() as es:
        xt = [k.sb(es, f"xt6{i}", [128, D], F32) for i in range(2)]
        junk = k.sb(es, "junk6", [128, D], BF16)
        hTt = es.enter_context(k.nc.sbuf_tensor("hT6", [128, 32, 512], BF16))
        hTb = Buf(hTt, "hT6")
        ssq = k.sb(es, "ssq6", [128, 1], F32)
        rstd = k.sb(es, "rstd6", [128, 1], F32)
        gam = k.sb(es, "gam6", [128, 32], F32)
        k.dma("sp", gam[:, :], X["norm_ffn"][:, :], [X["norm_ffn"]], [gam], gam)
        sg = [k.sb(es, f"sg{i}", [128, 512], F32) for i in range(8)]
        ab = [k.sb(es, f"ab{i}", [128, 512], BF16) for i in range(4)]
        ring = WRing(k, es)
        wg = X["w_gate"].t.rearrange("(c p) n -> p c n", p=128)
        wu = X["w_up"].t.rearrange("(c p) n -> p c n", p=128)
        jobs = []
        st = dict(g=0, a=0, sgmap={})
        for t in range(C["n_own_tiles"]):
            off = 0
            while off < FFN:
                n_ = min(512, FFN - off)

                def gfac(o_, t=t):
                    def evac(q, ps, ncols):
                        s_ = sg[st["g"] % 8]; st["g"] += 1
                        st["sgmap"][(t, o_, q)] = s_
                        k.op("act", lambda e: e.activation(out=s_[:, :], in_=ps[:, 0:512], func=AF.Silu), [ps], [s_])
                    return evac

                def ufac(o_, t=t, off=off):
                    def evac(q, ps, ncols):
                        s_ = st["sgmap"].pop((t, o_, q))
                        a_ = ab[st["a"] % 4]; st["a"] += 1
                        k.op("dve", lambda e: e.tensor_tensor(out=a_[:, :], in0=ps[:, 0:512], in1=s_[:, :], op=ALU.mult),
                             [ps, s_], [a_])
                        r = off + q * 128
                        k.dma("sp", X["aT"][r:r + 128, t * 512:(t + 1) * 512], a_[:, :], [a_], [X["aT"]], X["aT"])
                    return evac
                jobs.append(dict(pass_id=t, mode="T", wbuf=X["w_gate"], wv=wg, KC=32, c0=off, ncols=n_, hT=(hTt, hTb),
                                 evac=gfac(0), nsub=4))
                jobs.append(dict(pass_id=t, mode="T", wbuf=X["w_up"], wv=wu, KC=32, c0=off, ncols=n_, hT=(hTt, hTb),
                                 evac=ufac(0), nsub=4))
                off += n_
        linear(k, ring, jobs, P, lambda pid: build_hT(k, X["out"], pid * 512, xt, hTt, hTb, ssq, rstd, junk, gam, ident, cst, P[6:8]))
    k.barrier()


def phase6b(k, X, P, C):
    with ExitStack() as es:
        aTt = es.enter_context(k.nc.sbuf_tensor("aT6", [128, 86, 512], BF16))
        aTb = Buf(aTt, "aT6")
        xr = [k.sb(es, f"xr6{i}", [128, 512], F32) for i in range(4)]
        ot = [k.sb(es, f"ot6{i}", [128, 512], F32) for i in range(4)]
        ring = WRing(k, es, npieces=4)
        wd = X["w_down"].t.rearrange("(c p) n -> p c n", p=128)
        av = X["aT"].t.rearrange("(c p) t -> p c t", p=128)
        jobs = []
        ctr = [0]
        for t in range(C["n_own_tiles"]):
            mk_jobs(jobs, t, "N", X["w_down"], wd, 86, 0, D, (aTt, aTb),
                    resid_evac_factory(k, X, X["out"], t * 512, xr, ot, ctr))

        def prep(t):
            for c0 in range(0, 86, 16):
                n_ = min(16, 86 - c0)
                k.dma("sp", aTt[:, c0:c0 + n_, :], av[:, c0:c0 + n_, t * 512:(t + 1) * 512], [X["aT"]], [aTb], aTb)
        linear(k, ring, jobs, P, prep)
    k.barrier()


def build(dbg=False, stop=99, n_all_tiles=8, n_own_tiles=4, n_chunks=32, n_gdn_heads=16, n_own_blocks=16, n_key_blocks=32,
          phases=None, ext_in=(), ext_out=()):
    nc = bass.Bass("TRN2", target_bir_lowering=False)
    k = K(nc)
    shapes = {}

    class LazyX(dict):
        def __missing__(self, name):
            b = Buf(nc.dram_tensor(name, list(shapes[name]), F32, kind="ExternalInput").ap(), name)
            b.dram = True
            self[name] = b
            return b
    X = LazyX()

    def inp(name, shape):
        shapes[name] = shape
        if phases is None:
            X[name]
    inp("x_all", [S, D]); inp("x_own", [NOWN, D]); inp("mem", [256, D])
    inp("norm_mix", [128, 32]); inp("w_in", [D, DIN]); inp("consts", [128, 512]); inp("consts2", [16, 2048])
    inp("cw", [128, 48, 4]); inp("a_log", [16]); inp("dt_bias", [16])
    inp("gdn_norm", [128]); inp("att_q_norm", [128]); inp("att_k_norm", [128])
    inp("w_out", [D, D]); inp("norm_mem_q", [128, 32]); inp("norm_mem_kv", [128, 32])
    inp("w_mem_q", [D, 512]); inp("w_mem_kv", [D, 1024]); inp("mem_q_norm", [128]); inp("mem_k_norm", [128])
    inp("w_mem_o", [512, D]); inp("norm_ffn", [128, 32]); inp("w_gate", [D, FFN]); inp("w_up", [D, FFN]); inp("w_down", [FFN, D])
    inp("cs_all", [S, 32]); inp("cs_own", [NOWN, 32]); inp("pos_own", [NOWN, 1]); inp("kpos", [256]); inp("jsel", [128, 1])

    def scr(name, shape, dtype):
        kind = "ExternalInput" if name in ext_in else ("ExternalOutput" if name in ext_out else "Internal")
        X[name] = k.dram(name, shape, dtype, kind)
    scr("qkvT", [6144, S], F32)
    scr("gab", [S, 32], F32)
    scr("akv", [S, 1024], F32)
    scr("ikr", [S, 128], F32)
    scr("gz", [NOWN, 2048], F32)
    scr("aq", [NOWN, 2048], F32)
    scr("iq", [NOWN, 4096], F32)
    scr("iw", [NOWN, 32], F32)
    scr("oscr", [S, 2048], F32)
    scr("mixT", [D, NOWN], BF16)
    scr("aT", [FFN, NOWN], BF16)
    if "out" in ext_in:
        X["out_in"] = k.dram("out_in", [NOWN, D], F32, "ExternalInput")
    X["out"] = k.dram("out", [NOWN, D], F32, "ExternalOutput")
    with ExitStack() as es:
        P = [k.ps(es, f"P{i}") for i in range(8)]
        cst = k.sb(es, "cst", [128, 512], F32)
        k.dma("sp", cst[:, :], X["consts"][:, :], [X["consts"]], [cst], cst)
        C = dict(n_all_tiles=n_all_tiles, n_own_tiles=n_own_tiles, n_chunks=n_chunks, n_gdn_heads=n_gdn_heads,
                 n_own_blocks=n_own_blocks, n_key_blocks=n_key_blocks)
        C["ident"] = cst.t[:, 0:128]
        C["cst"] = cst
        allph = [phase1, phase2, phase3, phase4, phase5, phase6a, phase6b]
        for i, ph in enumerate(allph):
            if (phases is None and i < stop) or (phases is not None and (i + 1) in phases):
                ph(k, X, P, C)
                print("phase", i + 1, "instructions", k.nins, "waits", k.nwait, flush=True)
        k.finish([X["out"]])
    print("instructions", k.nins, "waits", k.nwait, "dma sems", len(k.dsems))
    return nc


def _consts():
    c = np.zeros((128, 512), np.float32)
    c[:, 0:128] = np.eye(128)
    p = np.arange(128)[:, None]; f = np.arange(128)[None, :]
    c[:, 128:256] = (p <= f)
    c[:, 256:384] = (p < f)
    c[:, 384:512] = 1.0
    c2 = np.zeros((16, 2048), np.float32)
    for h in range(16):
        c2[h, h * 128:(h + 1) * 128] = 1.0
    return c, c2


def _rot_table(pos):
    half = 16
    inv_freq = (np.float32(500000.0) ** (-np.arange(half, dtype=np.float32) * np.float32(2.0) / np.float32(32))).astype(np.float32)
    ang = pos.astype(np.float32)[:, None] * inv_freq[None, :]
    return np.concatenate([np.cos(ang), np.sin(ang)], axis=1).astype(np.float32)


def make_inputs(inp, core):
    b = core // 2
    j = core % 2
    f = lambda a: np.ascontiguousarray(a, dtype=np.float32)
    x = inp["x"][b]
    own_blocks = [2 * n + j for n in range(16)]
    own_rows = np.concatenate([np.arange(g * 128, (g + 1) * 128) for g in own_blocks])
    c, c2 = _consts()
    g32 = lambda v: f(v.reshape(32, 128).T)
    cwv = inp["conv_w"][0]
    cw = f(cwv.reshape(4, 3, 16, 128).transpose(3, 1, 2, 0).reshape(128, 48, 4))
    cs_all = _rot_table(np.arange(S))
    m = dict(
        x_all=f(x), x_own=f(x[own_rows]), mem=f(inp["mem"][b]),
        norm_mix=g32(inp["norm_mix"][0]), w_in=f(inp["w_in"][0]), consts=c, consts2=c2, cw=cw,
        a_log=f(inp["a_log"][0]), dt_bias=f(inp["dt_bias"][0]),
        gdn_norm=f(inp["gdn_norm"][0]), att_q_norm=f(inp["att_q_norm"][0]), att_k_norm=f(inp["att_k_norm"][0]),
        w_out=f(inp["w_out"][0]), norm_mem_q=g32(inp["norm_mem_q"][0]), norm_mem_kv=g32(inp["norm_mem_kv"][0]),
        w_mem_q=f(inp["w_mem_q"][0]), w_mem_kv=f(inp["w_mem_kv"][0]),
        mem_q_norm=f(inp["mem_q_norm"][0]), mem_k_norm=f(inp["mem_k_norm"][0]),
        w_mem_o=f(inp["w_mem_o"][0]), norm_ffn=g32(inp["norm_ffn"][0]),
        w_gate=f(inp["w_gate"][0]), w_up=f(inp["w_up"][0]), w_down=f(inp["w_down"][0]),
        cs_all=cs_all, cs_own=f(cs_all[own_rows]), pos_own=f(own_rows.astype(np.float32)[:, None]),
        kpos=f(np.arange(256)), jsel=np.full((128, 1), float(j), np.float32),
    )
    return m, own_rows


def kernel(**inputs):
    inp = {k_: np.asarray(v) for k_, v in inputs.items()}
    nc = build()
    maps = []
    rows = []
    for core in range(8):
        m, r = make_inputs(inp, core)
        maps.append(m)
        rows.append(r)
    res = run_bass_kernel_spmd(nc, maps, core_ids=list(range(8)))
    out = np.zeros((4, S, D), np.float32)
    for core in range(8):
        out[core // 2][rows[core]] = res.results[core]["out"]
    return out
```

```python
import numpy as np
import concourse.bass as bass
import concourse.mybir as mybir

F32 = mybir.dt.float32
BF16 = mybir.dt.bfloat16
AF = mybir.ActivationFunctionType
ALU = mybir.AluOpType
AX = mybir.AxisListType


class Buf:
    __slots__ = ("t", "w", "r", "dsem", "dcnt", "name", "dram", "psum")

    def __init__(self, t, name=""):
        self.t = t
        self.w = None
        self.r = {}
        self.dsem = None
        self.dcnt = 0
        self.name = name
        self.dram = False
        self.psum = False

    def __getitem__(self, key):
        return self.t[key]


class K:
    def __init__(self, nc):
        self.nc = nc
        self.engs = {"pe": nc.tensor, "act": nc.scalar, "dve": nc.vector,
                     "pool": nc.gpsimd, "sp": nc.sync}
        self.sem = {e: nc.alloc_semaphore(name="s_" + e) for e in self.engs}
        self.cnt = {e: 0 for e in self.engs}
        self.pending = {e: False for e in self.engs}
        self.waited = {e: {} for e in self.engs}
        self.dsems = []
        self.free_sems = []
        self.semcnt = {}
        self.nsem_alloc = 0
        self.nwait = 0
        self.nins = 0

    def sb(self, es, name, shape, dtype):
        self.uid = getattr(self, "uid", 0) + 1
        t = es.enter_context(self.nc.sbuf_tensor(f"sb{self.uid}_{name}", list(shape), dtype))
        return Buf(t, name)

    def ps(self, es, name, shape=(128, 512), dtype=F32):
        t = es.enter_context(self.nc.psum_tensor(name, list(shape), dtype))
        b = Buf(t, name)
        b.psum = True
        return b

    def dram(self, name, shape, dtype, kind="Internal"):
        t = self.nc.dram_tensor(name, list(shape), dtype, kind=kind)
        b = Buf(t.ap(), name)
        b.dram = True
        return b

    def _deps(self, eng, reads, writes, skip=None, strict=False):
        need = {}
        own = self.sem[eng]

        def add(p, raw):
            if p is None:
                return
            s, c = p
            if (s is own) and (not raw) and (not strict):
                return
            if need.get(s, 0) < c:
                need[s] = c
        for b in reads:
            add(b.w, True)
            if b.psum:
                for s, c in b.r.items():
                    add((s, c), False)
        for b in writes:
            add(b.w, False)
            for s, c in b.r.items():
                add((s, c), False)
        e = self.engs[eng]
        for s, c in need.items():
            if eng == "pe" and s is own:
                continue
            if skip is not None and s is skip:
                continue
            if self.waited[eng].get(s, 0) < c:
                e.wait_ge(s, c)
                self.nwait += 1
                self.waited[eng][s] = c

    def op(self, eng, fn, reads=(), writes=(), sig=True):
        self._deps(eng, reads, writes)
        ins = fn(self.engs[eng])
        self.nins += 1
        if sig:
            self.cnt[eng] += 1
            ins.then_inc(self.sem[eng], 1)
            p = (self.sem[eng], self.cnt[eng])
            self.pending[eng] = False
        else:
            p = (self.sem[eng], self.cnt[eng] + 1)
            self.pending[eng] = True
        s, c = p
        for b in reads:
            if b.r.get(s, 0) < c:
                b.r[s] = c
        for b in writes:
            b.w = p
            b.r = {}
        return ins

    def dma(self, q, out_ap, in_ap, reads, writes, semb, **kw):
        if getattr(semb, "dram", False):
            srcs = [b for b in reads if not getattr(b, "dram", False)]
            assert srcs
            semb = srcs[0]
        if semb.dsem is None:
            if self.free_sems:
                semb.dsem = self.free_sems.pop()
            else:
                self.nsem_alloc += 1
                semb.dsem = self.nc.alloc_semaphore(name=f"d{self.nsem_alloc}")
            semb.dcnt = self.semcnt.get(semb.dsem, 0)
            self.dsems.append(semb)
        self._deps(q, reads, writes, skip=semb.dsem, strict=True)
        ins = self.engs[q].dma_start(out=out_ap, in_=in_ap, **kw)
        self.nins += 1
        semb.dcnt += 16
        self.semcnt[semb.dsem] = semb.dcnt
        ins.then_inc(semb.dsem, 16)
        s, c = semb.dsem, semb.dcnt
        for b in reads:
            if b.r.get(s, 0) < c:
                b.r[s] = c
        for b in writes:
            b.w = (s, c)
            b.r = {}
        return ins

    def barrier(self):
        for e, eng in self.engs.items():
            for f in self.engs:
                if f == e:
                    continue
                c = self.cnt[f]
                if c > 0 and self.waited[e].get(self.sem[f], 0) < c:
                    eng.wait_ge(self.sem[f], c)
                    self.waited[e][self.sem[f]] = c
            for b in self.dsems:
                if b.dcnt > 0 and self.waited[e].get(b.dsem, 0) < b.dcnt:
                    eng.wait_ge(b.dsem, b.dcnt)
                    self.waited[e][b.dsem] = b.dcnt
        for b in self.dsems:
            self.free_sems.append(b.dsem)
            b.dsem = None
        self.dsems = []

    def finish(self, out_bufs):
        for e in self.engs:
            assert not self.pending[e], e
        self.barrier()

from contextlib import ExitStack
from concourse.bass_utils import run_bass_kernel_spmd

S = 4096; D = 4096; NOWN = 2048; DIN = 15552; FFN = 11008
TT = 512
C_GQ, C_GK, C_GV, C_GZ, C_GA, C_GB = 0, 2048, 4096, 6144, 8192, 8208
C_AQ, C_AK, C_AV, C_IQ, C_IK, C_IW = 8224, 10272, 10784, 11296, 15392, 15520


EPS = 1e-6
import os as _os
GDN_STAGE = int(_os.environ.get('GDN_STAGE', '99'))
CH_POOL = 'pool'


class WRing:
    def __init__(self, k, es, npieces=6, nst=2):
        self.k = k
        self.st = [k.sb(es, f"wst{i}", [128, 8, 512], F32) for i in range(nst)]
        self.pc = [k.sb(es, f"wpc{i}", [128, 8, 512], BF16) for i in range(npieces)]
        self.si = self.pi = self.ci = 0

    def load(self, wbuf, wv, kc0, nk, c0, ncols, castengs):
        k = self.k
        st = self.st[self.si % len(self.st)]; self.si += 1
        pc = self.pc[self.pi % len(self.pc)]; self.pi += 1
        k.dma("sp", st[:, 0:nk, 0:ncols], wv[:, kc0:kc0 + nk, c0:c0 + ncols], [wbuf], [st], st)
        eng = castengs[self.ci % len(castengs)]; self.ci += 1
        if eng == "act":
            k.op("act", lambda e: e.copy(out=pc[:, 0:nk, 0:ncols], in_=st[:, 0:nk, 0:ncols]), [st], [pc])
        else:
            k.op(eng, lambda e: e.tensor_copy(pc[:, 0:nk, 0:ncols], st[:, 0:nk, 0:ncols]), [st], [pc])
        return pc


def linear(k, ring, jobs, P, prep, castengs=("dve", "act"), LA=3):
    items = []
    for ji, jb in enumerate(jobs):
        for pi in range((jb["KC"] + 7) // 8):
            items.append((ji, pi))
    pcs = {}

    def issue(idx):
        ji, pi = items[idx]
        jb = jobs[ji]
        kc0 = pi * 8
        nk = min(8, jb["KC"] - kc0)
        pcs[idx] = ring.load(jb["wbuf"], jb["wv"], kc0, nk, jb["c0"], jb["ncols"], castengs)
    for idx in range(min(LA, len(items))):
        issue(idx)
    cur_pass = None
    bank = 0
    for idx, (ji, pi) in enumerate(items):
        jb = jobs[ji]
        if pi == 0:
            if jb["pass_id"] != cur_pass:
                cur_pass = jb["pass_id"]
                prep(cur_pass)
            jb["_banks"] = P[bank * 4:(bank + 1) * 4]
            bank = (bank + 1) % (len(P) // 4)
        if idx + LA < len(items):
            issue(idx + LA)
        pc = pcs.pop(idx)
        KC = jb["KC"]; ncols = jb["ncols"]; kc0 = pi * 8; nk = min(8, KC - kc0)
        hT, hTb = jb["hT"]
        banks = jb["_banks"]
        nsub = jb.get("nsub", 4)
        ntok = nsub * 128
        nq = nsub if jb["mode"] == "N" else ncols // 128
        for q in range(nq):
            ps = banks[q]
            for cc in range(nk):
                c = kc0 + cc
                if jb["mode"] == "N":
                    k.op("pe", lambda e: e.matmul(ps[:, 0:ncols], lhsT=hT[:, c, q * 128:(q + 1) * 128],
                                                  rhs=pc[:, cc, 0:ncols], start=(c == 0), stop=(c == KC - 1)),
                         [hTb, pc], [ps], sig=(cc == nk - 1))
                else:
                    k.op("pe", lambda e: e.matmul(ps[:, 0:ntok], lhsT=pc[:, cc, q * 128:(q + 1) * 128],
                                                  rhs=hT[:, c, 0:ntok], start=(c == 0), stop=(c == KC - 1)),
                         [hTb, pc], [ps], sig=(cc == nk - 1))
        if kc0 + nk == KC:
            for q in range(nq):
                jb["evac"](q, banks[q], ncols)


def build_hT(k, x_dram, row0, xt, hT, hTb, ssq, rstd, junk, gam, ident, cstb, pst, n_sub=4, col0=0):
    for sub in range(n_sub):
        xb = xt[sub % len(xt)]
        r0 = row0 + sub * 128
        k.dma("sp", xb[:, :], x_dram[r0:r0 + 128, col0:col0 + D], [x_dram], [xb], xb)
        k.op("dve", lambda e: e.memset(ssq[:, 0:1], 0.0), [], [ssq])
        k.op("act", lambda e: e.activation(out=junk[:, :], in_=xb[:, :], func=AF.Square,
                                           accum_out=ssq[:, 0:1]), [xb, ssq], [junk, ssq])
        k.op("act", lambda e: e.activation(out=rstd[:, 0:1], in_=ssq[:, 0:1], func=AF.Sqrt, scale=1.0 / D, bias=EPS),
             [ssq], [rstd])
        k.op("dve", lambda e: e.reciprocal(rstd[:, 0:1], rstd[:, 0:1]), [rstd], [rstd])
        k.op("dve", lambda e: e.tensor_scalar(xb[:, :], xb[:, :], rstd[:, 0:1], None, ALU.mult),
             [xb, rstd], [xb])
        for g in range(8):
            ps = pst[g % len(pst)]
            for j in range(4):
                c = g * 4 + j
                k.op("pe", lambda e: e.transpose(out=ps[:, j * 128:(j + 1) * 128],
                                                 in_=xb[:, c * 128:(c + 1) * 128], identity=ident),
                     [xb, cstb], [ps], sig=(j == 3))
            k.op("dve", lambda e: e.tensor_tensor(
                out=hT[:, g * 4:(g + 1) * 4, sub * 128:(sub + 1) * 128],
                in0=ps[:, :].rearrange("p (a b) -> p a b", a=4),
                in1=gam[:, g * 4:(g + 1) * 4].unsqueeze(2).to_broadcast([128, 4, 128]),
                op=ALU.mult), [ps, gam], [hTb])


def mk_jobs(jobs, pass_id, mode, wbuf, wv, KC, c0, n, hT, evac_factory, nsub=4):
    off = 0
    while off < n:
        nc_ = min(512, n - off)
        jobs.append(dict(pass_id=pass_id, mode=mode, wbuf=wbuf, wv=wv, KC=KC, c0=c0 + off, ncols=nc_,
                         hT=hT, evac=evac_factory(off), nsub=nsub))
        off += nc_


def phase1(k, X, P, C):
    with ExitStack() as es:
        xt = [k.sb(es, f"xt{i}", [128, D], F32) for i in range(2)]
        junk = k.sb(es, "junk", [128, D], BF16)
        hTt = es.enter_context(k.nc.sbuf_tensor("hT", [128, 32, 512], BF16))
        hTb = Buf(hTt, "hT")
        ssq = k.sb(es, "ssq", [128, 1], F32)
        rstd = k.sb(es, "rstd", [128, 1], F32)
        gam = k.sb(es, "gam", [128, 32], F32)
        ot = [k.sb(es, f"ot{i}", [128, 512], F32) for i in range(4)]
        oi = [0]
        ring = WRing(k, es)
        k.dma("sp", gam[:, :], X["norm_mix"][:, :], [X["norm_mix"]], [gam], gam)
        wv = X["w_in"].t.rearrange("(c p) n -> p c n", p=128)
        jobs = []
        passes = {}

        def fac(mode, dst, row0, dcol0):
            def f(off):
                def evac(q, ps, ncols):
                    o = ot[oi[0] % 4]; oi[0] += 1
                    w = ncols if mode == "N" else 512
                    k.op("act", lambda e: e.copy(out=o[:, 0:w], in_=ps[:, 0:w]), [ps], [o])
                    if mode == "N":
                        r = row0 + q * 128
                        dc = dcol0 + off
                        k.dma("sp", dst[r:r + 128, dc:dc + ncols], o[:, 0:ncols], [o], [dst], dst)
                    else:
                        r = dcol0 + off + q * 128
                        k.dma("sp", dst[r:r + 128, row0:row0 + 512], o[:, 0:512], [o], [dst], dst)
                return evac
            return f

        for t in range(C["n_all_tiles"]):
            pid = ("all", t)
            passes[pid] = (X["x_all"], t * 512)
            r0 = t * 512
            mk_jobs(jobs, pid, "T", X["w_in"], wv, 32, C_GQ, 6144, (hTt, hTb), fac("T", X["qkvT"], r0, 0))
            mk_jobs(jobs, pid, "N", X["w_in"], wv, 32, C_GA, 32, (hTt, hTb), fac("N", X["gab"], r0, 0))
            mk_jobs(jobs, pid, "N", X["w_in"], wv, 32, C_AK, 1024, (hTt, hTb), fac("N", X["akv"], r0, 0))
            mk_jobs(jobs, pid, "N", X["w_in"], wv, 32, C_IK, 128, (hTt, hTb), fac("N", X["ikr"], r0, 0))
        for t in range(C["n_own_tiles"]):
            pid = ("own", t)
            passes[pid] = (X["x_own"], t * 512)
            r0 = t * 512
            mk_jobs(jobs, pid, "N", X["w_in"], wv, 32, C_GZ, 2048, (hTt, hTb), fac("N", X["gz"], r0, 0))
            mk_jobs(jobs, pid, "N", X["w_in"], wv, 32, C_AQ, 2048, (hTt, hTb), fac("N", X["aq"], r0, 0))
            mk_jobs(jobs, pid, "N", X["w_in"], wv, 32, C_IQ, 4096, (hTt, hTb), fac("N", X["iq"], r0, 0))
            mk_jobs(jobs, pid, "N", X["w_in"], wv, 32, C_IW, 32, (hTt, hTb), fac("N", X["iw"], r0, 0))

        def prep(pid):
            xd, r0 = passes[pid]
            build_hT(k, xd, r0, xt, hTt, hTb, ssq, rstd, junk, gam, C["ident"], C["cst"], P[6:8])

        linear(k, ring, jobs, P, prep)
    k.barrier()


def phase2(k, X, P, C):
    cst = C["cst"]
    ident = C["ident"]
    tri = cst.t[:, 128:256]
    tris = cst.t[:, 256:384]
    ones = cst.t[:, 384:512]
    NCH = C["n_chunks"]
    NH = C["n_gdn_heads"]
    with ExitStack() as es:
        cw = k.sb(es, "cw", [128, 48, 4], F32)
        oh = k.sb(es, "oh", [16, 2048], F32)
        alog = k.sb(es, "alog", [128, 16], F32)
        dtb = k.sb(es, "dtb", [128, 16], F32)
        nea = k.sb(es, "nea", [128, 16], F32)
        G = {n: k.sb(es, "g_" + n, [128, 32, 16], F32) for n in ("gcum", "eg_unused", "edec", "egl", "nbeta", "beta")}
        gcumT = k.sb(es, "gcumT", [16, 32, 128], F32)
        gtmp = [k.sb(es, f"gtmp{i}", [128, 32], F32) for i in range(4)]
        k.dma("sp", cw[:, :, :], X["cw"][:, :, :], [X["cw"]], [cw], cw)
        k.dma("sp", oh[:, :], X["consts2"][:, :], [X["consts2"]], [oh], oh)
        k.dma("sp", alog[:, :], X["a_log"].t.partition_broadcast(128), [X["a_log"]], [alog], alog)
        k.dma("sp", dtb[:, :], X["dt_bias"].t.partition_broadcast(128), [X["dt_bias"]], [dtb], dtb)
        k.op("act", lambda e: e.activation(out=nea[:, :], in_=alog[:, :], func=AF.Exp), [alog], [nea])
        k.op("dve", lambda e: e.tensor_scalar(nea[:, :], nea[:, :], -1.0, None, ALU.mult), [nea], [nea])

        for ch in range(NCH):
            t0 = ch * 128
            ga = gtmp[0]; t1 = gtmp[1]; t2 = gtmp[2]; g = gtmp[3]
            k.dma("sp", ga[:, :], X["gab"][t0:t0 + 128, :], [X["gab"]], [ga], ga)
            k.op("dve", lambda e: e.tensor_tensor(out=t1[:, 0:16], in0=ga[:, 0:16], in1=dtb[:, :], op=ALU.add),
                 [ga, dtb], [t1])
            k.op("act", lambda e: e.activation(out=t1[:, 0:16], in_=t1[:, 0:16], func=AF.Exp), [t1], [t1])
            k.op("act", lambda e: e.activation(out=t1[:, 0:16], in_=t1[:, 0:16], func=AF.Ln, bias=1.0), [t1], [t1])
            k.op("dve", lambda e: e.tensor_tensor(out=g[:, 0:16], in0=t1[:, 0:16], in1=nea[:, :], op=ALU.mult),
                 [t1, nea], [g])
            k.op("act", lambda e: e.activation(out=t2[:, 0:16], in_=ga[:, 16:32], func=AF.Exp, scale=-1.0), [ga], [t2])
            k.op("dve", lambda e: e.tensor_scalar(t2[:, 0:16], t2[:, 0:16], 1.0, None, ALU.add), [t2], [t2])
            k.op("dve", lambda e: e.reciprocal(G["beta"][:, ch, :], t2[:, 0:16]), [t2], [G["beta"]])
            k.op("dve", lambda e: e.tensor_scalar(G["nbeta"][:, ch, :], G["beta"][:, ch, :], -1.0, None, ALU.mult),
                 [G["beta"]], [G["nbeta"]])
            ps = P[ch % 2]
            k.op("pe", lambda e: e.matmul(ps[:, 0:16], lhsT=tri, rhs=g[:, 0:16], start=True, stop=True), [cst, g], [ps], sig=False)
            k.op("pe", lambda e: e.matmul(ps[:, 16:32], lhsT=ones, rhs=g[:, 0:16], start=True, stop=True), [cst, g], [ps])
            k.op("act", lambda e: e.copy(out=G["gcum"][:, ch, :], in_=ps[:, 0:16]), [ps], [G["gcum"]])
            k.op("act", lambda e: e.activation(out=G["egl"][:, ch, :], in_=ps[:, 16:32], func=AF.Exp), [ps], [G["egl"]])
            k.op("dve", lambda e: e.tensor_tensor(out=t2[:, 16:32], in0=ps[:, 16:32], in1=G["gcum"][:, ch, :], op=ALU.subtract),
                 [ps, G["gcum"]], [t2])
            k.op("act", lambda e: e.activation(out=G["edec"][:, ch, :], in_=t2[:, 16:32], func=AF.Exp), [t2], [G["edec"]])
            ps2 = P[2 + ch % 2]
            k.op("pe", lambda e: e.transpose(out=ps2[0:16, 0:128], in_=G["gcum"][:, ch, :], identity=ident),
                 [G["gcum"], cst], [ps2])
            k.op("act", lambda e: e.copy(out=gcumT[:, ch, :], in_=ps2[0:16, 0:128]), [ps2], [gcumT])


        def head_gen(h, B):
            raw = B['raw']
            y = B['y']
            sq = B['sq']
            rn = B['rn']
            St = B['St']
            W = B['W']
            OSB = B['OSB']
            NY = B['NY']
            Mb = B['Mb']

            def nps():
                p = B['banks'][B['pi'] % 2]
                B['pi'] += 1
                return p
            k.op('dve', lambda e: e.memset(St[:, :], 0.0), [], [St])
            yield
            for tl in range((NCH + 3) // 4):
                tok0 = tl * 512
                nch_t = min(4, NCH - tl * 4)
                ntk = nch_t * 128
                for gi in range(3):
                    row0 = gi * 2048 + h * 128
                    if tl == 0:
                        k.op('dve', lambda e: e.memset(raw[gi][:, 0:3], 0.0), [], [raw[gi]])
                        yield
                        k.dma('sp', raw[gi][:, 3:3 + ntk], X['qkvT'][row0:row0 + 128, 0:ntk], [X['qkvT']], [raw[gi]], raw[gi])
                        yield
                    else:
                        k.dma('sp', raw[gi][:, 0:3 + ntk], X['qkvT'][row0:row0 + 128, tok0 - 3:tok0 + ntk], [X['qkvT']], [raw[gi]], raw[gi])
                        yield
                    ce = 'dve'
                    wi = gi * 16 + h
                    k.op(ce, lambda e: e.tensor_scalar(y[gi][:, 0:ntk], raw[gi][:, 0:ntk], cw[:, wi, 0:1], None, ALU.mult), [raw[gi], cw], [y[gi]])
                    yield
                    for kk in range(1, 4):
                        k.op(ce, lambda e: e.scalar_tensor_tensor(out=y[gi][:, 0:ntk], in0=raw[gi][:, kk:kk + ntk], scalar=cw[:, wi, kk:kk + 1], in1=y[gi][:, 0:ntk], op0=ALU.mult, op1=ALU.add), [raw[gi], cw, y[gi]], [y[gi]])
                        yield
                    k.op('act', lambda e: e.activation(out=y[gi][:, 0:ntk], in_=y[gi][:, 0:ntk], func=AF.Silu), [y[gi]], [y[gi]])
                    yield
                for gi in range(2):
                    k.op('dve', lambda e: e.tensor_tensor(out=sq[:, 0:ntk], in0=y[gi][:, 0:ntk], in1=y[gi][:, 0:ntk], op=ALU.mult), [y[gi]], [sq])
                    yield
                    for hf in range((ntk + 511) // 512):
                        w_ = min(512, ntk - hf * 512)
                        ps = nps()
                        k.op('pe', lambda e: e.matmul(ps[:, 0:w_], lhsT=ones, rhs=sq[:, hf * 512:hf * 512 + w_], start=True, stop=True), [cst, sq], [ps])
                        yield
                        sc_ = 128.0 if gi == 0 else 1.0
                        k.op('act', lambda e: e.activation(out=rn[:, hf * 512:hf * 512 + w_], in_=ps[:, 0:w_], func=AF.Sqrt, scale=sc_, bias=sc_ * EPS), [ps], [rn])
                        yield
                    k.op('dve', lambda e: e.reciprocal(rn[:, 0:ntk], rn[:, 0:ntk]), [rn], [rn])
                    yield
                    k.op('dve', lambda e: e.tensor_tensor(out=y[gi][:, 0:ntk], in0=y[gi][:, 0:ntk], in1=rn[:, 0:ntk], op=ALU.mult), [y[gi], rn], [y[gi]])
                    yield
                for cl in range(nch_t):
                    ch = tl * 4 + cl
                    cs_ = slice(cl * 128, (cl + 1) * 128)
                    qn = y[0].t[:, cs_]
                    kn = y[1].t[:, cs_]
                    vv = y[2].t[:, cs_]
                    gc = G['gcum'].t[:, ch, h:h + 1]
                    psB = nps()
                    k.op('pe', lambda e: e.matmul(psB[:, 0:128], lhsT=oh[0:16, h * 128:(h + 1) * 128], rhs=gcumT[0:16, ch, :], start=True, stop=True), [oh, gcumT], [psB])
                    yield
                    k.op('dve', lambda e: e.tensor_scalar(W['Dm'][:, :], psB[:, 0:128], gc, 0.0, ALU.subtract, ALU.min), [psB, G['gcum']], [W['Dm']])
                    yield
                    k.op('act', lambda e: e.activation(out=W['DT'][:, :], in_=W['Dm'][:, :], func=AF.Exp), [W['Dm']], [W['DT']])
                    yield
                    k.op('act', lambda e: e.activation(out=W['Eg'][:, :], in_=psB[:, 0:128], func=AF.Exp), [psB], [W['Eg']])
                    yield
                    if GDN_STAGE < 1:
                        continue
                    psK = nps()
                    k.op('pe', lambda e: e.matmul(psK[:, 0:128], lhsT=kn, rhs=kn, start=True, stop=True), [y[1]], [psK], sig=False)
                    yield
                    k.op('pe', lambda e: e.matmul(psK[:, 128:256], lhsT=kn, rhs=qn, start=True, stop=True), [y[1], y[0]], [psK])
                    yield
                    k.op(CH_POOL, lambda e: e.tensor_tensor(out=W['DTs'][:, :], in0=W['DT'][:, :], in1=tris, op=ALU.mult), [W['DT'], cst], [W['DTs']])
                    yield
                    k.op(CH_POOL, lambda e: e.tensor_tensor(out=W['DTi'][:, :], in0=W['DT'][:, :], in1=tri, op=ALU.mult), [W['DT'], cst], [W['DTi']])
                    yield
                    if GDN_STAGE < 2:
                        continue
                    N0 = NY[0]
                    k.op('dve', lambda e: e.scalar_tensor_tensor(out=N0[:, 0:128], in0=psK[:, 0:128], scalar=G['nbeta'].t[:, ch, h:h + 1], in1=W['DTs'][:, :], op0=ALU.mult, op1=ALU.mult), [psK, G['nbeta'], W['DTs']], [N0])
                    yield
                    k.op('dve', lambda e: e.tensor_tensor(out=W['QKm'][:, :], in0=psK[:, 128:256], in1=W['DTi'][:, :], op=ALU.mult), [psK, W['DTi']], [W['QKm']])
                    yield
                    if GDN_STAGE < 3:
                        continue
                    psT = nps()
                    k.op('pe', lambda e: e.matmul(psT[:, 0:128], lhsT=N0[:, 0:128], rhs=ident, start=True, stop=True), [N0, cst], [psT])
                    yield
                    if not _os.environ.get('SKIP_M0COPY'):
                        k.op(_os.environ.get('M0ENG', 'act'), (lambda e: e.copy(out=Mb[0][:, :], in_=psT[:, 0:128])) if _os.environ.get('M0ENG', 'act') == 'act' else lambda e: e.tensor_copy(Mb[0][:, :], psT[:, 0:128]), [psT], [Mb[0]])
                        yield
                    if not _os.environ.get('GDN_SKIPY1'):
                        k.op(CH_POOL, lambda e: e.tensor_tensor(out=NY[1][:, 128:256], in0=N0[:, 0:128], in1=ident, op=ALU.add), [N0, cst], [NY[1]])
                        yield
                    if GDN_STAGE < 4:
                        continue
                    ps1 = nps()
                    LV0 = int(_os.environ.get('LV0', '9'))
                    k.op('pe', lambda e: e.matmul(ps1[:, 0:128], lhsT=Mb[0][:, :], rhs=N0[:, 0:128], start=True, stop=True), [Mb[0], N0], [ps1], sig=LV0 < 2)
                    yield
                    if LV0 >= 2:
                        k.op('pe', lambda e: e.matmul(ps1[:, 128:256], lhsT=N0[:, 0:128], rhs=Mb[0][:, :], start=True, stop=True), [Mb[0], N0], [ps1])
                        yield
                    if LV0 >= 3:
                        k.op('act', lambda e: e.copy(out=NY[1][:, 0:128], in_=ps1[:, 0:128]), [ps1], [NY[1]])
                        yield
                    if LV0 >= 4:
                        k.op('act', lambda e: e.copy(out=Mb[1][:, :], in_=ps1[:, 128:256]), [ps1], [Mb[1]])
                        yield
                    if GDN_STAGE < 5:
                        continue
                    cur = 1
                    for lv in range(1, 6):
                        nyc = NY[cur]
                        nyn = NY[1 - cur]
                        mc = Mb[cur]
                        mn = Mb[1 - cur]
                        ps2 = nps()
                        k.op('pe', lambda e: e.matmul(ps2[:, 0:256], lhsT=mc[:, :], rhs=nyc[:, 0:256], start=True, stop=True), [mc, nyc], [ps2], sig=False)
                        yield
                        k.op('pe', lambda e: e.matmul(ps2[:, 256:384], lhsT=nyc[:, 0:128], rhs=mc[:, :], start=True, stop=True), [mc, nyc], [ps2])
                        yield
                        k.op('act', lambda e: e.copy(out=nyn[:, 0:128], in_=ps2[:, 0:128]), [ps2], [nyn])
                        yield
                        k.op('dve', lambda e: e.tensor_tensor(out=nyn[:, 128:256], in0=ps2[:, 128:256], in1=nyc[:, 128:256], op=ALU.add), [ps2, nyc], [nyn])
                        yield
                        k.op('act', lambda e: e.copy(out=mn[:, :], in_=ps2[:, 256:384]), [ps2], [mn])
                        yield
                        cur = 1 - cur
                    if GDN_STAGE < 6:
                        continue
                    ps3 = nps()
                    k.op('pe', lambda e: e.matmul(ps3[:, 0:128], lhsT=Mb[cur][:, :], rhs=NY[cur][:, 128:256], start=True, stop=True), [Mb[cur], NY[cur]], [ps3])
                    yield
                    k.op('dve', lambda e: e.tensor_tensor(out=W['ZT'][:, :], in0=ps3[:, 0:128], in1=NY[cur][:, 128:256], op=ALU.add), [ps3, NY[cur]], [W['ZT']])
                    yield
                    if GDN_STAGE < 7:
                        continue
                    ps4 = nps()
                    k.op('pe', lambda e: e.transpose(out=ps4[:, 0:128], in_=kn, identity=ident), [y[1], cst], [ps4], sig=False)
                    yield
                    k.op('pe', lambda e: e.transpose(out=ps4[:, 128:256], in_=vv, identity=ident), [y[2], cst], [ps4])
                    yield
                    k.op('act', lambda e: e.activation(out=W['kdec'][:, :], in_=ps4[:, 0:128], func=AF.Copy, scale=G['edec'].t[:, ch, h:h + 1]), [ps4, G['edec']], [W['kdec']])
                    yield
                    k.op('act', lambda e: e.copy(out=W['vtok'][:, :], in_=ps4[:, 128:256]), [ps4], [W['vtok']])
                    yield
                    k.op(CH_POOL, lambda e: e.tensor_tensor(out=W['kegT'][:, :], in0=kn, in1=W['Eg'][:, :], op=ALU.mult), [y[1], W['Eg']], [W['kegT']])
                    yield
                    k.op(CH_POOL, lambda e: e.tensor_tensor(out=W['qdT'][:, :], in0=qn, in1=W['Eg'][:, :], op=ALU.mult), [y[0], W['Eg']], [W['qdT']])
                    yield
                    if GDN_STAGE < 8:
                        continue
                    ps5 = nps()
                    k.op('pe', lambda e: e.matmul(ps5[:, 0:128], lhsT=W['kegT'][:, :], rhs=St[:, :], start=True, stop=True), [W['kegT'], St], [ps5])
                    yield
                    k.op('dve', lambda e: e.scalar_tensor_tensor(out=W['r'][:, :], in0=ps5[:, 0:128], scalar=-1.0, in1=W['vtok'][:, :], op0=ALU.mult, op1=ALU.add), [W['vtok'], ps5], [W['r']])
                    yield
                    k.op('pe', lambda e: e.matmul(ps5[:, 128:256], lhsT=W['ZT'][:, :], rhs=W['r'][:, :], start=True, stop=True), [W['ZT'], W['r']], [ps5])
                    yield
                    k.op('act', lambda e: e.activation(out=W['vnew'][:, :], in_=ps5[:, 128:256], func=AF.Copy, scale=G['beta'].t[:, ch, h:h + 1]), [ps5, G['beta']], [W['vnew']])
                    yield
                    ps6 = nps()
                    k.op('pe', lambda e: e.matmul(ps6[:, 0:128], lhsT=W['qdT'][:, :], rhs=St[:, :], start=True, stop=False), [W['qdT'], St], [ps6], sig=False)
                    yield
                    k.op('pe', lambda e: e.matmul(ps6[:, 0:128], lhsT=W['QKm'][:, :], rhs=W['vnew'][:, :], start=False, stop=True), [W['QKm'], W['vnew']], [ps6], sig=False)
                    yield
                    k.op('pe', lambda e: e.matmul(ps6[:, 128:256], lhsT=W['kdec'][:, :], rhs=W['vnew'][:, :], start=True, stop=True), [W['kdec'], W['vnew']], [ps6])
                    yield
                    osb_ = OSB[ch % 2]
                    k.op('act', lambda e: e.copy(out=osb_[:, :], in_=ps6[:, 0:128]), [ps6], [osb_])
                    yield
                    k.dma(_os.environ.get('OQ', 'sp'), X['oscr'][ch * 128:(ch + 1) * 128, h * 128:(h + 1) * 128], osb_[:, :], [osb_], [X['oscr']], X['oscr'])
                    yield
                    k.op('dve', lambda e: e.tensor_scalar(St[:, :], St[:, :], G['egl'].t[:, ch, h:h + 1], None, ALU.mult), [St, G['egl']], [St])
                    yield
                    k.op('dve', lambda e: e.tensor_tensor(out=St[:, :], in0=ps6[:, 128:256], in1=St[:, :], op=ALU.add), [St, ps6], [St])
                    yield

        HG = C.get("gdn_interleave", 4)
        ctxs = []
        for ci in range(HG):
            Bc = dict(raw=[k.sb(es, f"raw{ci}_{i}", [128, 515], F32) for i in range(3)],
                      y=[k.sb(es, f"y{ci}_{i}", [128, 512], F32) for i in range(3)],
                      sq=k.sb(es, f"sq{ci}", [128, 512], F32), rn=k.sb(es, f"rn{ci}", [128, 512], F32),
                      St=k.sb(es, f"S{ci}", [128, 128], F32), W={},
                      OSB=[k.sb(es, f"osb{ci}_{i}", [128, 128], F32) for i in range(2)],
                      NY=[k.sb(es, f"NY{ci}_{i}", [128, 256], F32) for i in range(2)],
                      Mb=[k.sb(es, f"Mb{ci}_{i}", [128, 128], F32) for i in range(2)],
                      banks=[P[(2 * ci) % 8], P[(2 * ci + 1) % 8]], pi=0)
            for n in ("Dm", "DT", "Eg", "DTs", "DTi", "QKm", "M0", "kdec", "vtok", "kegT", "qdT", "r", "vnew", "ZT"):
                Bc["W"][n] = k.sb(es, f"w{ci}_" + n, [128, 128], F32)
            ctxs.append(Bc)
        for h0 in range(0, NH, HG):
            gens = []
            for ci, h in enumerate(range(h0, min(h0 + HG, NH))):
                ctxs[ci]["pi"] = 0
                gens.append(head_gen(h, ctxs[ci]))
            while gens:
                for g_ in list(gens):
                    try:
                        next(g_)
                    except StopIteration:
                        gens.remove(g_)
    k.barrier()


def rot_norm(k, src, dst, nh, ssq, rstd, tmp, gain, cs, scale, do_norm, eng="dve", rot=True):
    s3 = src.t[:, 0:nh * 128].rearrange("p (h d) -> p h d", h=nh)
    t3 = tmp.t[:, 0:nh * 128].rearrange("p (h d) -> p h d", h=nh)
    d3 = dst.t[:, 0:nh * 128].rearrange("p (h d) -> p h d", h=nh)
    if do_norm:
        k.op(eng, lambda e: e.tensor_tensor(out=t3, in0=s3, in1=s3, op=ALU.mult), [src], [tmp])
        k.op("dve", lambda e: e.tensor_reduce(out=ssq[:, 0:nh], in_=t3, axis=AX.X, op=ALU.add), [tmp], [ssq])
        k.op("act", lambda e: e.activation(out=rstd[:, 0:nh], in_=ssq[:, 0:nh], func=AF.Sqrt, scale=1.0 / 128, bias=EPS),
             [ssq], [rstd])
        k.op("dve", lambda e: e.reciprocal(rstd[:, 0:nh], rstd[:, 0:nh]), [rstd], [rstd])
        k.op(eng, lambda e: e.tensor_tensor(out=t3, in0=s3, in1=rstd.t[:, 0:nh].unsqueeze(2).to_broadcast([128, nh, 128]),
                                            op=ALU.mult), [src, rstd], [tmp])
        k.op(eng, lambda e: e.scalar_tensor_tensor(out=t3, in0=t3, scalar=scale,
                                                   in1=gain.t[:, 0:128].unsqueeze(1).to_broadcast([128, nh, 128]),
                                                   op0=ALU.mult, op1=ALU.mult), [tmp, gain], [tmp])
        base = tmp
        b3 = t3
    else:
        k.op(eng, lambda e: e.tensor_scalar(t3, s3, scale, None, ALU.mult), [src], [tmp])
        base = tmp
        b3 = t3
    if not rot:
        k.op("act", lambda e: e.copy(out=d3, in_=b3), [base], [dst])
        return
    cosb = cs.t[:, 0:16].unsqueeze(1).to_broadcast([128, nh, 16])
    sinb = cs.t[:, 16:32].unsqueeze(1).to_broadcast([128, nh, 16])
    k.op("act", lambda e: e.copy(out=d3[:, :, 32:128], in_=b3[:, :, 32:128]), [base], [dst])
    ra = src.t[:, 0:nh * 128].rearrange("p (h d) -> p h d", h=nh)
    k.op(eng, lambda e: e.tensor_tensor(out=ra[:, :, 32:48], in0=b3[:, :, 0:16], in1=cosb, op=ALU.mult), [base, cs], [src])
    k.op(eng, lambda e: e.tensor_tensor(out=ra[:, :, 48:64], in0=b3[:, :, 16:32], in1=sinb, op=ALU.mult), [base, cs], [src])
    k.op(eng, lambda e: e.tensor_tensor(out=ra[:, :, 64:80], in0=b3[:, :, 16:32], in1=cosb, op=ALU.mult), [base, cs], [src])
    k.op(eng, lambda e: e.tensor_tensor(out=ra[:, :, 80:96], in0=b3[:, :, 0:16], in1=sinb, op=ALU.mult), [base, cs], [src])
    k.op(eng, lambda e: e.tensor_tensor(out=d3[:, :, 0:16], in0=ra[:, :, 32:48], in1=ra[:, :, 48:64], op=ALU.subtract), [src], [dst])
    k.op(eng, lambda e: e.tensor_tensor(out=d3[:, :, 16:32], in0=ra[:, :, 64:80], in1=ra[:, :, 80:96], op=ALU.add), [src], [dst])


def phase3(k, X, P, C):
    cst = C["cst"]
    ident = C["ident"]
    NB = C["n_own_blocks"]
    NKB = C["n_key_blocks"]
    with ExitStack() as es:
        identb = k.sb(es, "identb", [128, 128], BF16)
        onesb = k.sb(es, "onesb", [128, 128], BF16)
        k.op("dve", lambda e: e.tensor_copy(identb[:, :], ident), [cst], [identb])
        k.op("dve", lambda e: e.tensor_copy(onesb[:, :], cst.t[:, 384:512]), [cst], [onesb])
        KT = k.sb(es, "KT", [128, 4, S], BF16)
        IKT = k.sb(es, "IKT", [128, S], BF16)
        V = k.sb(es, "V", [128, 32, 512], BF16)
        kpos = k.sb(es, "kpos", [128, 256], F32)
        posr = k.sb(es, "posr", [128, 1], F32)
        gq_n = k.sb(es, "gq_n", [128, 128], F32)
        gk_n = k.sb(es, "gk_n", [128, 128], F32)
        gd_n = k.sb(es, "gd_n", [128, 128], F32)
        jsel = k.sb(es, "jsel", [128, 1], F32)
        for b_, nm in ((gq_n, "att_q_norm"), (gk_n, "att_k_norm"), (gd_n, "gdn_norm")):
            k.dma("sp", b_[:, :], X[nm].t.partition_broadcast(128), [X[nm]], [b_], b_)
        k.dma("sp", kpos[:, :], X["kpos"].t.partition_broadcast(128), [X["kpos"]], [kpos], kpos)
        k.dma("sp", jsel[:, :], X["jsel"][:, :], [X["jsel"]], [jsel], jsel)
        med = [k.sb(es, f"med{i}", [128, 2048], F32) for i in range(3)]
        bfb = k.sb(es, "bfb", [128, 2048], BF16)
        cs = k.sb(es, "cs", [128, 32], F32)
        pos = k.sb(es, "pos", [128, 1], F32)
        iw = k.sb(es, "iw", [128, 32], F32)
        ssq = k.sb(es, "ssq3", [128, 16], F32)
        rstd = k.sb(es, "rstd3", [128, 16], F32)
        QT = k.sb(es, "QT", [128, 16, 128], BF16)
        IQT = k.sb(es, "IQT", [128, 32, 128], BF16)
        sc = k.sb(es, "sc", [128, S], F32)
        wk = k.sb(es, "wk", [128, S], F32)
        m8 = k.sb(es, "m8", [128, 8], F32)
        tau = k.sb(es, "tau", [128, 1], F32)
        mk = k.sb(es, "mk", [128, S], BF16)
        mkT = k.sb(es, "mkT", [128, 32, 128], BF16)
        rl = [k.sb(es, f"rl{i}", [128, 512], F32) for i in range(3)]
        pe_ = [k.sb(es, f"pe{i}", [128, 512], BF16) for i in range(3)]
        pm = [k.sb(es, f"pm{i}", [128, 512], BF16) for i in range(3)]
        rinv = k.sb(es, "rinv", [128, 512], F32)
        obT = [k.sb(es, f"obT{i}", [128, 512], BF16) for i in range(2)]
        mT = [k.sb(es, f"mT{i}", [128, 512], BF16) for i in range(2)]
        pi = [0]

        def nps():
            p = P[pi[0] % 8]; pi[0] += 1
            return p

        for sb_ in range(NKB):
            t0 = sb_ * 128
            a = med[0]; tm = med[1]
            k.dma("sp", a[:, 0:1024], X["akv"][t0:t0 + 128, :], [X["akv"]], [a], a)
            k.dma("sp", a[:, 1024:1152], X["ikr"][t0:t0 + 128, :], [X["ikr"]], [a], a)
            k.dma("sp", cs[:, :], X["cs_all"][t0:t0 + 128, :], [X["cs_all"]], [cs], cs)
            k.op("act", lambda e: e.copy(out=V[:, sb_, :], in_=a[:, 512:1024]), [a], [V])
            rot_norm(k, _view(a, 1024, 128), _view(bfb, 1024, 128), 1, ssq, rstd, _view(tm, 1024, 128), gk_n, cs, 1.0, False)
            rot_norm(k, _view(a, 0, 512), _view(bfb, 0, 512), 4, ssq, rstd, _view(tm, 0, 512), gk_n, cs, 1.0, True)
            ps = nps()
            psb = ps.t[:, :].bitcast(BF16)
            for g in range(4):
                k.op("pe", lambda e: e.transpose(out=psb[:, g * 128:(g + 1) * 128], in_=bfb[:, g * 128:(g + 1) * 128], identity=identb[:, :]),
                     [bfb, identb], [ps], sig=False)
            k.op("pe", lambda e: e.transpose(out=psb[:, 512:640], in_=bfb[:, 1024:1152], identity=identb[:, :]),
                 [bfb, identb], [ps])
            k.op("act", lambda e: e.copy(out=KT[:, :, t0:t0 + 128], in_=psb[:, 0:512].rearrange("p (g s) -> p g s", g=4)),
                 [ps], [KT])
            k.op("act", lambda e: e.copy(out=IKT[:, t0:t0 + 128], in_=psb[:, 512:640]), [ps], [IKT])

        for n in range(NB):
            r0 = n * 128
            NK = min(2 * n + 2, NKB)
            NKc = NK * 128
            aq = med[0]; tm = med[1]
            k.dma("sp", aq[:, :], X["aq"][r0:r0 + 128, :], [X["aq"]], [aq], aq)
            k.dma("sp", iw[:, :], X["iw"][r0:r0 + 128, :], [X["iw"]], [iw], iw)
            k.dma("sp", cs[:, :], X["cs_own"][r0:r0 + 128, :], [X["cs_own"]], [cs], cs)
            k.dma("sp", pos[:, :], X["pos_own"][r0:r0 + 128, :], [X["pos_own"]], [pos], pos)
            k.op("dve", lambda e: e.tensor_scalar(iw[:, :], iw[:, :], 32 ** -0.5, None, ALU.mult), [iw], [iw])
            rot_norm(k, aq, bfb, 16, ssq, rstd, tm, gq_n, cs, 128 ** -0.5, True)
            for g4 in range(4):
                ps = nps(); psb = ps.t[:, :].bitcast(BF16)
                for j in range(4):
                    hq = g4 * 4 + j
                    k.op("pe", lambda e: e.transpose(out=psb[:, j * 128:(j + 1) * 128], in_=bfb[:, hq * 128:(hq + 1) * 128],
                                                     identity=identb[:, :]), [bfb, identb], [ps], sig=(j == 3))
                k.op("act", lambda e: e.copy(out=QT[:, g4 * 4:(g4 + 1) * 4, :], in_=psb[:, 0:512].rearrange("p (g s) -> p g s", g=4)),
                     [ps], [QT])
            for half in range(2):
                iqr = med[0]; itm = med[1]
                k.dma("sp", iqr[:, :], X["iq"][r0:r0 + 128, half * 2048:(half + 1) * 2048], [X["iq"]], [iqr], iqr)
                rot_norm(k, iqr, bfb, 16, ssq, rstd, itm, gq_n, cs, 128 ** -0.5, False, eng="dve")
                for g4 in range(4):
                    ps = nps(); psb = ps.t[:, :].bitcast(BF16)
                    for j in range(4):
                        hq = g4 * 4 + j
                        k.op("pe", lambda e: e.transpose(out=psb[:, j * 128:(j + 1) * 128], in_=bfb[:, hq * 128:(hq + 1) * 128],
                                                         identity=identb[:, :]), [bfb, identb], [ps], sig=(j == 3))
                    k.op("act", lambda e: e.copy(out=IQT[:, half * 16 + g4 * 4:half * 16 + (g4 + 1) * 4, :],
                                                 in_=psb[:, 0:512].rearrange("p (g s) -> p g s", g=4)), [ps], [IQT])
            nkt = (NKc + 511) // 512
            scv = [Buf(sc.t[:, kt * 512:kt * 512 + min(512, NKc - kt * 512)], f"scv{kt}") for kt in range(nkt)]
            ri = 0
            for kt in range(nkt):
                w_ = min(512, NKc - kt * 512)
                acc_eng = "dve"
                for hi in range(32):
                    ps = nps()
                    k.op("pe", lambda e: e.matmul(ps[:, 0:w_], lhsT=IQT[:, hi, :], rhs=IKT[:, kt * 512:kt * 512 + w_],
                                                  start=True, stop=True), [IQT, IKT], [ps])
                    r_ = rl[ri % 3]; ri += 1
                    k.op("act", lambda e: e.activation(out=r_[:, 0:w_], in_=ps[:, 0:w_], func=AF.Relu), [ps], [r_])
                    if hi == 0:
                        k.op(acc_eng, lambda e: e.tensor_scalar(sc[:, kt * 512:kt * 512 + w_], r_[:, 0:w_], iw[:, 0:1], None, ALU.mult),
                             [r_, iw], [scv[kt], sc])
                    else:
                        k.op(acc_eng, lambda e: e.scalar_tensor_tensor(out=sc[:, kt * 512:kt * 512 + w_], in0=r_[:, 0:w_],
                                                                       scalar=iw[:, hi:hi + 1], in1=sc[:, kt * 512:kt * 512 + w_],
                                                                       op0=ALU.mult, op1=ALU.add), [r_, iw, scv[kt]], [scv[kt]])
            c0 = NKc - 256
            k.op("dve", lambda e: e.tensor_scalar(posr[:, :], pos[:, :], float(-c0), None, ALU.add), [pos], [posr])
            k.op("dve", lambda e: e.tensor_scalar(wk[:, c0:NKc], kpos[:, 0:256], posr[:, 0:1], -1e30, ALU.is_gt, ALU.mult),
                 [kpos, posr], [wk])
            k.op("dve", lambda e: e.tensor_tensor(out=sc[:, c0:NKc], in0=sc[:, c0:NKc], in1=wk[:, c0:NKc], op=ALU.add),
                 [wk] + scv, [sc] + scv)
            if NKc > 256:
                src = sc
                for rnd in range(32):
                    k.op("dve", lambda e: e.max(out=m8[:, :], in_=src[:, 0:NKc]), [src], [m8])
                    if rnd < 31:
                        k.op("dve", lambda e: e.match_replace(out=wk[:, 0:NKc], in_to_replace=m8[:, :], in_values=src[:, 0:NKc],
                                                              imm_value=-3e38), [src, m8], [wk])
                        src = wk
                k.op("dve", lambda e: e.tensor_scalar(tau[:, :], m8[:, 7:8], -1e29, None, ALU.max), [m8], [tau])
            else:
                k.op("dve", lambda e: e.memset(tau[:, :], -1e29), [], [tau])
            k.op("dve", lambda e: e.tensor_scalar(mk[:, 0:NKc], sc[:, 0:NKc], tau[:, 0:1], None, ALU.is_ge), [sc, tau], [mk])
            for kb4 in range((NK + 3) // 4):
                ps = nps(); psb = ps.t[:, :].bitcast(BF16)
                nb_ = min(4, NK - kb4 * 4)
                for j in range(nb_):
                    kb = kb4 * 4 + j
                    k.op("pe", lambda e: e.transpose(out=psb[:, j * 128:(j + 1) * 128], in_=mk[:, kb * 128:(kb + 1) * 128],
                                                     identity=identb[:, :]), [mk, identb], [ps], sig=(j == nb_ - 1))
                k.op("act", lambda e: e.copy(out=mkT[:, kb4 * 4:kb4 * 4 + nb_, :],
                                             in_=psb[:, 0:nb_ * 128].rearrange("p (g s) -> p g s", g=nb_)), [ps], [mkT])
            for g in range(4):
                psO = P[0]; psR = P[1]
                xi = 0
                for kb in range(NK):
                    psS = P[2 + (pi[0] % 6)]; pi[0] += 1
                    k.op("pe", lambda e: e.matmul(psS[:, 0:512], lhsT=KT[:, g, kb * 128:(kb + 1) * 128],
                                                  rhs=QT[:, g * 4:(g + 1) * 4, :].rearrange("p g s -> p (g s)"), start=True, stop=True),
                         [KT, QT], [psS])
                    pe1 = pe_[xi % 3]; pm1 = pm[xi % 3]; xi += 1
                    k.op("act", lambda e: e.activation(out=pe1[:, :], in_=psS[:, 0:512], func=AF.Exp), [psS], [pe1])
                    me = "dve"
                    k.op(me, lambda e: e.tensor_tensor(out=pm1[:, :].rearrange("p (g s) -> p g s", g=4),
                                                       in0=pe1[:, :].rearrange("p (g s) -> p g s", g=4),
                                                       in1=mkT.t[:, kb, :].unsqueeze(1).to_broadcast([128, 4, 128]), op=ALU.mult),
                         [pe1, mkT], [pm1])
                    k.op("pe", lambda e: e.matmul(psO[:, 0:512], lhsT=V[:, kb, g * 128:(g + 1) * 128], rhs=pm1[:, :],
                                                  start=(kb == 0), stop=(kb == NK - 1)), [V, pm1], [psO], sig=False)
                    k.op("pe", lambda e: e.matmul(psR[:, 0:512], lhsT=onesb[:, :], rhs=pm1[:, :],
                                                  start=(kb == 0), stop=(kb == NK - 1)), [onesb, pm1], [psR])
                k.op("act", lambda e: e.copy(out=rinv[:, :], in_=psR[:, 0:512]), [psR], [rinv])
                k.op("dve", lambda e: e.reciprocal(rinv[:, :], rinv[:, :]), [rinv], [rinv])
                ob = obT[g % 2]
                k.op("dve", lambda e: e.tensor_tensor(out=ob[:, :], in0=psO[:, 0:512], in1=rinv[:, :], op=ALU.mult), [psO, rinv], [ob])
                for hh in range(4):
                    rr = 2048 + (g * 4 + hh) * 128
                    k.dma("sp", X["mixT"][rr:rr + 128, r0:r0 + 128], ob[:, hh * 128:(hh + 1) * 128], [ob], [X["mixT"]], X["mixT"])
            A = med[0]; Bt = med[1]; gzt = med[2]
            k.dma("sp", A[:, :], X["oscr"][(2 * n) * 128:(2 * n + 1) * 128, :], [X["oscr"]], [A], A)
            k.dma("sp", Bt[:, :], X["oscr"][(2 * n + 1) * 128:(2 * n + 2) * 128, :], [X["oscr"]], [Bt], Bt)
            k.dma("sp", gzt[:, 0:2048], X["gz"][r0:r0 + 128, :], [X["gz"]], [gzt], gzt)
            k.op("dve", lambda e: e.tensor_tensor(out=Bt[:, :], in0=Bt[:, :], in1=A[:, :], op=ALU.subtract), [A, Bt], [Bt])
            k.op("dve", lambda e: e.scalar_tensor_tensor(out=A[:, :], in0=Bt[:, :], scalar=jsel[:, 0:1], in1=A[:, :],
                                                         op0=ALU.mult, op1=ALU.add), [A, Bt, jsel], [A])
            A3 = A.t[:, :].rearrange("p (h d) -> p h d", h=16)
            B3 = Bt.t[:, :].rearrange("p (h d) -> p h d", h=16)
            k.op("dve", lambda e: e.tensor_tensor(out=B3, in0=A3, in1=A3, op=ALU.mult), [A], [Bt])
            k.op("dve", lambda e: e.tensor_reduce(out=ssq[:, 0:16], in_=B3, axis=AX.X, op=ALU.add), [Bt], [ssq])
            k.op("act", lambda e: e.activation(out=rstd[:, 0:16], in_=ssq[:, 0:16], func=AF.Sqrt, scale=1.0 / 128, bias=EPS), [ssq], [rstd])
            k.op("dve", lambda e: e.reciprocal(rstd[:, 0:16], rstd[:, 0:16]), [rstd], [rstd])
            k.op("dve", lambda e: e.tensor_tensor(out=A3, in0=A3, in1=rstd.t[:, 0:16].unsqueeze(2).to_broadcast([128, 16, 128]), op=ALU.mult),
                 [A, rstd], [A])
            k.op("dve", lambda e: e.tensor_tensor(out=A3, in0=A3, in1=gd_n.t[:, :].unsqueeze(1).to_broadcast([128, 16, 128]), op=ALU.mult),
                 [A, gd_n], [A])
            k.op("act", lambda e: e.activation(out=gzt[:, 0:2048], in_=gzt[:, 0:2048], func=AF.Silu), [gzt], [gzt])
            k.op("dve", lambda e: e.tensor_tensor(out=A[:, :], in0=A[:, :], in1=gzt[:, 0:2048], op=ALU.mult), [A, gzt], [A])
            for g4 in range(4):
                ps = nps()
                for j in range(4):
                    hh = g4 * 4 + j
                    k.op("pe", lambda e: e.transpose(out=ps[:, j * 128:(j + 1) * 128], in_=A[:, hh * 128:(hh + 1) * 128], identity=ident),
                         [A, cst], [ps], sig=(j == 3))
                m_ = mT[g4 % 2]
                k.op("act", lambda e: e.copy(out=m_[:, :], in_=ps[:, 0:512]), [ps], [m_])
                for j in range(4):
                    rr = (g4 * 4 + j) * 128
                    k.dma("sp", X["mixT"][rr:rr + 128, r0:r0 + 128], m_[:, j * 128:(j + 1) * 128], [m_], [X["mixT"]], X["mixT"])
    k.barrier()


class _View:
    def __init__(self, parent, c0, n):
        self.p = parent
        self.t = parent.t[:, c0:c0 + n]
        self.name = parent.name + "_v"
        self.psum = False
        self.dram = False

    def __getitem__(self, key):
        return self.t[key]
    w = property(lambda self: self.p.w, lambda self, v: setattr(self.p, "w", v))
    r = property(lambda self: self.p.r, lambda self, v: setattr(self.p, "r", v))
    dsem = property(lambda self: self.p.dsem, lambda self, v: setattr(self.p, "dsem", v))
    dcnt = property(lambda self: self.p.dcnt, lambda self, v: setattr(self.p, "dcnt", v))


def _view(parent, c0, n):
    return _View(parent, c0, n)


def resid_evac_factory(k, X, src, row0, xr, ot, ctr):
    def f(off):
        def evac(q, ps, ncols):
            i = ctr[0] % len(xr); ctr[0] += 1
            xb = xr[i]; o = ot[i]
            r = row0 + q * 128
            k.dma("sp", xb[:, 0:ncols], src[r:r + 128, off:off + ncols], [src], [xb], xb)
            k.op("dve", lambda e: e.tensor_tensor(out=o[:, 0:ncols], in0=ps[:, 0:ncols], in1=xb[:, 0:ncols], op=ALU.add),
                 [ps, xb], [o])
            k.dma("sp", X["out"][r:r + 128, off:off + ncols], o[:, 0:ncols], [o], [X["out"]], X["out"])
        return evac
    return f


def phase4(k, X, P, C):
    with ExitStack() as es:
        hTt = es.enter_context(k.nc.sbuf_tensor("hT4", [128, 32, 512], BF16))
        hTb = Buf(hTt, "hT4")
        xr = [k.sb(es, f"xr{i}", [128, 512], F32) for i in range(4)]
        ot = [k.sb(es, f"ot4{i}", [128, 512], F32) for i in range(4)]
        ring = WRing(k, es)
        wv = X["w_out"].t.rearrange("(c p) n -> p c n", p=128)
        mv = X["mixT"].t.rearrange("(c p) t -> p c t", p=128)
        jobs = []
        ctr = [0]
        for t in range(C["n_own_tiles"]):
            mk_jobs(jobs, t, "N", X["w_out"], wv, 32, 0, D, (hTt, hTb),
                    resid_evac_factory(k, X, X["x_own"], t * 512, xr, ot, ctr))

        def prep(t):
            for c4 in range(4):
                k.dma("sp", hTt[:, c4 * 8:(c4 + 1) * 8, :], mv[:, c4 * 8:(c4 + 1) * 8, t * 512:(t + 1) * 512],
                      [X["mixT"]], [hTb], hTb)
        linear(k, ring, jobs, P, prep)
    k.barrier()


def phase5(k, X, P, C):
    cst = C["cst"]; ident = C["ident"]
    with ExitStack() as es:
        xt = [k.sb(es, f"xt5{i}", [128, D], F32) for i in range(2)]
        junk = k.sb(es, "junk5", [128, D], BF16)
        hTt = es.enter_context(k.nc.sbuf_tensor("hT5", [128, 32, 512], BF16))
        hTb = Buf(hTt, "hT5")
        ssq = k.sb(es, "ssq5", [128, 16], F32)
        rstd = k.sb(es, "rstd5", [128, 16], F32)
        gamq = k.sb(es, "gamq", [128, 32], F32)
        gamkv = k.sb(es, "gamkv", [128, 32], F32)
        qn_g = k.sb(es, "qn_g", [128, 128], F32)
        kn_g = k.sb(es, "kn_g", [128, 128], F32)
        identb = k.sb(es, "identb5", [128, 128], BF16)
        onesb = k.sb(es, "onesb5", [128, 128], BF16)
        k.op("dve", lambda e: e.tensor_copy(identb[:, :], ident), [cst], [identb])
        k.op("dve", lambda e: e.tensor_copy(onesb[:, :], cst.t[:, 384:512]), [cst], [onesb])
        k.dma("sp", gamq[:, :], X["norm_mem_q"][:, :], [X["norm_mem_q"]], [gamq], gamq)
        k.dma("sp", gamkv[:, :], X["norm_mem_kv"][:, :], [X["norm_mem_kv"]], [gamkv], gamkv)
        k.dma("sp", qn_g[:, :], X["mem_q_norm"].t.partition_broadcast(128), [X["mem_q_norm"]], [qn_g], qn_g)
        k.dma("sp", kn_g[:, :], X["mem_k_norm"].t.partition_broadcast(128), [X["mem_k_norm"]], [kn_g], kn_g)
        MKT = k.sb(es, "MKT", [128, 4, 256], BF16)
        MV = k.sb(es, "MV", [128, 2, 512], BF16)
        MQT = k.sb(es, "MQT", [128, 4, 512], BF16)
        moTt = es.enter_context(k.nc.sbuf_tensor("moT", [128, 4, 512], BF16))
        moTb = Buf(moTt, "moT")
        o32 = [k.sb(es, f"o32_{i}", [128, 512], F32) for i in range(2)]
        t32 = k.sb(es, "t32", [128, 512], F32)
        obf = k.sb(es, "obf5", [128, 512], BF16)
        pe_ = [k.sb(es, f"pe5{i}", [128, 512], BF16) for i in range(2)]
        rinv = k.sb(es, "rinv5", [128, 512], F32)
        xr = [k.sb(es, f"xr5{i}", [128, 512], F32) for i in range(4)]
        ot = [k.sb(es, f"ot5{i}", [128, 512], F32) for i in range(4)]
        ring = WRing(k, es, npieces=4)
        ctr = [0]
        oc = [0]
        wkv = X["w_mem_kv"].t.rearrange("(c p) n -> p c n", p=128)

        def kv_fac(off):
            def evac(q, ps, ncols):
                o = o32[oc[0] % 2]; oc[0] += 1
                k.op("act", lambda e: e.copy(out=o[:, :], in_=ps[:, 0:512]), [ps], [o])
                if off == 0:
                    rot_norm(k, o, obf, 4, ssq, rstd, t32, kn_g, None, 1.0, True, rot=False)
                    p2 = P[6 + q % 2]; pb = p2.t[:, :].bitcast(BF16)
                    for j in range(4):
                        k.op("pe", lambda e: e.transpose(out=pb[:, j * 128:(j + 1) * 128], in_=obf[:, j * 128:(j + 1) * 128],
                                                         identity=identb[:, :]), [obf, identb], [p2], sig=(j == 3))
                    k.op("act", lambda e: e.copy(out=MKT[:, :, q * 128:(q + 1) * 128],
                                                 in_=pb[:, 0:512].rearrange("p (g s) -> p g s", g=4)), [p2], [MKT])
                else:
                    k.op("dve", lambda e: e.tensor_copy(MV[:, q, :], o[:, :]), [o], [MV])
            return evac
        jobs = []
        mk_jobs(jobs, "kv", "N", X["w_mem_kv"], wkv, 32, 0, 1024, (hTt, hTb), kv_fac, nsub=2)
        linear(k, ring, jobs, P[0:4], lambda pid: build_hT(k, X["mem"], 0, xt, hTt, hTb, ssq, rstd, junk, gamkv, ident, cst, P[6:8], n_sub=2))
        wq = X["w_mem_q"].t.rearrange("(c p) n -> p c n", p=128)
        wo = X["w_mem_o"].t.rearrange("(c p) n -> p c n", p=128)
        for t in range(C["n_own_tiles"]):
            def q_fac(off):
                def evac(q, ps, ncols):
                    o = o32[oc[0] % 2]; oc[0] += 1
                    k.op("act", lambda e: e.copy(out=o[:, :], in_=ps[:, 0:512]), [ps], [o])
                    rot_norm(k, o, obf, 4, ssq, rstd, t32, qn_g, None, 128 ** -0.5, True, rot=False)
                    p2 = P[6 + q % 2]; pb = p2.t[:, :].bitcast(BF16)
                    for j in range(4):
                        k.op("pe", lambda e: e.transpose(out=pb[:, j * 128:(j + 1) * 128], in_=obf[:, j * 128:(j + 1) * 128],
                                                         identity=identb[:, :]), [obf, identb], [p2], sig=(j == 3))
                    k.op("act", lambda e: e.copy(out=MQT[:, :, q * 128:(q + 1) * 128],
                                                 in_=pb[:, 0:512].rearrange("p (g s) -> p g s", g=4)), [p2], [MQT])
                return evac
            jobs = []
            mk_jobs(jobs, t, "N", X["w_mem_q"], wq, 32, 0, 512, (hTt, hTb), q_fac)
            linear(k, ring, jobs, P[0:4], lambda pid: build_hT(k, X["out"], pid * 512, xt, hTt, hTb, ssq, rstd, junk, gamq, ident, cst, P[6:8]))
            for hd in range(4):
                psO = P[0]; psR = P[1]
                for mb in range(2):
                    psS = P[2 + mb]
                    k.op("pe", lambda e: e.matmul(psS[:, 0:512], lhsT=MKT[:, hd, mb * 128:(mb + 1) * 128], rhs=MQT[:, hd, :],
                                                  start=True, stop=True), [MKT, MQT], [psS])
                    pe1 = pe_[mb]
                    k.op("act", lambda e: e.activation(out=pe1[:, :], in_=psS[:, 0:512], func=AF.Exp), [psS], [pe1])
                    k.op("pe", lambda e: e.matmul(psO[:, 0:512], lhsT=MV[:, mb, hd * 128:(hd + 1) * 128], rhs=pe1[:, :],
                                                  start=(mb == 0), stop=(mb == 1)), [MV, pe1], [psO], sig=False)
                    k.op("pe", lambda e: e.matmul(psR[:, 0:512], lhsT=onesb[:, :], rhs=pe1[:, :],
                                                  start=(mb == 0), stop=(mb == 1)), [onesb, pe1], [psR])
                k.op("act", lambda e: e.copy(out=rinv[:, :], in_=psR[:, 0:512]), [psR], [rinv])
                k.op("dve", lambda e: e.reciprocal(rinv[:, :], rinv[:, :]), [rinv], [rinv])
                k.op("dve", lambda e: e.tensor_tensor(out=moTt[:, hd, :], in0=psO[:, 0:512], in1=rinv[:, :], op=ALU.mult),
                     [psO, rinv], [moTb])
            jobs = []
            mk_jobs(jobs, t, "N", X["w_mem_o"], wo, 4, 0, D, (moTt, moTb),
                    resid_evac_factory(k, X, X["out"], t * 512, xr, ot, ctr))
            linear(k, ring, jobs, P[0:4], lambda pid: None)
    k.barrier()


def phase6a(k, X, P, C):
    cst = C["cst"]; ident = C["ident"]
    with ExitStack() as es:
        xt = [k.sb(es, f"xt6{i}", [128, D], F32) for i in range(2)]
        junk = k.sb(es, "junk6", [128, D], BF16)
        hTt = es.enter_context(k.nc.sbuf_tensor("hT6", [128, 32, 512], BF16))
        hTb = Buf(hTt, "hT6")
        ssq = k.sb(es, "ssq6", [128, 1], F32)
        rstd = k.sb(es, "rstd6", [128, 1], F32)
        gam = k.sb(es, "gam6", [128, 32], F32)
        k.dma("sp", gam[:, :], X["norm_ffn"][:, :], [X["norm_ffn"]], [gam], gam)
        sg = [k.sb(es, f"sg{i}", [128, 512], F32) for i in range(8)]
        ab = [k.sb(es, f"ab{i}", [128, 512], BF16) for i in range(4)]
        ring = WRing(k, es)
        wg = X["w_gate"].t.rearrange("(c p) n -> p c n", p=128)
        wu = X["w_up"].t.rearrange("(c p) n -> p c n", p=128)
        jobs = []
        st = dict(g=0, a=0, sgmap={})
        for t in range(C["n_own_tiles"]):
            off = 0
            while off < FFN:
                n_ = min(512, FFN - off)

                def gfac(o_, t=t):
                    def evac(q, ps, ncols):
                        s_ = sg[st["g"] % 8]; st["g"] += 1
                        st["sgmap"][(t, o_, q)] = s_
                        k.op("act", lambda e: e.activation(out=s_[:, :], in_=ps[:, 0:512], func=AF.Silu), [ps], [s_])
                    return evac

                def ufac(o_, t=t, off=off):
                    def evac(q, ps, ncols):
                        s_ = st["sgmap"].pop((t, o_, q))
                        a_ = ab[st["a"] % 4]; st["a"] += 1
                        k.op("dve", lambda e: e.tensor_tensor(out=a_[:, :], in0=ps[:, 0:512], in1=s_[:, :], op=ALU.mult),
                             [ps, s_], [a_])
                        r = off + q * 128
                        k.dma("sp", X["aT"][r:r + 128, t * 512:(t + 1) * 512], a_[:, :], [a_], [X["aT"]], X["aT"])
                    return evac
                jobs.append(dict(pass_id=t, mode="T", wbuf=X["w_gate"], wv=wg, KC=32, c0=off, ncols=n_, hT=(hTt, hTb),
                                 evac=gfac(0), nsub=4))
                jobs.append(dict(pass_id=t, mode="T", wbuf=X["w_up"], wv=wu, KC=32, c0=off, ncols=n_, hT=(hTt, hTb),
                                 evac=ufac(0), nsub=4))
                off += n_
        linear(k, ring, jobs, P, lambda pid: build_hT(k, X["out"], pid * 512, xt, hTt, hTb, ssq, rstd, junk, gam, ident, cst, P[6:8]))
    k.barrier()


def phase6b(k, X, P, C):
    with ExitStack() as es:
        aTt = es.enter_context(k.nc.sbuf_tensor("aT6", [128, 86, 512], BF16))
        aTb = Buf(aTt, "aT6")
        xr = [k.sb(es, f"xr6{i}", [128, 512], F32) for i in range(4)]
        ot = [k.sb(es, f"ot6{i}", [128, 512], F32) for i in range(4)]
        ring = WRing(k, es, npieces=4)
        wd = X["w_down"].t.rearrange("(c p) n -> p c n", p=128)
        av = X["aT"].t.rearrange("(c p) t -> p c t", p=128)
        jobs = []
        ctr = [0]
        for t in range(C["n_own_tiles"]):
            mk_jobs(jobs, t, "N", X["w_down"], wd, 86, 0, D, (aTt, aTb),
                    resid_evac_factory(k, X, X["out"], t * 512, xr, ot, ctr))

        def prep(t):
            for c0 in range(0, 86, 16):
                n_ = min(16, 86 - c0)
                k.dma("sp", aTt[:, c0:c0 + n_, :], av[:, c0:c0 + n_, t * 512:(t + 1) * 512], [X["aT"]], [aTb], aTb)
        linear(k, ring, jobs, P, prep)
    k.barrier()


def build(dbg=False, stop=99, n_all_tiles=8, n_own_tiles=4, n_chunks=32, n_gdn_heads=16, n_own_blocks=16, n_key_blocks=32,
          phases=None, ext_in=(), ext_out=()):
    nc = bass.Bass("TRN2", target_bir_lowering=False)
    k = K(nc)
    shapes = {}

    class LazyX(dict):
        def __missing__(self, name):
            b = Buf(nc.dram_tensor(name, list(shapes[name]), F32, kind="ExternalInput").ap(), name)
            b.dram = True
            self[name] = b
            return b
    X = LazyX()

    def inp(name, shape):
        shapes[name] = shape
        if phases is None:
            X[name]
    inp("x_all", [S, D]); inp("x_own", [NOWN, D]); inp("mem", [256, D])
    inp("norm_mix", [128, 32]); inp("w_in", [D, DIN]); inp("consts", [128, 512]); inp("consts2", [16, 2048])
    inp("cw", [128, 48, 4]); inp("a_log", [16]); inp("dt_bias", [16])
    inp("gdn_norm", [128]); inp("att_q_norm", [128]); inp("att_k_norm", [128])
    inp("w_out", [D, D]); inp("norm_mem_q", [128, 32]); inp("norm_mem_kv", [128, 32])
    inp("w_mem_q", [D, 512]); inp("w_mem_kv", [D, 1024]); inp("mem_q_norm", [128]); inp("mem_k_norm", [128])
    inp("w_mem_o", [512, D]); inp("norm_ffn", [128, 32]); inp("w_gate", [D, FFN]); inp("w_up", [D, FFN]); inp("w_down", [FFN, D])
    inp("cs_all", [S, 32]); inp("cs_own", [NOWN, 32]); inp("pos_own", [NOWN, 1]); inp("kpos", [256]); inp("jsel", [128, 1])

    def scr(name, shape, dtype):
        kind = "ExternalInput" if name in ext_in else ("ExternalOutput" if name in ext_out else "Internal")
        X[name] = k.dram(name, shape, dtype, kind)
    scr("qkvT", [6144, S], F32)
    scr("gab", [S, 32], F32)
    scr("akv", [S, 1024], F32)
    scr("ikr", [S, 128], F32)
    scr("gz", [NOWN, 2048], F32)
    scr("aq", [NOWN, 2048], F32)
    scr("iq", [NOWN, 4096], F32)
    scr("iw", [NOWN, 32], F32)
    scr("oscr", [S, 2048], F32)
    scr("mixT", [D, NOWN], BF16)
    scr("aT", [FFN, NOWN], BF16)
    if "out" in ext_in:
        X["out_in"] = k.dram("out_in", [NOWN, D], F32, "ExternalInput")
    X["out"] = k.dram("out", [NOWN, D], F32, "ExternalOutput")
    with ExitStack() as es:
        P = [k.ps(es, f"P{i}") for i in range(8)]
        cst = k.sb(es, "cst", [128, 512], F32)
        k.dma("sp", cst[:, :], X["consts"][:, :], [X["consts"]], [cst], cst)
        C = dict(n_all_tiles=n_all_tiles, n_own_tiles=n_own_tiles, n_chunks=n_chunks, n_gdn_heads=n_gdn_heads,
                 n_own_blocks=n_own_blocks, n_key_blocks=n_key_blocks)
        C["ident"] = cst.t[:, 0:128]
        C["cst"] = cst
        allph = [phase1, phase2, phase3, phase4, phase5, phase6a, phase6b]
        for i, ph in enumerate(allph):
            if (phases is None and i < stop) or (phases is not None and (i + 1) in phases):
                ph(k, X, P, C)
                print("phase", i + 1, "instructions", k.nins, "waits", k.nwait, flush=True)
        k.finish([X["out"]])
    print("instructions", k.nins, "waits", k.nwait, "dma sems", len(k.dsems))
    return nc


def _consts():
    c = np.zeros((128, 512), np.float32)
    c[:, 0:128] = np.eye(128)
    p = np.arange(128)[:, None]; f = np.arange(128)[None, :]
    c[:, 128:256] = (p <= f)
    c[:, 256:384] = (p < f)
    c[:, 384:512] = 1.0
    c2 = np.zeros((16, 2048), np.float32)
    for h in range(16):
        c2[h, h * 128:(h + 1) * 128] = 1.0
    return c, c2


def _rot_table(pos):
    half = 16
    inv_freq = (np.float32(500000.0) ** (-np.arange(half, dtype=np.float32) * np.float32(2.0) / np.float32(32))).astype(np.float32)
    ang = pos.astype(np.float32)[:, None] * inv_freq[None, :]
    return np.concatenate([np.cos(ang), np.sin(ang)], axis=1).astype(np.float32)


def make_inputs(inp, core):
    b = core // 2
    j = core % 2
    f = lambda a: np.ascontiguousarray(a, dtype=np.float32)
    x = inp["x"][b]
    own_blocks = [2 * n + j for n in range(16)]
    own_rows = np.concatenate([np.arange(g * 128, (g + 1) * 128) for g in own_blocks])
    c, c2 = _consts()
    g32 = lambda v: f(v.reshape(32, 128).T)
    cwv = inp["conv_w"][0]
    cw = f(cwv.reshape(4, 3, 16, 128).transpose(3, 1, 2, 0).reshape(128, 48, 4))
    cs_all = _rot_table(np.arange(S))
    m = dict(
        x_all=f(x), x_own=f(x[own_rows]), mem=f(inp["mem"][b]),
        norm_mix=g32(inp["norm_mix"][0]), w_in=f(inp["w_in"][0]), consts=c, consts2=c2, cw=cw,
        a_log=f(inp["a_log"][0]), dt_bias=f(inp["dt_bias"][0]),
        gdn_norm=f(inp["gdn_norm"][0]), att_q_norm=f(inp["att_q_norm"][0]), att_k_norm=f(inp["att_k_norm"][0]),
        w_out=f(inp["w_out"][0]), norm_mem_q=g32(inp["norm_mem_q"][0]), norm_mem_kv=g32(inp["norm_mem_kv"][0]),
        w_mem_q=f(inp["w_mem_q"][0]), w_mem_kv=f(inp["w_mem_kv"][0]),
        mem_q_norm=f(inp["mem_q_norm"][0]), mem_k_norm=f(inp["mem_k_norm"][0]),
        w_mem_o=f(inp["w_mem_o"][0]), norm_ffn=g32(inp["norm_ffn"][0]),
        w_gate=f(inp["w_gate"][0]), w_up=f(inp["w_up"][0]), w_down=f(inp["w_down"][0]),
        cs_all=cs_all, cs_own=f(cs_all[own_rows]), pos_own=f(own_rows.astype(np.float32)[:, None]),
        kpos=f(np.arange(256)), jsel=np.full((128, 1), float(j), np.float32),
    )
    return m, own_rows


def kernel(**inputs):
    inp = {k_: np.asarray(v) for k_, v in inputs.items()}
    nc = build()
    maps = []
    rows = []
    for core in range(8):
        m, r = make_inputs(inp, core)
        maps.append(m)
        rows.append(r)
    res = run_bass_kernel_spmd(nc, maps, core_ids=list(range(8)))
    out = np.zeros((4, S, D), np.float32)
    for core in range(8):
        out[core // 2][rows[core]] = res.results[core]["out"]
    return out
```

```python
import numpy as np
import concourse.bass as bass
import concourse.mybir as mybir

F32 = mybir.dt.float32
BF16 = mybir.dt.bfloat16
AF = mybir.ActivationFunctionType
ALU = mybir.AluOpType
AX = mybir.AxisListType


class Buf:
    __slots__ = ("t", "w", "r", "dsem", "dcnt", "name", "dram", "psum")

    def __init__(self, t, name=""):
        self.t = t
        self.w = None
        self.r = {}
        self.dsem = None
        self.dcnt = 0
        self.name = name
        self.dram = False
        self.psum = False

    def __getitem__(self, key):
        return self.t[key]


class K:
    def __init__(self, nc):
        self.nc = nc
        self.engs = {"pe": nc.tensor, "act": nc.scalar, "dve": nc.vector,
                     "pool": nc.gpsimd, "sp": nc.sync}
        self.sem = {e: nc.alloc_semaphore(name="s_" + e) for e in self.engs}
        self.cnt = {e: 0 for e in self.engs}
        self.pending = {e: False for e in self.engs}
        self.waited = {e: {} for e in self.engs}
        self.dsems = []
        self.free_sems = []
        self.semcnt = {}
        self.nsem_alloc = 0
        self.nwait = 0
        self.nins = 0

    def sb(self, es, name, shape, dtype):
        self.uid = getattr(self, "uid", 0) + 1
        t = es.enter_context(self.nc.sbuf_tensor(f"sb{self.uid}_{name}", list(shape), dtype))
        return Buf(t, name)

    def ps(self, es, name, shape=(128, 512), dtype=F32):
        t = es.enter_context(self.nc.psum_tensor(name, list(shape), dtype))
        b = Buf(t, name)
        b.psum = True
        return b

    def dram(self, name, shape, dtype, kind="Internal"):
        t = self.nc.dram_tensor(name, list(shape), dtype, kind=kind)
        b = Buf(t.ap(), name)
        b.dram = True
        return b

    def _deps(self, eng, reads, writes, skip=None, strict=False):
        need = {}
        own = self.sem[eng]

        def add(p, raw):
            if p is None:
                return
            s, c = p
            if (s is own) and (not raw) and (not strict):
                return
            if need.get(s, 0) < c:
                need[s] = c
        for b in reads:
            add(b.w, True)
            if b.psum:
                for s, c in b.r.items():
                    add((s, c), False)
        for b in writes:
            add(b.w, False)
            for s, c in b.r.items():
                add((s, c), False)
        e = self.engs[eng]
        for s, c in need.items():
            if eng == "pe" and s is own:
                continue
            if skip is not None and s is skip:
                continue
            if self.waited[eng].get(s, 0) < c:
                e.wait_ge(s, c)
                self.nwait += 1
                self.waited[eng][s] = c

    def op(self, eng, fn, reads=(), writes=(), sig=True):
        self._deps(eng, reads, writes)
        ins = fn(self.engs[eng])
        self.nins += 1
        if sig:
            self.cnt[eng] += 1
            ins.then_inc(self.sem[eng], 1)
            p = (self.sem[eng], self.cnt[eng])
            self.pending[eng] = False
        else:
            p = (self.sem[eng], self.cnt[eng] + 1)
            self.pending[eng] = True
        s, c = p
        for b in reads:
            if b.r.get(s, 0) < c:
                b.r[s] = c
        for b in writes:
            b.w = p
            b.r = {}
        return ins

    def dma(self, q, out_ap, in_ap, reads, writes, semb, **kw):
        if getattr(semb, "dram", False):
            srcs = [b for b in reads if not getattr(b, "dram", False)]
            assert srcs
            semb = srcs[0]
        if semb.dsem is None:
            if self.free_sems:
                semb.dsem = self.free_sems.pop()
            else:
                self.nsem_alloc += 1
                semb.dsem = self.nc.alloc_semaphore(name=f"d{self.nsem_alloc}")
            semb.dcnt = self.semcnt.get(semb.dsem, 0)
            self.dsems.append(semb)
        self._deps(q, reads, writes, skip=semb.dsem, strict=True)
        ins = self.engs[q].dma_start(out=out_ap, in_=in_ap, **kw)
        self.nins += 1
        semb.dcnt += 16
        self.semcnt[semb.dsem] = semb.dcnt
        ins.then_inc(semb.dsem, 16)
        s, c = semb.dsem, semb.dcnt
        for b in reads:
            if b.r.get(s, 0) < c:
                b.r[s] = c
        for b in writes:
            b.w = (s, c)
            b.r = {}
        return ins

    def barrier(self):
        for e, eng in self.engs.items():
            for f in self.engs:
                if f == e:
                    continue
                c = self.cnt[f]
                if c > 0 and self.waited[e].get(self.sem[f], 0) < c:
                    eng.wait_ge(self.sem[f], c)
                    self.waited[e][self.sem[f]] = c
            for b in self.dsems:
                if b.dcnt > 0 and self.waited[e].get(b.dsem, 0) < b.dcnt:
                    eng.wait_ge(b.dsem, b.dcnt)
                    self.waited[e][b.dsem] = b.dcnt
        for b in self.dsems:
            self.free_sems.append(b.dsem)
            b.dsem = None
        self.dsems = []

    def finish(self, out_bufs):
        for e in self.engs:
            assert not self.pending[e], e
        self.barrier()

from contextlib import ExitStack
from concourse.bass_utils import run_bass_kernel_spmd

S = 4096; D = 4096; NOWN = 2048; DIN = 15552; FFN = 11008
TT = 512
C_GQ, C_GK, C_GV, C_GZ, C_GA, C_GB = 0, 2048, 4096, 6144, 8192, 8208
C_AQ, C_AK, C_AV, C_IQ, C_IK, C_IW = 8224, 10272, 10784, 11296, 15392, 15520


EPS = 1e-6
import os as _os
GDN_STAGE = int(_os.environ.get('GDN_STAGE', '99'))
CH_POOL = 'pool'


class WRing:
    def __init__(self, k, es, npieces=6, nst=2):
        self.k = k
        self.st = [k.sb(es, f"wst{i}", [128, 8, 512], F32) for i in range(nst)]
        self.pc = [k.sb(es, f"wpc{i}", [128, 8, 512], BF16) for i in range(npieces)]
        self.si = self.pi = self.ci = 0

    def load(self, wbuf, wv, kc0, nk, c0, ncols, castengs):
        k = self.k
        st = self.st[self.si % len(self.st)]; self.si += 1
        pc = self.pc[self.pi % len(self.pc)]; self.pi += 1
        k.dma("sp", st[:, 0:nk, 0:ncols], wv[:, kc0:kc0 + nk, c0:c0 + ncols], [wbuf], [st], st)
        eng = castengs[self.ci % len(castengs)]; self.ci += 1
        if eng == "act":
            k.op("act", lambda e: e.copy(out=pc[:, 0:nk, 0:ncols], in_=st[:, 0:nk, 0:ncols]), [st], [pc])
        else:
            k.op(eng, lambda e: e.tensor_copy(pc[:, 0:nk, 0:ncols], st[:, 0:nk, 0:ncols]), [st], [pc])
        return pc


def linear(k, ring, jobs, P, prep, castengs=("dve", "act"), LA=3):
    items = []
    for ji, jb in enumerate(jobs):
        for pi in range((jb["KC"] + 7) // 8):
            items.append((ji, pi))
    pcs = {}

    def issue(idx):
        ji, pi = items[idx]
        jb = jobs[ji]
        kc0 = pi * 8
        nk = min(8, jb["KC"] - kc0)
        pcs[idx] = ring.load(jb["wbuf"], jb["wv"], kc0, nk, jb["c0"], jb["ncols"], castengs)
    for idx in range(min(LA, len(items))):
        issue(idx)
    cur_pass = None
    bank = 0
    for idx, (ji, pi) in enumerate(items):
        jb = jobs[ji]
        if pi == 0:
            if jb["pass_id"] != cur_pass:
                cur_pass = jb["pass_id"]
                prep(cur_pass)
            nsub_ = jb.get("nsub", 4)
            nq_ = nsub_ if jb["mode"] == "N" else (jb["ncols"] // 128) * ((nsub_ * 128 + 511) // 512)
            if nq_ > 4:
                jb["_banks"] = P[0:8]
            else:
                jb["_banks"] = P[bank * 4:(bank + 1) * 4]
                bank = (bank + 1) % (len(P) // 4)
        if idx + LA < len(items):
            issue(idx + LA)
        pc = pcs.pop(idx)
        KC = jb["KC"]; ncols = jb["ncols"]; kc0 = pi * 8; nk = min(8, KC - kc0)
        hT, hTb = jb["hT"]
        banks = jb["_banks"]
        nsub = jb.get("nsub", 4)
        ntok = nsub * 128
        nhalf = (ntok + 511) // 512
        nq = nsub if jb["mode"] == "N" else (ncols // 128) * nhalf
        for q in range(nq):
            ps = banks[q]
            for cc in range(nk):
                c = kc0 + cc
                if jb["mode"] == "N":
                    k.op("pe", lambda e: e.matmul(ps[:, 0:ncols], lhsT=hT[:, c, q * 128:(q + 1) * 128],
                                                  rhs=pc[:, cc, 0:ncols], start=(c == 0), stop=(c == KC - 1)),
                         [hTb, pc], [ps], sig=(cc == nk - 1))
                else:
                    cs_ = q // nhalf; hf_ = q % nhalf
                    w_ = min(512, ntok - hf_ * 512)
                    k.op("pe", lambda e: e.matmul(ps[:, 0:w_], lhsT=pc[:, cc, cs_ * 128:(cs_ + 1) * 128],
                                                  rhs=hT[:, c, hf_ * 512:hf_ * 512 + w_], start=(c == 0), stop=(c == KC - 1)),
                         [hTb, pc], [ps], sig=(cc == nk - 1))
        if kc0 + nk == KC:
            for q in range(nq):
                jb["evac"](q, banks[q], ncols)


def build_hT(k, x_dram, row0, xt, hT, hTb, ssq, rstd, junk, gam, ident, cstb, pst, n_sub=4, col0=0):
    for sub in range(n_sub):
        xb = xt[sub % len(xt)]
        r0 = row0 + sub * 128
        k.dma("sp", xb[:, :], x_dram[r0:r0 + 128, col0:col0 + D], [x_dram], [xb], xb)
        k.op("dve", lambda e: e.memset(ssq[:, 0:1], 0.0), [], [ssq])
        k.op("act", lambda e: e.activation(out=junk[:, :], in_=xb[:, :], func=AF.Square,
                                           accum_out=ssq[:, 0:1]), [xb, ssq], [junk, ssq])
        k.op("act", lambda e: e.activation(out=rstd[:, 0:1], in_=ssq[:, 0:1], func=AF.Sqrt, scale=1.0 / D, bias=EPS),
             [ssq], [rstd])
        k.op("dve", lambda e: e.reciprocal(rstd[:, 0:1], rstd[:, 0:1]), [rstd], [rstd])
        k.op("dve", lambda e: e.tensor_scalar(xb[:, :], xb[:, :], rstd[:, 0:1], None, ALU.mult),
             [xb, rstd], [xb])
        for g in range(8):
            ps = pst[g % len(pst)]
            for j in range(4):
                c = g * 4 + j
                k.op("pe", lambda e: e.transpose(out=ps[:, j * 128:(j + 1) * 128],
                                                 in_=xb[:, c * 128:(c + 1) * 128], identity=ident),
                     [xb, cstb], [ps], sig=(j == 3))
            k.op("dve", lambda e: e.tensor_tensor(
                out=hT[:, g * 4:(g + 1) * 4, sub * 128:(sub + 1) * 128],
                in0=ps[:, :].rearrange("p (a b) -> p a b", a=4),
                in1=gam[:, g * 4:(g + 1) * 4].unsqueeze(2).to_broadcast([128, 4, 128]),
                op=ALU.mult), [ps, gam], [hTb])


def mk_jobs(jobs, pass_id, mode, wbuf, wv, KC, c0, n, hT, evac_factory, nsub=4):
    off = 0
    while off < n:
        nc_ = min(512, n - off)
        jobs.append(dict(pass_id=pass_id, mode=mode, wbuf=wbuf, wv=wv, KC=KC, c0=c0 + off, ncols=nc_,
                         hT=hT, evac=evac_factory(off), nsub=nsub))
        off += nc_


def phase1(k, X, P, C):
    with ExitStack() as es:
        xt = [k.sb(es, f"xt{i}", [128, D], F32) for i in range(2)]
        junk = k.sb(es, "junk", [128, D], BF16)
        hTt = es.enter_context(k.nc.sbuf_tensor("hT", [128, 32, 1024], BF16))
        hTb = Buf(hTt, "hT")
        ssq = k.sb(es, "ssq", [128, 1], F32)
        rstd = k.sb(es, "rstd", [128, 1], F32)
        gam = k.sb(es, "gam", [128, 32], F32)
        ot = [k.sb(es, f"ot{i}", [128, 512], F32) for i in range(4)]
        oi = [0]
        ring = WRing(k, es, npieces=4)
        k.dma("sp", gam[:, :], X["norm_mix"][:, :], [X["norm_mix"]], [gam], gam)
        wv = X["w_in"].t.rearrange("(c p) n -> p c n", p=128)
        jobs = []
        passes = {}

        def fac(mode, dst, row0, dcol0):
            def f(off):
                def evac(q, ps, ncols):
                    o = ot[oi[0] % 4]; oi[0] += 1
                    w = ncols if mode == "N" else 512
                    k.op("act", lambda e: e.copy(out=o[:, 0:w], in_=ps[:, 0:w]), [ps], [o])
                    if mode == "N":
                        r = row0 + q * 128
                        dc = dcol0 + off
                        k.dma("sp", dst[r:r + 128, dc:dc + ncols], o[:, 0:ncols], [o], [dst], dst)
                    else:
                        r = dcol0 + off + (q // 2) * 128
                        c_ = row0 + (q % 2) * 512
                        k.dma("sp", dst[r:r + 128, c_:c_ + 512], o[:, 0:512], [o], [dst], dst)
                return evac
            return f

        for t in range(C["n_all_tiles"] // 2):
            pid = ("all", t)
            passes[pid] = (X["x_all"], t * 1024)
            r0 = t * 1024
            mk_jobs(jobs, pid, "T", X["w_in"], wv, 32, C_GQ, 6144, (hTt, hTb), fac("T", X["qkvT"], r0, 0), nsub=8)
            mk_jobs(jobs, pid, "N", X["w_in"], wv, 32, C_GA, 32, (hTt, hTb), fac("N", X["gab"], r0, 0), nsub=8)
            mk_jobs(jobs, pid, "N", X["w_in"], wv, 32, C_AK, 1024, (hTt, hTb), fac("N", X["akv"], r0, 0), nsub=8)
            mk_jobs(jobs, pid, "N", X["w_in"], wv, 32, C_IK, 128, (hTt, hTb), fac("N", X["ikr"], r0, 0), nsub=8)
        for t in range(C["n_own_tiles"] // 2):
            pid = ("own", t)
            passes[pid] = (X["x_own"], t * 1024)
            r0 = t * 1024
            mk_jobs(jobs, pid, "N", X["w_in"], wv, 32, C_GZ, 2048, (hTt, hTb), fac("N", X["gz"], r0, 0), nsub=8)
            mk_jobs(jobs, pid, "N", X["w_in"], wv, 32, C_AQ, 2048, (hTt, hTb), fac("N", X["aq"], r0, 0), nsub=8)
            mk_jobs(jobs, pid, "N", X["w_in"], wv, 32, C_IQ, 4096, (hTt, hTb), fac("N", X["iq"], r0, 0), nsub=8)
            mk_jobs(jobs, pid, "N", X["w_in"], wv, 32, C_IW, 32, (hTt, hTb), fac("N", X["iw"], r0, 0), nsub=8)

        def prep(pid):
            xd, r0 = passes[pid]
            build_hT(k, xd, r0, xt, hTt, hTb, ssq, rstd, junk, gam, C["ident"], C["cst"], P[6:8], n_sub=8)

        linear(k, ring, jobs, P, prep)
    k.barrier()


def phase2(k, X, P, C):
    cst = C["cst"]
    ident = C["ident"]
    tri = cst.t[:, 128:256]
    tris = cst.t[:, 256:384]
    ones = cst.t[:, 384:512]
    NCH = C["n_chunks"]
    NH = C["n_gdn_heads"]
    with ExitStack() as es:
        cw = k.sb(es, "cw", [128, 48, 4], F32)
        oh = k.sb(es, "oh", [16, 2048], F32)
        alog = k.sb(es, "alog", [128, 16], F32)
        dtb = k.sb(es, "dtb", [128, 16], F32)
        nea = k.sb(es, "nea", [128, 16], F32)
        G = {n: k.sb(es, "g_" + n, [128, 32, 16], F32) for n in ("gcum", "eg_unused", "edec", "egl", "nbeta", "beta")}
        gcumT = k.sb(es, "gcumT", [16, 32, 128], F32)
        gtmp = [k.sb(es, f"gtmp{i}", [128, 32], F32) for i in range(4)]
        k.dma("sp", cw[:, :, :], X["cw"][:, :, :], [X["cw"]], [cw], cw)
        k.dma("sp", oh[:, :], X["consts2"][:, :], [X["consts2"]], [oh], oh)
        k.dma("sp", alog[:, :], X["a_log"].t.partition_broadcast(128), [X["a_log"]], [alog], alog)
        k.dma("sp", dtb[:, :], X["dt_bias"].t.partition_broadcast(128), [X["dt_bias"]], [dtb], dtb)
        k.op("act", lambda e: e.activation(out=nea[:, :], in_=alog[:, :], func=AF.Exp), [alog], [nea])
        k.op("dve", lambda e: e.tensor_scalar(nea[:, :], nea[:, :], -1.0, None, ALU.mult), [nea], [nea])

        for ch in range(NCH):
            t0 = ch * 128
            ga = gtmp[0]; t1 = gtmp[1]; t2 = gtmp[2]; g = gtmp[3]
            k.dma("sp", ga[:, :], X["gab"][t0:t0 + 128, :], [X["gab"]], [ga], ga)
            k.op("dve", lambda e: e.tensor_tensor(out=t1[:, 0:16], in0=ga[:, 0:16], in1=dtb[:, :], op=ALU.add),
                 [ga, dtb], [t1])
            k.op("act", lambda e: e.activation(out=t1[:, 0:16], in_=t1[:, 0:16], func=AF.Exp), [t1], [t1])
            k.op("act", lambda e: e.activation(out=t1[:, 0:16], in_=t1[:, 0:16], func=AF.Ln, bias=1.0), [t1], [t1])
            k.op("dve", lambda e: e.tensor_tensor(out=g[:, 0:16], in0=t1[:, 0:16], in1=nea[:, :], op=ALU.mult),
                 [t1, nea], [g])
            k.op("act", lambda e: e.activation(out=t2[:, 0:16], in_=ga[:, 16:32], func=AF.Exp, scale=-1.0), [ga], [t2])
            k.op("dve", lambda e: e.tensor_scalar(t2[:, 0:16], t2[:, 0:16], 1.0, None, ALU.add), [t2], [t2])
            k.op("dve", lambda e: e.reciprocal(G["beta"][:, ch, :], t2[:, 0:16]), [t2], [G["beta"]])
            k.op("dve", lambda e: e.tensor_scalar(G["nbeta"][:, ch, :], G["beta"][:, ch, :], -1.0, None, ALU.mult),
                 [G["beta"]], [G["nbeta"]])
            ps = P[ch % 2]
            k.op("pe", lambda e: e.matmul(ps[:, 0:16], lhsT=tri, rhs=g[:, 0:16], start=True, stop=True), [cst, g], [ps], sig=False)
            k.op("pe", lambda e: e.matmul(ps[:, 16:32], lhsT=ones, rhs=g[:, 0:16], start=True, stop=True), [cst, g], [ps])
            k.op("act", lambda e: e.copy(out=G["gcum"][:, ch, :], in_=ps[:, 0:16]), [ps], [G["gcum"]])
            k.op("act", lambda e: e.activation(out=G["egl"][:, ch, :], in_=ps[:, 16:32], func=AF.Exp), [ps], [G["egl"]])
            k.op("dve", lambda e: e.tensor_tensor(out=t2[:, 16:32], in0=ps[:, 16:32], in1=G["gcum"][:, ch, :], op=ALU.subtract),
                 [ps, G["gcum"]], [t2])
            k.op("act", lambda e: e.activation(out=G["edec"][:, ch, :], in_=t2[:, 16:32], func=AF.Exp), [t2], [G["edec"]])
            ps2 = P[2 + ch % 2]
            k.op("pe", lambda e: e.transpose(out=ps2[0:16, 0:128], in_=G["gcum"][:, ch, :], identity=ident),
                 [G["gcum"], cst], [ps2])
            k.op("act", lambda e: e.copy(out=gcumT[:, ch, :], in_=ps2[0:16, 0:128]), [ps2], [gcumT])


        def head_gen(h, B):
            raw = B['raw']
            y = B['y']
            sq = B['sq']
            rn = B['rn']
            St = B['St']
            W = B['W']
            OSB = B['OSB']
            NY = B['NY']
            Mb = B['Mb']

            def nps():
                p = B['banks'][B['pi'] % 2]
                B['pi'] += 1
                return p
            k.op('dve', lambda e: e.memset(St[:, :], 0.0), [], [St])
            yield
            for tl in range((NCH + 3) // 4):
                tok0 = tl * 512
                nch_t = min(4, NCH - tl * 4)
                ntk = nch_t * 128
                for gi in range(3):
                    row0 = gi * 2048 + h * 128
                    if tl == 0:
                        k.op('dve', lambda e: e.memset(raw[gi][:, 0:3], 0.0), [], [raw[gi]])
                        yield
                        k.dma('sp', raw[gi][:, 3:3 + ntk], X['qkvT'][row0:row0 + 128, 0:ntk], [X['qkvT']], [raw[gi]], raw[gi])
                        yield
                    else:
                        k.dma('sp', raw[gi][:, 0:3 + ntk], X['qkvT'][row0:row0 + 128, tok0 - 3:tok0 + ntk], [X['qkvT']], [raw[gi]], raw[gi])
                        yield
                    ce = 'dve'
                    wi = gi * 16 + h
                    k.op(ce, lambda e: e.tensor_scalar(y[gi][:, 0:ntk], raw[gi][:, 0:ntk], cw[:, wi, 0:1], None, ALU.mult), [raw[gi], cw], [y[gi]])
                    yield
                    for kk in range(1, 4):
                        k.op(ce, lambda e: e.scalar_tensor_tensor(out=y[gi][:, 0:ntk], in0=raw[gi][:, kk:kk + ntk], scalar=cw[:, wi, kk:kk + 1], in1=y[gi][:, 0:ntk], op0=ALU.mult, op1=ALU.add), [raw[gi], cw, y[gi]], [y[gi]])
                        yield
                    k.op('act', lambda e: e.activation(out=y[gi][:, 0:ntk], in_=y[gi][:, 0:ntk], func=AF.Silu), [y[gi]], [y[gi]])
                    yield
                for gi in range(2):
                    k.op('dve', lambda e: e.tensor_tensor(out=sq[:, 0:ntk], in0=y[gi][:, 0:ntk], in1=y[gi][:, 0:ntk], op=ALU.mult), [y[gi]], [sq])
                    yield
                    for hf in range((ntk + 511) // 512):
                        w_ = min(512, ntk - hf * 512)
                        ps = nps()
                        k.op('pe', lambda e: e.matmul(ps[:, 0:w_], lhsT=ones, rhs=sq[:, hf * 512:hf * 512 + w_], start=True, stop=True), [cst, sq], [ps])
                        yield
                        sc_ = 128.0 if gi == 0 else 1.0
                        k.op('act', lambda e: e.activation(out=rn[:, hf * 512:hf * 512 + w_], in_=ps[:, 0:w_], func=AF.Sqrt, scale=sc_, bias=sc_ * EPS), [ps], [rn])
                        yield
                    k.op('dve', lambda e: e.reciprocal(rn[:, 0:ntk], rn[:, 0:ntk]), [rn], [rn])
                    yield
                    k.op('dve', lambda e: e.tensor_tensor(out=y[gi][:, 0:ntk], in0=y[gi][:, 0:ntk], in1=rn[:, 0:ntk], op=ALU.mult), [y[gi], rn], [y[gi]])
                    yield
                for cl in range(nch_t):
                    ch = tl * 4 + cl
                    cs_ = slice(cl * 128, (cl + 1) * 128)
                    qn = y[0].t[:, cs_]
                    kn = y[1].t[:, cs_]
                    vv = y[2].t[:, cs_]
                    gc = G['gcum'].t[:, ch, h:h + 1]
                    psB = nps()
                    k.op('pe', lambda e: e.matmul(psB[:, 0:128], lhsT=oh[0:16, h * 128:(h + 1) * 128], rhs=gcumT[0:16, ch, :], start=True, stop=True), [oh, gcumT], [psB])
                    yield
                    k.op('dve', lambda e: e.tensor_scalar(W['Dm'][:, :], psB[:, 0:128], gc, 0.0, ALU.subtract, ALU.min), [psB, G['gcum']], [W['Dm']])
                    yield
                    k.op('act', lambda e: e.activation(out=W['DT'][:, :], in_=W['Dm'][:, :], func=AF.Exp), [W['Dm']], [W['DT']])
                    yield
                    k.op('act', lambda e: e.activation(out=W['Eg'][:, :], in_=psB[:, 0:128], func=AF.Exp), [psB], [W['Eg']])
                    yield
                    if GDN_STAGE < 1:
                        continue
                    psK = nps()
                    k.op('pe', lambda e: e.matmul(psK[:, 0:128], lhsT=kn, rhs=kn, start=True, stop=True), [y[1]], [psK], sig=False)
                    yield
                    k.op('pe', lambda e: e.matmul(psK[:, 128:256], lhsT=kn, rhs=qn, start=True, stop=True), [y[1], y[0]], [psK])
                    yield
                    k.op(CH_POOL, lambda e: e.tensor_tensor(out=W['DTs'][:, :], in0=W['DT'][:, :], in1=tris, op=ALU.mult), [W['DT'], cst], [W['DTs']])
                    yield
                    k.op(CH_POOL, lambda e: e.tensor_tensor(out=W['DTi'][:, :], in0=W['DT'][:, :], in1=tri, op=ALU.mult), [W['DT'], cst], [W['DTi']])
                    yield
                    if GDN_STAGE < 2:
                        continue
                    N0 = NY[0]
                    k.op('dve', lambda e: e.scalar_tensor_tensor(out=N0[:, 0:128], in0=psK[:, 0:128], scalar=G['nbeta'].t[:, ch, h:h + 1], in1=W['DTs'][:, :], op0=ALU.mult, op1=ALU.mult), [psK, G['nbeta'], W['DTs']], [N0])
                    yield
                    k.op('dve', lambda e: e.tensor_tensor(out=W['QKm'][:, :], in0=psK[:, 128:256], in1=W['DTi'][:, :], op=ALU.mult), [psK, W['DTi']], [W['QKm']])
                    yield
                    if GDN_STAGE < 3:
                        continue
                    psT = nps()
                    k.op('pe', lambda e: e.matmul(psT[:, 0:128], lhsT=N0[:, 0:128], rhs=ident, start=True, stop=True), [N0, cst], [psT])
                    yield
                    if not _os.environ.get('SKIP_M0COPY'):
                        k.op(_os.environ.get('M0ENG', 'act'), (lambda e: e.copy(out=Mb[0][:, :], in_=psT[:, 0:128])) if _os.environ.get('M0ENG', 'act') == 'act' else lambda e: e.tensor_copy(Mb[0][:, :], psT[:, 0:128]), [psT], [Mb[0]])
                        yield
                    if not _os.environ.get('GDN_SKIPY1'):
                        k.op(CH_POOL, lambda e: e.tensor_tensor(out=NY[1][:, 128:256], in0=N0[:, 0:128], in1=ident, op=ALU.add), [N0, cst], [NY[1]])
                        yield
                    if GDN_STAGE < 4:
                        continue
                    ps1 = nps()
                    LV0 = int(_os.environ.get('LV0', '9'))
                    k.op('pe', lambda e: e.matmul(ps1[:, 0:128], lhsT=Mb[0][:, :], rhs=N0[:, 0:128], start=True, stop=True), [Mb[0], N0], [ps1], sig=LV0 < 2)
                    yield
                    if LV0 >= 2:
                        k.op('pe', lambda e: e.matmul(ps1[:, 128:256], lhsT=N0[:, 0:128], rhs=Mb[0][:, :], start=True, stop=True), [Mb[0], N0], [ps1])
                        yield
                    if LV0 >= 3:
                        k.op('act', lambda e: e.copy(out=NY[1][:, 0:128], in_=ps1[:, 0:128]), [ps1], [NY[1]])
                        yield
                    if LV0 >= 4:
                        k.op('act', lambda e: e.copy(out=Mb[1][:, :], in_=ps1[:, 128:256]), [ps1], [Mb[1]])
                        yield
                    if GDN_STAGE < 5:
                        continue
                    cur = 1
                    for lv in range(1, 6):
                        nyc = NY[cur]
                        nyn = NY[1 - cur]
                        mc = Mb[cur]
                        mn = Mb[1 - cur]
                        ps2 = nps()
                        k.op('pe', lambda e: e.matmul(ps2[:, 0:256], lhsT=mc[:, :], rhs=nyc[:, 0:256], start=True, stop=True), [mc, nyc], [ps2], sig=False)
                        yield
                        k.op('pe', lambda e: e.matmul(ps2[:, 256:384], lhsT=nyc[:, 0:128], rhs=mc[:, :], start=True, stop=True), [mc, nyc], [ps2])
                        yield
                        k.op('act', lambda e: e.copy(out=nyn[:, 0:128], in_=ps2[:, 0:128]), [ps2], [nyn])
                        yield
                        k.op('dve', lambda e: e.tensor_tensor(out=nyn[:, 128:256], in0=ps2[:, 128:256], in1=nyc[:, 128:256], op=ALU.add), [ps2, nyc], [nyn])
                        yield
                        k.op('act', lambda e: e.copy(out=mn[:, :], in_=ps2[:, 256:384]), [ps2], [mn])
                        yield
                        cur = 1 - cur
                    if GDN_STAGE < 6:
                        continue
                    ps3 = nps()
                    k.op('pe', lambda e: e.matmul(ps3[:, 0:128], lhsT=Mb[cur][:, :], rhs=NY[cur][:, 128:256], start=True, stop=True), [Mb[cur], NY[cur]], [ps3])
                    yield
                    k.op('dve', lambda e: e.tensor_tensor(out=W['ZT'][:, :], in0=ps3[:, 0:128], in1=NY[cur][:, 128:256], op=ALU.add), [ps3, NY[cur]], [W['ZT']])
                    yield
                    if GDN_STAGE < 7:
                        continue
                    ps4 = nps()
                    k.op('pe', lambda e: e.transpose(out=ps4[:, 0:128], in_=kn, identity=ident), [y[1], cst], [ps4], sig=False)
                    yield
                    k.op('pe', lambda e: e.transpose(out=ps4[:, 128:256], in_=vv, identity=ident), [y[2], cst], [ps4])
                    yield
                    k.op('act', lambda e: e.activation(out=W['kdec'][:, :], in_=ps4[:, 0:128], func=AF.Copy, scale=G['edec'].t[:, ch, h:h + 1]), [ps4, G['edec']], [W['kdec']])
                    yield
                    k.op('act', lambda e: e.copy(out=W['vtok'][:, :], in_=ps4[:, 128:256]), [ps4], [W['vtok']])
                    yield
                    k.op(CH_POOL, lambda e: e.tensor_tensor(out=W['kegT'][:, :], in0=kn, in1=W['Eg'][:, :], op=ALU.mult), [y[1], W['Eg']], [W['kegT']])
                    yield
                    k.op(CH_POOL, lambda e: e.tensor_tensor(out=W['qdT'][:, :], in0=qn, in1=W['Eg'][:, :], op=ALU.mult), [y[0], W['Eg']], [W['qdT']])
                    yield
                    if GDN_STAGE < 8:
                        continue
                    ps5 = nps()
                    k.op('pe', lambda e: e.matmul(ps5[:, 0:128], lhsT=W['kegT'][:, :], rhs=St[:, :], start=True, stop=True), [W['kegT'], St], [ps5])
                    yield
                    k.op('dve', lambda e: e.scalar_tensor_tensor(out=W['r'][:, :], in0=ps5[:, 0:128], scalar=-1.0, in1=W['vtok'][:, :], op0=ALU.mult, op1=ALU.add), [W['vtok'], ps5], [W['r']])
                    yield
                    k.op('pe', lambda e: e.matmul(ps5[:, 128:256], lhsT=W['ZT'][:, :], rhs=W['r'][:, :], start=True, stop=True), [W['ZT'], W['r']], [ps5])
                    yield
                    k.op('act', lambda e: e.activation(out=W['vnew'][:, :], in_=ps5[:, 128:256], func=AF.Copy, scale=G['beta'].t[:, ch, h:h + 1]), [ps5, G['beta']], [W['vnew']])
                    yield
                    ps6 = nps()
                    k.op('pe', lambda e: e.matmul(ps6[:, 0:128], lhsT=W['qdT'][:, :], rhs=St[:, :], start=True, stop=False), [W['qdT'], St], [ps6], sig=False)
                    yield
                    k.op('pe', lambda e: e.matmul(ps6[:, 0:128], lhsT=W['QKm'][:, :], rhs=W['vnew'][:, :], start=False, stop=True), [W['QKm'], W['vnew']], [ps6], sig=False)
                    yield
                    k.op('pe', lambda e: e.matmul(ps6[:, 128:256], lhsT=W['kdec'][:, :], rhs=W['vnew'][:, :], start=True, stop=True), [W['kdec'], W['vnew']], [ps6])
                    yield
                    osb_ = OSB[ch % 2]
                    k.op('act', lambda e: e.copy(out=osb_[:, :], in_=ps6[:, 0:128]), [ps6], [osb_])
                    yield
                    k.dma(_os.environ.get('OQ', 'sp'), X['oscr'][ch * 128:(ch + 1) * 128, h * 128:(h + 1) * 128], osb_[:, :], [osb_], [X['oscr']], X['oscr'])
                    yield
                    k.op('dve', lambda e: e.tensor_scalar(St[:, :], St[:, :], G['egl'].t[:, ch, h:h + 1], None, ALU.mult), [St, G['egl']], [St])
                    yield
                    k.op('dve', lambda e: e.tensor_tensor(out=St[:, :], in0=ps6[:, 128:256], in1=St[:, :], op=ALU.add), [St, ps6], [St])
                    yield

        HG = C.get("gdn_interleave", 4)
        ctxs = []
        for ci in range(HG):
            Bc = dict(raw=[k.sb(es, f"raw{ci}_{i}", [128, 515], F32) for i in range(3)],
                      y=[k.sb(es, f"y{ci}_{i}", [128, 512], F32) for i in range(3)],
                      sq=k.sb(es, f"sq{ci}", [128, 512], F32), rn=k.sb(es, f"rn{ci}", [128, 512], F32),
                      St=k.sb(es, f"S{ci}", [128, 128], F32), W={},
                      OSB=[k.sb(es, f"osb{ci}_{i}", [128, 128], F32) for i in range(2)],
                      NY=[k.sb(es, f"NY{ci}_{i}", [128, 256], F32) for i in range(2)],
                      Mb=[k.sb(es, f"Mb{ci}_{i}", [128, 128], F32) for i in range(2)],
                      banks=[P[(2 * ci) % 8], P[(2 * ci + 1) % 8]], pi=0)
            for n in ("Dm", "DT", "Eg", "DTs", "DTi", "QKm", "M0", "kdec", "vtok", "kegT", "qdT", "r", "vnew", "ZT"):
                Bc["W"][n] = k.sb(es, f"w{ci}_" + n, [128, 128], F32)
            ctxs.append(Bc)
        for h0 in range(0, NH, HG):
            gens = []
            for ci, h in enumerate(range(h0, min(h0 + HG, NH))):
                ctxs[ci]["pi"] = 0
                gens.append(head_gen(h, ctxs[ci]))
            while gens:
                for g_ in list(gens):
                    try:
                        next(g_)
                    except StopIteration:
                        gens.remove(g_)
    k.barrier()


def rot_norm(k, src, dst, nh, ssq, rstd, tmp, gain, cs, scale, do_norm, eng="dve", rot=True):
    s3 = src.t[:, 0:nh * 128].rearrange("p (h d) -> p h d", h=nh)
    t3 = tmp.t[:, 0:nh * 128].rearrange("p (h d) -> p h d", h=nh)
    d3 = dst.t[:, 0:nh * 128].rearrange("p (h d) -> p h d", h=nh)
    if do_norm:
        k.op(eng, lambda e: e.tensor_tensor(out=t3, in0=s3, in1=s3, op=ALU.mult), [src], [tmp])
        k.op("dve", lambda e: e.tensor_reduce(out=ssq[:, 0:nh], in_=t3, axis=AX.X, op=ALU.add), [tmp], [ssq])
        k.op("act", lambda e: e.activation(out=rstd[:, 0:nh], in_=ssq[:, 0:nh], func=AF.Sqrt, scale=1.0 / 128, bias=EPS),
             [ssq], [rstd])
        k.op("dve", lambda e: e.reciprocal(rstd[:, 0:nh], rstd[:, 0:nh]), [rstd], [rstd])
        k.op(eng, lambda e: e.tensor_tensor(out=t3, in0=s3, in1=rstd.t[:, 0:nh].unsqueeze(2).to_broadcast([128, nh, 128]),
                                            op=ALU.mult), [src, rstd], [tmp])
        k.op(eng, lambda e: e.scalar_tensor_tensor(out=t3, in0=t3, scalar=scale,
                                                   in1=gain.t[:, 0:128].unsqueeze(1).to_broadcast([128, nh, 128]),
                                                   op0=ALU.mult, op1=ALU.mult), [tmp, gain], [tmp])
        base = tmp
        b3 = t3
    else:
        k.op(eng, lambda e: e.tensor_scalar(t3, s3, scale, None, ALU.mult), [src], [tmp])
        base = tmp
        b3 = t3
    if not rot:
        k.op("act", lambda e: e.copy(out=d3, in_=b3), [base], [dst])
        return
    cosb = cs.t[:, 0:16].unsqueeze(1).to_broadcast([128, nh, 16])
    sinb = cs.t[:, 16:32].unsqueeze(1).to_broadcast([128, nh, 16])
    k.op("act", lambda e: e.copy(out=d3[:, :, 32:128], in_=b3[:, :, 32:128]), [base], [dst])
    ra = src.t[:, 0:nh * 128].rearrange("p (h d) -> p h d", h=nh)
    k.op(eng, lambda e: e.tensor_tensor(out=ra[:, :, 32:48], in0=b3[:, :, 0:16], in1=cosb, op=ALU.mult), [base, cs], [src])
    k.op(eng, lambda e: e.tensor_tensor(out=ra[:, :, 48:64], in0=b3[:, :, 16:32], in1=sinb, op=ALU.mult), [base, cs], [src])
    k.op(eng, lambda e: e.tensor_tensor(out=ra[:, :, 64:80], in0=b3[:, :, 16:32], in1=cosb, op=ALU.mult), [base, cs], [src])
    k.op(eng, lambda e: e.tensor_tensor(out=ra[:, :, 80:96], in0=b3[:, :, 0:16], in1=sinb, op=ALU.mult), [base, cs], [src])
    k.op(eng, lambda e: e.tensor_tensor(out=d3[:, :, 0:16], in0=ra[:, :, 32:48], in1=ra[:, :, 48:64], op=ALU.subtract), [src], [dst])
    k.op(eng, lambda e: e.tensor_tensor(out=d3[:, :, 16:32], in0=ra[:, :, 64:80], in1=ra[:, :, 80:96], op=ALU.add), [src], [dst])


def phase3(k, X, P, C):
    cst = C["cst"]
    ident = C["ident"]
    NB = C["n_own_blocks"]
    NKB = C["n_key_blocks"]
    with ExitStack() as es:
        identb = k.sb(es, "identb", [128, 128], BF16)
        onesb = k.sb(es, "onesb", [128, 128], BF16)
        k.op("dve", lambda e: e.tensor_copy(identb[:, :], ident), [cst], [identb])
        k.op("dve", lambda e: e.tensor_copy(onesb[:, :], cst.t[:, 384:512]), [cst], [onesb])
        KT = k.sb(es, "KT", [128, 4, S], BF16)
        IKT = k.sb(es, "IKT", [128, S], BF16)
        V = k.sb(es, "V", [128, 32, 512], BF16)
        kpos = k.sb(es, "kpos", [128, 256], F32)
        posr = k.sb(es, "posr", [128, 1], F32)
        gq_n = k.sb(es, "gq_n", [128, 128], F32)
        gk_n = k.sb(es, "gk_n", [128, 128], F32)
        gd_n = k.sb(es, "gd_n", [128, 128], F32)
        jsel = k.sb(es, "jsel", [128, 1], F32)
        for b_, nm in ((gq_n, "att_q_norm"), (gk_n, "att_k_norm"), (gd_n, "gdn_norm")):
            k.dma("sp", b_[:, :], X[nm].t.partition_broadcast(128), [X[nm]], [b_], b_)
        k.dma("sp", kpos[:, :], X["kpos"].t.partition_broadcast(128), [X["kpos"]], [kpos], kpos)
        k.dma("sp", jsel[:, :], X["jsel"][:, :], [X["jsel"]], [jsel], jsel)
        med = [k.sb(es, f"med{i}", [128, 2048], F32) for i in range(3)]
        bfb = k.sb(es, "bfb", [128, 2048], BF16)
        cs = k.sb(es, "cs", [128, 32], F32)
        pos = k.sb(es, "pos", [128, 1], F32)
        iw = k.sb(es, "iw", [128, 32], F32)
        ssq = k.sb(es, "ssq3", [128, 16], F32)
        rstd = k.sb(es, "rstd3", [128, 16], F32)
        QT = k.sb(es, "QT", [128, 16, 128], BF16)
        IQT = k.sb(es, "IQT", [128, 32, 128], BF16)
        sc = k.sb(es, "sc", [128, S], F32)
        wk = k.sb(es, "wk", [128, S], F32)
        m8 = k.sb(es, "m8", [128, 8], F32)
        tau = k.sb(es, "tau", [128, 1], F32)
        mk = k.sb(es, "mk", [128, S], BF16)
        mkT = k.sb(es, "mkT", [128, 32, 128], BF16)
        rl = [k.sb(es, f"rl{i}", [128, 512], F32) for i in range(3)]
        pe_ = [k.sb(es, f"pe{i}", [128, 512], BF16) for i in range(3)]
        pm = [k.sb(es, f"pm{i}", [128, 512], BF16) for i in range(3)]
        rinv = k.sb(es, "rinv", [128, 512], F32)
        obT = [k.sb(es, f"obT{i}", [128, 512], BF16) for i in range(2)]
        mT = [k.sb(es, f"mT{i}", [128, 512], BF16) for i in range(2)]
        pi = [0]

        def nps():
            p = P[pi[0] % 8]; pi[0] += 1
            return p

        for sb_ in range(NKB):
            t0 = sb_ * 128
            a = med[0]; tm = med[1]
            k.dma("sp", a[:, 0:1024], X["akv"][t0:t0 + 128, :], [X["akv"]], [a], a)
            k.dma("sp", a[:, 1024:1152], X["ikr"][t0:t0 + 128, :], [X["ikr"]], [a], a)
            k.dma("sp", cs[:, :], X["cs_all"][t0:t0 + 128, :], [X["cs_all"]], [cs], cs)
            k.op("act", lambda e: e.copy(out=V[:, sb_, :], in_=a[:, 512:1024]), [a], [V])
            rot_norm(k, _view(a, 1024, 128), _view(bfb, 1024, 128), 1, ssq, rstd, _view(tm, 1024, 128), gk_n, cs, 1.0, False)
            rot_norm(k, _view(a, 0, 512), _view(bfb, 0, 512), 4, ssq, rstd, _view(tm, 0, 512), gk_n, cs, 1.0, True)
            ps = nps()
            psb = ps.t[:, :].bitcast(BF16)
            for g in range(4):
                k.op("pe", lambda e: e.transpose(out=psb[:, g * 128:(g + 1) * 128], in_=bfb[:, g * 128:(g + 1) * 128], identity=identb[:, :]),
                     [bfb, identb], [ps], sig=False)
            k.op("pe", lambda e: e.transpose(out=psb[:, 512:640], in_=bfb[:, 1024:1152], identity=identb[:, :]),
                 [bfb, identb], [ps])
            k.op("act", lambda e: e.copy(out=KT[:, :, t0:t0 + 128], in_=psb[:, 0:512].rearrange("p (g s) -> p g s", g=4)),
                 [ps], [KT])
            k.op("act", lambda e: e.copy(out=IKT[:, t0:t0 + 128], in_=psb[:, 512:640]), [ps], [IKT])

        for n in range(NB):
            r0 = n * 128
            NK = min(2 * n + 2, NKB)
            NKc = NK * 128
            aq = med[0]; tm = med[1]
            k.dma("sp", aq[:, :], X["aq"][r0:r0 + 128, :], [X["aq"]], [aq], aq)
            k.dma("sp", iw[:, :], X["iw"][r0:r0 + 128, :], [X["iw"]], [iw], iw)
            k.dma("sp", cs[:, :], X["cs_own"][r0:r0 + 128, :], [X["cs_own"]], [cs], cs)
            k.dma("sp", pos[:, :], X["pos_own"][r0:r0 + 128, :], [X["pos_own"]], [pos], pos)
            k.op("dve", lambda e: e.tensor_scalar(iw[:, :], iw[:, :], 32 ** -0.5, None, ALU.mult), [iw], [iw])
            rot_norm(k, aq, bfb, 16, ssq, rstd, tm, gq_n, cs, 128 ** -0.5, True)
            for g4 in range(4):
                ps = nps(); psb = ps.t[:, :].bitcast(BF16)
                for j in range(4):
                    hq = g4 * 4 + j
                    k.op("pe", lambda e: e.transpose(out=psb[:, j * 128:(j + 1) * 128], in_=bfb[:, hq * 128:(hq + 1) * 128],
                                                     identity=identb[:, :]), [bfb, identb], [ps], sig=(j == 3))
                k.op("act", lambda e: e.copy(out=QT[:, g4 * 4:(g4 + 1) * 4, :], in_=psb[:, 0:512].rearrange("p (g s) -> p g s", g=4)),
                     [ps], [QT])
            for half in range(2):
                iqr = med[0]; itm = med[1]
                k.dma("sp", iqr[:, :], X["iq"][r0:r0 + 128, half * 2048:(half + 1) * 2048], [X["iq"]], [iqr], iqr)
                rot_norm(k, iqr, bfb, 16, ssq, rstd, itm, gq_n, cs, 128 ** -0.5, False, eng="dve")
                for g4 in range(4):
                    ps = nps(); psb = ps.t[:, :].bitcast(BF16)
                    for j in range(4):
                        hq = g4 * 4 + j
                        k.op("pe", lambda e: e.transpose(out=psb[:, j * 128:(j + 1) * 128], in_=bfb[:, hq * 128:(hq + 1) * 128],
                                                         identity=identb[:, :]), [bfb, identb], [ps], sig=(j == 3))
                    k.op("act", lambda e: e.copy(out=IQT[:, half * 16 + g4 * 4:half * 16 + (g4 + 1) * 4, :],
                                                 in_=psb[:, 0:512].rearrange("p (g s) -> p g s", g=4)), [ps], [IQT])
            nkt = (NKc + 511) // 512
            scv = [Buf(sc.t[:, kt * 512:kt * 512 + min(512, NKc - kt * 512)], f"scv{kt}") for kt in range(nkt)]
            ri = 0
            for kt in range(nkt):
                w_ = min(512, NKc - kt * 512)
                acc_eng = "dve"
                for hi in range(32):
                    ps = nps()
                    k.op("pe", lambda e: e.matmul(ps[:, 0:w_], lhsT=IQT[:, hi, :], rhs=IKT[:, kt * 512:kt * 512 + w_],
                                                  start=True, stop=True), [IQT, IKT], [ps])
                    r_ = rl[ri % 3]; ri += 1
                    k.op("act", lambda e: e.activation(out=r_[:, 0:w_], in_=ps[:, 0:w_], func=AF.Relu), [ps], [r_])
                    if hi == 0:
                        k.op(acc_eng, lambda e: e.tensor_scalar(sc[:, kt * 512:kt * 512 + w_], r_[:, 0:w_], iw[:, 0:1], None, ALU.mult),
                             [r_, iw], [scv[kt], sc])
                    else:
                        k.op(acc_eng, lambda e: e.scalar_tensor_tensor(out=sc[:, kt * 512:kt * 512 + w_], in0=r_[:, 0:w_],
                                                                       scalar=iw[:, hi:hi + 1], in1=sc[:, kt * 512:kt * 512 + w_],
                                                                       op0=ALU.mult, op1=ALU.add), [r_, iw, scv[kt]], [scv[kt]])
            c0 = NKc - 256
            k.op("dve", lambda e: e.tensor_scalar(posr[:, :], pos[:, :], float(-c0), None, ALU.add), [pos], [posr])
            k.op("dve", lambda e: e.tensor_scalar(wk[:, c0:NKc], kpos[:, 0:256], posr[:, 0:1], -1e30, ALU.is_gt, ALU.mult),
                 [kpos, posr], [wk])
            k.op("dve", lambda e: e.tensor_tensor(out=sc[:, c0:NKc], in0=sc[:, c0:NKc], in1=wk[:, c0:NKc], op=ALU.add),
                 [wk] + scv, [sc] + scv)
            if NKc > 256:
                src = sc
                for rnd in range(32):
                    k.op("dve", lambda e: e.max(out=m8[:, :], in_=src[:, 0:NKc]), [src], [m8])
                    if rnd < 31:
                        k.op("dve", lambda e: e.match_replace(out=wk[:, 0:NKc], in_to_replace=m8[:, :], in_values=src[:, 0:NKc],
                                                              imm_value=-3e38), [src, m8], [wk])
                        src = wk
                k.op("dve", lambda e: e.tensor_scalar(tau[:, :], m8[:, 7:8], -1e29, None, ALU.max), [m8], [tau])
            else:
                k.op("dve", lambda e: e.memset(tau[:, :], -1e29), [], [tau])
            k.op("dve", lambda e: e.tensor_scalar(mk[:, 0:NKc], sc[:, 0:NKc], tau[:, 0:1], None, ALU.is_ge), [sc, tau], [mk])
            for kb4 in range((NK + 3) // 4):
                ps = nps(); psb = ps.t[:, :].bitcast(BF16)
                nb_ = min(4, NK - kb4 * 4)
                for j in range(nb_):
                    kb = kb4 * 4 + j
                    k.op("pe", lambda e: e.transpose(out=psb[:, j * 128:(j + 1) * 128], in_=mk[:, kb * 128:(kb + 1) * 128],
                                                     identity=identb[:, :]), [mk, identb], [ps], sig=(j == nb_ - 1))
                k.op("act", lambda e: e.copy(out=mkT[:, kb4 * 4:kb4 * 4 + nb_, :],
                                             in_=psb[:, 0:nb_ * 128].rearrange("p (g s) -> p g s", g=nb_)), [ps], [mkT])
            for g in range(4):
                psO = P[0]; psR = P[1]
                xi = 0
                for kb in range(NK):
                    psS = P[2 + (pi[0] % 6)]; pi[0] += 1
                    k.op("pe", lambda e: e.matmul(psS[:, 0:512], lhsT=KT[:, g, kb * 128:(kb + 1) * 128],
                                                  rhs=QT[:, g * 4:(g + 1) * 4, :].rearrange("p g s -> p (g s)"), start=True, stop=True),
                         [KT, QT], [psS])
                    pe1 = pe_[xi % 3]; pm1 = pm[xi % 3]; xi += 1
                    k.op("act", lambda e: e.activation(out=pe1[:, :], in_=psS[:, 0:512], func=AF.Exp), [psS], [pe1])
                    me = "dve"
                    k.op(me, lambda e: e.tensor_tensor(out=pm1[:, :].rearrange("p (g s) -> p g s", g=4),
                                                       in0=pe1[:, :].rearrange("p (g s) -> p g s", g=4),
                                                       in1=mkT.t[:, kb, :].unsqueeze(1).to_broadcast([128, 4, 128]), op=ALU.mult),
                         [pe1, mkT], [pm1])
                    k.op("pe", lambda e: e.matmul(psO[:, 0:512], lhsT=V[:, kb, g * 128:(g + 1) * 128], rhs=pm1[:, :],
                                                  start=(kb == 0), stop=(kb == NK - 1)), [V, pm1], [psO], sig=False)
                    k.op("pe", lambda e: e.matmul(psR[:, 0:512], lhsT=onesb[:, :], rhs=pm1[:, :],
                                                  start=(kb == 0), stop=(kb == NK - 1)), [onesb, pm1], [psR])
                k.op("act", lambda e: e.copy(out=rinv[:, :], in_=psR[:, 0:512]), [psR], [rinv])
                k.op("dve", lambda e: e.reciprocal(rinv[:, :], rinv[:, :]), [rinv], [rinv])
                ob = obT[g % 2]
                k.op("dve", lambda e: e.tensor_tensor(out=ob[:, :], in0=psO[:, 0:512], in1=rinv[:, :], op=ALU.mult), [psO, rinv], [ob])
                for hh in range(4):
                    rr = 2048 + (g * 4 + hh) * 128
                    k.dma("sp", X["mixT"][rr:rr + 128, r0:r0 + 128], ob[:, hh * 128:(hh + 1) * 128], [ob], [X["mixT"]], X["mixT"])
            A = med[0]; Bt = med[1]; gzt = med[2]
            k.dma("sp", A[:, :], X["oscr"][(2 * n) * 128:(2 * n + 1) * 128, :], [X["oscr"]], [A], A)
            k.dma("sp", Bt[:, :], X["oscr"][(2 * n + 1) * 128:(2 * n + 2) * 128, :], [X["oscr"]], [Bt], Bt)
            k.dma("sp", gzt[:, 0:2048], X["gz"][r0:r0 + 128, :], [X["gz"]], [gzt], gzt)
            k.op("dve", lambda e: e.tensor_tensor(out=Bt[:, :], in0=Bt[:, :], in1=A[:, :], op=ALU.subtract), [A, Bt], [Bt])
            k.op("dve", lambda e: e.scalar_tensor_tensor(out=A[:, :], in0=Bt[:, :], scalar=jsel[:, 0:1], in1=A[:, :],
                                                         op0=ALU.mult, op1=ALU.add), [A, Bt, jsel], [A])
            A3 = A.t[:, :].rearrange("p (h d) -> p h d", h=16)
            B3 = Bt.t[:, :].rearrange("p (h d) -> p h d", h=16)
            k.op("dve", lambda e: e.tensor_tensor(out=B3, in0=A3, in1=A3, op=ALU.mult), [A], [Bt])
            k.op("dve", lambda e: e.tensor_reduce(out=ssq[:, 0:16], in_=B3, axis=AX.X, op=ALU.add), [Bt], [ssq])
            k.op("act", lambda e: e.activation(out=rstd[:, 0:16], in_=ssq[:, 0:16], func=AF.Sqrt, scale=1.0 / 128, bias=EPS), [ssq], [rstd])
            k.op("dve", lambda e: e.reciprocal(rstd[:, 0:16], rstd[:, 0:16]), [rstd], [rstd])
            k.op("dve", lambda e: e.tensor_tensor(out=A3, in0=A3, in1=rstd.t[:, 0:16].unsqueeze(2).to_broadcast([128, 16, 128]), op=ALU.mult),
                 [A, rstd], [A])
            k.op("dve", lambda e: e.tensor_tensor(out=A3, in0=A3, in1=gd_n.t[:, :].unsqueeze(1).to_broadcast([128, 16, 128]), op=ALU.mult),
                 [A, gd_n], [A])
            k.op("act", lambda e: e.activation(out=gzt[:, 0:2048], in_=gzt[:, 0:2048], func=AF.Silu), [gzt], [gzt])
            k.op("dve", lambda e: e.tensor_tensor(out=A[:, :], in0=A[:, :], in1=gzt[:, 0:2048], op=ALU.mult), [A, gzt], [A])
            for g4 in range(4):
                ps = nps()
                for j in range(4):
                    hh = g4 * 4 + j
                    k.op("pe", lambda e: e.transpose(out=ps[:, j * 128:(j + 1) * 128], in_=A[:, hh * 128:(hh + 1) * 128], identity=ident),
                         [A, cst], [ps], sig=(j == 3))
                m_ = mT[g4 % 2]
                k.op("act", lambda e: e.copy(out=m_[:, :], in_=ps[:, 0:512]), [ps], [m_])
                for j in range(4):
                    rr = (g4 * 4 + j) * 128
                    k.dma("sp", X["mixT"][rr:rr + 128, r0:r0 + 128], m_[:, j * 128:(j + 1) * 128], [m_], [X["mixT"]], X["mixT"])
    k.barrier()


class _View:
    def __init__(self, parent, c0, n):
        self.p = parent
        self.t = parent.t[:, c0:c0 + n]
        self.name = parent.name + "_v"
        self.psum = False
        self.dram = False

    def __getitem__(self, key):
        return self.t[key]
    w = property(lambda self: self.p.w, lambda self, v: setattr(self.p, "w", v))
    r = property(lambda self: self.p.r, lambda self, v: setattr(self.p, "r", v))
    dsem = property(lambda self: self.p.dsem, lambda self, v: setattr(self.p, "dsem", v))
    dcnt = property(lambda self: self.p.dcnt, lambda self, v: setattr(self.p, "dcnt", v))


def _view(parent, c0, n):
    return _View(parent, c0, n)


def resid_evac_factory(k, X, src, row0, xr, ot, ctr):
    def f(off):
        def evac(q, ps, ncols):
            i = ctr[0] % len(xr); ctr[0] += 1
            xb = xr[i]; o = ot[i]
            r = row0 + q * 128
            k.dma("sp", xb[:, 0:ncols], src[r:r + 128, off:off + ncols], [src], [xb], xb)
            k.op("dve", lambda e: e.tensor_tensor(out=o[:, 0:ncols], in0=ps[:, 0:ncols], in1=xb[:, 0:ncols], op=ALU.add),
                 [ps, xb], [o])
            k.dma("sp", X["out"][r:r + 128, off:off + ncols], o[:, 0:ncols], [o], [X["out"]], X["out"])
        return evac
    return f


def phase4(k, X, P, C):
    with ExitStack() as es:
        hTt = es.enter_context(k.nc.sbuf_tensor("hT4", [128, 32, 1024], BF16))
        hTb = Buf(hTt, "hT4")
        xr = [k.sb(es, f"xr{i}", [128, 512], F32) for i in range(4)]
        ot = [k.sb(es, f"ot4{i}", [128, 512], F32) for i in range(4)]
        ring = WRing(k, es)
        wv = X["w_out"].t.rearrange("(c p) n -> p c n", p=128)
        mv = X["mixT"].t.rearrange("(c p) t -> p c t", p=128)
        jobs = []
        ctr = [0]
        for t in range(C["n_own_tiles"] // 2):
            mk_jobs(jobs, t, "N", X["w_out"], wv, 32, 0, D, (hTt, hTb),
                    resid_evac_factory(k, X, X["x_own"], t * 1024, xr, ot, ctr), nsub=8)

        def prep(t):
            for c4 in range(4):
                k.dma("sp", hTt[:, c4 * 8:(c4 + 1) * 8, :], mv[:, c4 * 8:(c4 + 1) * 8, t * 1024:(t + 1) * 1024],
                      [X["mixT"]], [hTb], hTb)
        linear(k, ring, jobs, P, prep)
    k.barrier()


def phase5(k, X, P, C):
    cst = C["cst"]; ident = C["ident"]
    with ExitStack() as es:
        xt = [k.sb(es, f"xt5{i}", [128, D], F32) for i in range(2)]
        junk = k.sb(es, "junk5", [128, D], BF16)
        hTt = es.enter_context(k.nc.sbuf_tensor("hT5", [128, 32, 512], BF16))
        hTb = Buf(hTt, "hT5")
        ssq = k.sb(es, "ssq5", [128, 16], F32)
        rstd = k.sb(es, "rstd5", [128, 16], F32)
        gamq = k.sb(es, "gamq", [128, 32], F32)
        gamkv = k.sb(es, "gamkv", [128, 32], F32)
        qn_g = k.sb(es, "qn_g", [128, 128], F32)
        kn_g = k.sb(es, "kn_g", [128, 128], F32)
        identb = k.sb(es, "identb5", [128, 128], BF16)
        onesb = k.sb(es, "onesb5", [128, 128], BF16)
        k.op("dve", lambda e: e.tensor_copy(identb[:, :], ident), [cst], [identb])
        k.op("dve", lambda e: e.tensor_copy(onesb[:, :], cst.t[:, 384:512]), [cst], [onesb])
        k.dma("sp", gamq[:, :], X["norm_mem_q"][:, :], [X["norm_mem_q"]], [gamq], gamq)
        k.dma("sp", gamkv[:, :], X["norm_mem_kv"][:, :], [X["norm_mem_kv"]], [gamkv], gamkv)
        k.dma("sp", qn_g[:, :], X["mem_q_norm"].t.partition_broadcast(128), [X["mem_q_norm"]], [qn_g], qn_g)
        k.dma("sp", kn_g[:, :], X["mem_k_norm"].t.partition_broadcast(128), [X["mem_k_norm"]], [kn_g], kn_g)
        MKT = k.sb(es, "MKT", [128, 4, 256], BF16)
        MV = k.sb(es, "MV", [128, 2, 512], BF16)
        MQT = k.sb(es, "MQT", [128, 4, 512], BF16)
        moTt = es.enter_context(k.nc.sbuf_tensor("moT", [128, 4, 512], BF16))
        moTb = Buf(moTt, "moT")
        o32 = [k.sb(es, f"o32_{i}", [128, 512], F32) for i in range(2)]
        t32 = k.sb(es, "t32", [128, 512], F32)
        obf = k.sb(es, "obf5", [128, 512], BF16)
        pe_ = [k.sb(es, f"pe5{i}", [128, 512], BF16) for i in range(2)]
        rinv = k.sb(es, "rinv5", [128, 512], F32)
        xr = [k.sb(es, f"xr5{i}", [128, 512], F32) for i in range(4)]
        ot = [k.sb(es, f"ot5{i}", [128, 512], F32) for i in range(4)]
        ring = WRing(k, es, npieces=4)
        ctr = [0]
        oc = [0]
        wkv = X["w_mem_kv"].t.rearrange("(c p) n -> p c n", p=128)

        def kv_fac(off):
            def evac(q, ps, ncols):
                o = o32[oc[0] % 2]; oc[0] += 1
                k.op("act", lambda e: e.copy(out=o[:, :], in_=ps[:, 0:512]), [ps], [o])
                if off == 0:
                    rot_norm(k, o, obf, 4, ssq, rstd, t32, kn_g, None, 1.0, True, rot=False)
                    p2 = P[6 + q % 2]; pb = p2.t[:, :].bitcast(BF16)
                    for j in range(4):
                        k.op("pe", lambda e: e.transpose(out=pb[:, j * 128:(j + 1) * 128], in_=obf[:, j * 128:(j + 1) * 128],
                                                         identity=identb[:, :]), [obf, identb], [p2], sig=(j == 3))
                    k.op("act", lambda e: e.copy(out=MKT[:, :, q * 128:(q + 1) * 128],
                                                 in_=pb[:, 0:512].rearrange("p (g s) -> p g s", g=4)), [p2], [MKT])
                else:
                    k.op("dve", lambda e: e.tensor_copy(MV[:, q, :], o[:, :]), [o], [MV])
            return evac
        jobs = []
        mk_jobs(jobs, "kv", "N", X["w_mem_kv"], wkv, 32, 0, 1024, (hTt, hTb), kv_fac, nsub=2)
        linear(k, ring, jobs, P[0:4], lambda pid: build_hT(k, X["mem"], 0, xt, hTt, hTb, ssq, rstd, junk, gamkv, ident, cst, P[6:8], n_sub=2))
        wq = X["w_mem_q"].t.rearrange("(c p) n -> p c n", p=128)
        wo = X["w_mem_o"].t.rearrange("(c p) n -> p c n", p=128)
        for t in range(C["n_own_tiles"]):
            def q_fac(off):
                def evac(q, ps, ncols):
                    o = o32[oc[0] % 2]; oc[0] += 1
                    k.op("act", lambda e: e.copy(out=o[:, :], in_=ps[:, 0:512]), [ps], [o])
                    rot_norm(k, o, obf, 4, ssq, rstd, t32, qn_g, None, 128 ** -0.5, True, rot=False)
                    p2 = P[6 + q % 2]; pb = p2.t[:, :].bitcast(BF16)
                    for j in range(4):
                        k.op("pe", lambda e: e.transpose(out=pb[:, j * 128:(j + 1) * 128], in_=obf[:, j * 128:(j + 1) * 128],
                                                         identity=identb[:, :]), [obf, identb], [p2], sig=(j == 3))
                    k.op("act", lambda e: e.copy(out=MQT[:, :, q * 128:(q + 1) * 128],
                                                 in_=pb[:, 0:512].rearrange("p (g s) -> p g s", g=4)), [p2], [MQT])
                return evac
            jobs = []
            mk_jobs(jobs, t, "N", X["w_mem_q"], wq, 32, 0, 512, (hTt, hTb), q_fac)
            linear(k, ring, jobs, P[0:4], lambda pid: build_hT(k, X["out"], pid * 512, xt, hTt, hTb, ssq, rstd, junk, gamq, ident, cst, P[6:8]))
            for hd in range(4):
                psO = P[0]; psR = P[1]
                for mb in range(2):
                    psS = P[2 + mb]
                    k.op("pe", lambda e: e.matmul(psS[:, 0:512], lhsT=MKT[:, hd, mb * 128:(mb + 1) * 128], rhs=MQT[:, hd, :],
                                                  start=True, stop=True), [MKT, MQT], [psS])
                    pe1 = pe_[mb]
                    k.op("act", lambda e: e.activation(out=pe1[:, :], in_=psS[:, 0:512], func=AF.Exp), [psS], [pe1])
                    k.op("pe", lambda e: e.matmul(psO[:, 0:512], lhsT=MV[:, mb, hd * 128:(hd + 1) * 128], rhs=pe1[:, :],
                                                  start=(mb == 0), stop=(mb == 1)), [MV, pe1], [psO], sig=False)
                    k.op("pe", lambda e: e.matmul(psR[:, 0:512], lhsT=onesb[:, :], rhs=pe1[:, :],
                                                  start=(mb == 0), stop=(mb == 1)), [onesb, pe1], [psR])
                k.op("act", lambda e: e.copy(out=rinv[:, :], in_=psR[:, 0:512]), [psR], [rinv])
                k.op("dve", lambda e: e.reciprocal(rinv[:, :], rinv[:, :]), [rinv], [rinv])
                k.op("dve", lambda e: e.tensor_tensor(out=moTt[:, hd, :], in0=psO[:, 0:512], in1=rinv[:, :], op=ALU.mult),
                     [psO, rinv], [moTb])
            jobs = []
            mk_jobs(jobs, t, "N", X["w_mem_o"], wo, 4, 0, D, (moTt, moTb),
                    resid_evac_factory(k, X, X["out"], t * 512, xr, ot, ctr))
            linear(k, ring, jobs, P[0:4], lambda pid: None)
    k.barrier()


def phase6a(k, X, P, C):
    cst = C["cst"]; ident = C["ident"]
    with ExitStack() as es:
        xt = [k.sb(es, f"xt6{i}", [128, D], F32) for i in range(2)]
        junk = k.sb(es, "junk6", [128, D], BF16)
        hTt = es.enter_context(k.nc.sbuf_tensor("hT6", [128, 32, 1024], BF16))
        hTb = Buf(hTt, "hT6")
        ssq = k.sb(es, "ssq6", [128, 1], F32)
        rstd = k.sb(es, "rstd6", [128, 1], F32)
        gam = k.sb(es, "gam6", [128, 32], F32)
        k.dma("sp", gam[:, :], X["norm_ffn"][:, :], [X["norm_ffn"]], [gam], gam)
        sg = [k.sb(es, f"sg{i}", [128, 512], BF16) for i in range(8)]
        ab = [k.sb(es, f"ab{i}", [128, 512], BF16) for i in range(4)]
        ring = WRing(k, es, npieces=4)
        wg = X["w_gate"].t.rearrange("(c p) n -> p c n", p=128)
        wu = X["w_up"].t.rearrange("(c p) n -> p c n", p=128)
        jobs = []
        st = dict(g=0, a=0, sgmap={})
        for t in range(C["n_own_tiles"] // 2):
            off = 0
            while off < FFN:
                n_ = min(512, FFN - off)

                def gfac(o_, t=t):
                    def evac(q, ps, ncols):
                        s_ = sg[st["g"] % 8]; st["g"] += 1
                        st["sgmap"][(t, o_, q)] = s_
                        k.op("act", lambda e: e.activation(out=s_[:, :], in_=ps[:, 0:512], func=AF.Silu), [ps], [s_])
                    return evac

                def ufac(o_, t=t, off=off):
                    def evac(q, ps, ncols):
                        s_ = st["sgmap"].pop((t, o_, q))
                        a_ = ab[st["a"] % 4]; st["a"] += 1
                        k.op("dve", lambda e: e.tensor_tensor(out=a_[:, :], in0=ps[:, 0:512], in1=s_[:, :], op=ALU.mult),
                             [ps, s_], [a_])
                        r = off + (q // 2) * 128
                        c_ = t * 1024 + (q % 2) * 512
                        k.dma("sp", X["aT"][r:r + 128, c_:c_ + 512], a_[:, :], [a_], [X["aT"]], X["aT"])
                    return evac
                jobs.append(dict(pass_id=t, mode="T", wbuf=X["w_gate"], wv=wg, KC=32, c0=off, ncols=n_, hT=(hTt, hTb),
                                 evac=gfac(0), nsub=8))
                jobs.append(dict(pass_id=t, mode="T", wbuf=X["w_up"], wv=wu, KC=32, c0=off, ncols=n_, hT=(hTt, hTb),
                                 evac=ufac(0), nsub=8))
                off += n_
        linear(k, ring, jobs, P, lambda pid: build_hT(k, X["out"], pid * 1024, xt, hTt, hTb, ssq, rstd, junk, gam, ident, cst, P[6:8], n_sub=8))
    k.barrier()


def phase6b(k, X, P, C):
    with ExitStack() as es:
        aTt = es.enter_context(k.nc.sbuf_tensor("aT6", [128, 86, 512], BF16))
        aTb = Buf(aTt, "aT6")
        xr = [k.sb(es, f"xr6{i}", [128, 512], F32) for i in range(4)]
        ot = [k.sb(es, f"ot6{i}", [128, 512], F32) for i in range(4)]
        ring = WRing(k, es, npieces=4)
        wd = X["w_down"].t.rearrange("(c p) n -> p c n", p=128)
        av = X["aT"].t.rearrange("(c p) t -> p c t", p=128)
        jobs = []
        ctr = [0]
        for t in range(C["n_own_tiles"]):
            mk_jobs(jobs, t, "N", X["w_down"], wd, 86, 0, D, (aTt, aTb),
                    resid_evac_factory(k, X, X["out"], t * 512, xr, ot, ctr))

        def prep(t):
            for c0 in range(0, 86, 16):
                n_ = min(16, 86 - c0)
                k.dma("sp", aTt[:, c0:c0 + n_, :], av[:, c0:c0 + n_, t * 512:(t + 1) * 512], [X["aT"]], [aTb], aTb)
        linear(k, ring, jobs, P, prep)
    k.barrier()


def build(dbg=False, stop=99, n_all_tiles=8, n_own_tiles=4, n_chunks=32, n_gdn_heads=16, n_own_blocks=16, n_key_blocks=32,
          phases=None, ext_in=(), ext_out=()):
    nc = bass.Bass("TRN2", target_bir_lowering=False)
    k = K(nc)
    shapes = {}

    class LazyX(dict):
        def __missing__(self, name):
            b = Buf(nc.dram_tensor(name, list(shapes[name]), F32, kind="ExternalInput").ap(), name)
            b.dram = True
            self[name] = b
            return b
    X = LazyX()

    def inp(name, shape):
        shapes[name] = shape
        if phases is None:
            X[name]
    inp("x_all", [S, D]); inp("x_own", [NOWN, D]); inp("mem", [256, D])
    inp("norm_mix", [128, 32]); inp("w_in", [D, DIN]); inp("consts", [128, 512]); inp("consts2", [16, 2048])
    inp("cw", [128, 48, 4]); inp("a_log", [16]); inp("dt_bias", [16])
    inp("gdn_norm", [128]); inp("att_q_norm", [128]); inp("att_k_norm", [128])
    inp("w_out", [D, D]); inp("norm_mem_q", [128, 32]); inp("norm_mem_kv", [128, 32])
    inp("w_mem_q", [D, 512]); inp("w_mem_kv", [D, 1024]); inp("mem_q_norm", [128]); inp("mem_k_norm", [128])
    inp("w_mem_o", [512, D]); inp("norm_ffn", [128, 32]); inp("w_gate", [D, FFN]); inp("w_up", [D, FFN]); inp("w_down", [FFN, D])
    inp("cs_all", [S, 32]); inp("cs_own", [NOWN, 32]); inp("pos_own", [NOWN, 1]); inp("kpos", [256]); inp("jsel", [128, 1])

    def scr(name, shape, dtype):
        kind = "ExternalInput" if name in ext_in else ("ExternalOutput" if name in ext_out else "Internal")
        X[name] = k.dram(name, shape, dtype, kind)
    scr("qkvT", [6144, S], F32)
    scr("gab", [S, 32], F32)
    scr("akv", [S, 1024], F32)
    scr("ikr", [S, 128], F32)
    scr("gz", [NOWN, 2048], F32)
    scr("aq", [NOWN, 2048], F32)
    scr("iq", [NOWN, 4096], F32)
    scr("iw", [NOWN, 32], F32)
    scr("oscr", [S, 2048], F32)
    scr("mixT", [D, NOWN], BF16)
    scr("aT", [FFN, NOWN], BF16)
    if "out" in ext_in:
        X["out_in"] = k.dram("out_in", [NOWN, D], F32, "ExternalInput")
    X["out"] = k.dram("out", [NOWN, D], F32, "ExternalOutput")
    with ExitStack() as es:
        P = [k.ps(es, f"P{i}") for i in range(8)]
        cst = k.sb(es, "cst", [128, 512], F32)
        k.dma("sp", cst[:, :], X["consts"][:, :], [X["consts"]], [cst], cst)
        C = dict(n_all_tiles=n_all_tiles, n_own_tiles=n_own_tiles, n_chunks=n_chunks, n_gdn_heads=n_gdn_heads,
                 n_own_blocks=n_own_blocks, n_key_blocks=n_key_blocks)
        C["ident"] = cst.t[:, 0:128]
        C["cst"] = cst
        allph = [phase1, phase2, phase3, phase4, phase5, phase6a, phase6b]
        for i, ph in enumerate(allph):
            if (phases is None and i < stop) or (phases is not None and (i + 1) in phases):
                ph(k, X, P, C)
                print("phase", i + 1, "instructions", k.nins, "waits", k.nwait, flush=True)
        k.finish([X["out"]])
    print("instructions", k.nins, "waits", k.nwait, "dma sems", len(k.dsems))
    return nc


def _consts():
    c = np.zeros((128, 512), np.float32)
    c[:, 0:128] = np.eye(128)
    p = np.arange(128)[:, None]; f = np.arange(128)[None, :]
    c[:, 128:256] = (p <= f)
    c[:, 256:384] = (p < f)
    c[:, 384:512] = 1.0
    c2 = np.zeros((16, 2048), np.float32)
    for h in range(16):
        c2[h, h * 128:(h + 1) * 128] = 1.0
    return c, c2


def _rot_table(pos):
    half = 16
    inv_freq = (np.float32(500000.0) ** (-np.arange(half, dtype=np.float32) * np.float32(2.0) / np.float32(32))).astype(np.float32)
    ang = pos.astype(np.float32)[:, None] * inv_freq[None, :]
    return np.concatenate([np.cos(ang), np.sin(ang)], axis=1).astype(np.float32)


def make_inputs(inp, core):
    b = core // 2
    j = core % 2
    f = lambda a: np.ascontiguousarray(a, dtype=np.float32)
    x = inp["x"][b]
    own_blocks = [2 * n + j for n in range(16)]
    own_rows = np.concatenate([np.arange(g * 128, (g + 1) * 128) for g in own_blocks])
    c, c2 = _consts()
    g32 = lambda v: f(v.reshape(32, 128).T)
    cwv = inp["conv_w"][0]
    cw = f(cwv.reshape(4, 3, 16, 128).transpose(3, 1, 2, 0).reshape(128, 48, 4))
    cs_all = _rot_table(np.arange(S))
    m = dict(
        x_all=f(x), x_own=f(x[own_rows]), mem=f(inp["mem"][b]),
        norm_mix=g32(inp["norm_mix"][0]), w_in=f(inp["w_in"][0]), consts=c, consts2=c2, cw=cw,
        a_log=f(inp["a_log"][0]), dt_bias=f(inp["dt_bias"][0]),
        gdn_norm=f(inp["gdn_norm"][0]), att_q_norm=f(inp["att_q_norm"][0]), att_k_norm=f(inp["att_k_norm"][0]),
        w_out=f(inp["w_out"][0]), norm_mem_q=g32(inp["norm_mem_q"][0]), norm_mem_kv=g32(inp["norm_mem_kv"][0]),
        w_mem_q=f(inp["w_mem_q"][0]), w_mem_kv=f(inp["w_mem_kv"][0]),
        mem_q_norm=f(inp["mem_q_norm"][0]), mem_k_norm=f(inp["mem_k_norm"][0]),
        w_mem_o=f(inp["w_mem_o"][0]), norm_ffn=g32(inp["norm_ffn"][0]),
        w_gate=f(inp["w_gate"][0]), w_up=f(inp["w_up"][0]), w_down=f(inp["w_down"][0]),
        cs_all=cs_all, cs_own=f(cs_all[own_rows]), pos_own=f(own_rows.astype(np.float32)[:, None]),
        kpos=f(np.arange(256)), jsel=np.full((128, 1), float(j), np.float32),
    )
    return m, own_rows


def kernel(**inputs):
    inp = {k_: np.asarray(v) for k_, v in inputs.items()}
    nc = build()
    maps = []
    rows = []
    for core in range(8):
        m, r = make_inputs(inp, core)
        maps.append(m)
        rows.append(r)
    res = run_bass_kernel_spmd(nc, maps, core_ids=list(range(8)))
    out = np.zeros((4, S, D), np.float32)
    for core in range(8):
        out[core // 2][rows[core]] = res.results[core]["out"]
    return out
```

```python
import numpy as np
import concourse.bass as bass
import concourse.mybir as mybir

F32 = mybir.dt.float32
BF16 = mybir.dt.bfloat16
AF = mybir.ActivationFunctionType
ALU = mybir.AluOpType
AX = mybir.AxisListType


class Buf:
    __slots__ = ("t", "w", "r", "dsem", "dcnt", "name", "dram", "psum")

    def __init__(self, t, name=""):
        self.t = t
        self.w = None
        self.r = {}
        self.dsem = None
        self.dcnt = 0
        self.name = name
        self.dram = False
        self.psum = False

    def __getitem__(self, key):
        return self.t[key]


class K:
    def __init__(self, nc):
        self.nc = nc
        self.engs = {"pe": nc.tensor, "act": nc.scalar, "dve": nc.vector,
                     "pool": nc.gpsimd, "sp": nc.sync}
        self.sem = {e: nc.alloc_semaphore(name="s_" + e) for e in self.engs}
        self.cnt = {e: 0 for e in self.engs}
        self.pending = {e: False for e in self.engs}
        self.waited = {e: {} for e in self.engs}
        self.dsems = []
        self.free_sems = []
        self.semcnt = {}
        self.nsem_alloc = 0
        self.nwait = 0
        self.nins = 0

    def sb(self, es, name, shape, dtype):
        self.uid = getattr(self, "uid", 0) + 1
        t = es.enter_context(self.nc.sbuf_tensor(f"sb{self.uid}_{name}", list(shape), dtype))
        return Buf(t, name)

    def ps(self, es, name, shape=(128, 512), dtype=F32):
        t = es.enter_context(self.nc.psum_tensor(name, list(shape), dtype))
        b = Buf(t, name)
        b.psum = True
        return b

    def dram(self, name, shape, dtype, kind="Internal"):
        t = self.nc.dram_tensor(name, list(shape), dtype, kind=kind)
        b = Buf(t.ap(), name)
        b.dram = True
        return b

    def _deps(self, eng, reads, writes, skip=None, strict=False):
        need = {}
        own = self.sem[eng]

        def add(p, raw):
            if p is None:
                return
            s, c = p
            if (s is own) and (not raw) and (not strict):
                return
            if need.get(s, 0) < c:
                need[s] = c
        for b in reads:
            add(b.w, True)
            if b.psum:
                for s, c in b.r.items():
                    add((s, c), False)
        for b in writes:
            add(b.w, False)
            for s, c in b.r.items():
                add((s, c), False)
        e = self.engs[eng]
        for s, c in need.items():
            if eng == "pe" and s is own:
                continue
            if skip is not None and s is skip:
                continue
            if self.waited[eng].get(s, 0) < c:
                e.wait_ge(s, c)
                self.nwait += 1
                self.waited[eng][s] = c

    def op(self, eng, fn, reads=(), writes=(), sig=True):
        self._deps(eng, reads, writes)
        ins = fn(self.engs[eng])
        self.nins += 1
        if sig:
            self.cnt[eng] += 1
            ins.then_inc(self.sem[eng], 1)
            p = (self.sem[eng], self.cnt[eng])
            self.pending[eng] = False
        else:
            p = (self.sem[eng], self.cnt[eng] + 1)
            self.pending[eng] = True
        s, c = p
        for b in reads:
            if b.r.get(s, 0) < c:
                b.r[s] = c
        for b in writes:
            b.w = p
            b.r = {}
        return ins

    def dma(self, q, out_ap, in_ap, reads, writes, semb, **kw):
        if getattr(semb, "dram", False):
            srcs = [b for b in reads if not getattr(b, "dram", False)]
            assert srcs
            semb = srcs[0]
        if semb.dsem is None:
            if self.free_sems:
                semb.dsem = self.free_sems.pop()
            else:
                self.nsem_alloc += 1
                semb.dsem = self.nc.alloc_semaphore(name=f"d{self.nsem_alloc}")
            semb.dcnt = self.semcnt.get(semb.dsem, 0)
            self.dsems.append(semb)
        self._deps(q, reads, writes, skip=semb.dsem, strict=True)
        ins = self.engs[q].dma_start(out=out_ap, in_=in_ap, **kw)
        self.nins += 1
        semb.dcnt += 16
        self.semcnt[semb.dsem] = semb.dcnt
        ins.then_inc(semb.dsem, 16)
        s, c = semb.dsem, semb.dcnt
        for b in reads:
            if b.r.get(s, 0) < c:
                b.r[s] = c
        for b in writes:
            b.w = (s, c)
            b.r = {}
        return ins

    def barrier(self):
        for e, eng in self.engs.items():
            for f in self.engs:
                if f == e:
                    continue
                c = self.cnt[f]
                if c > 0 and self.waited[e].get(self.sem[f], 0) < c:
                    eng.wait_ge(self.sem[f], c)
                    self.waited[e][self.sem[f]] = c
            for b in self.dsems:
                if b.dcnt > 0 and self.waited[e].get(b.dsem, 0) < b.dcnt:
                    eng.wait_ge(b.dsem, b.dcnt)
                    self.waited[e][b.dsem] = b.dcnt
        for b in self.dsems:
            self.free_sems.append(b.dsem)
            b.dsem = None
        self.dsems = []

    def finish(self, out_bufs):
        for e in self.engs:
            assert not self.pending[e], e
        self.barrier()

from contextlib import ExitStack
from concourse.bass_utils import run_bass_kernel_spmd

S = 4096; D = 4096; NOWN = 2048; DIN = 15552; FFN = 11008
TT = 512
C_GQ, C_GK, C_GV, C_GZ, C_GA, C_GB = 0, 2048, 4096, 6144, 8192, 8208
C_AQ, C_AK, C_AV, C_IQ, C_IK, C_IW = 8224, 10272, 10784, 11296, 15392, 15520


EPS = 1e-6
import os as _os
GDN_STAGE = int(_os.environ.get('GDN_STAGE', '99'))
CH_POOL = 'pool'


class WRing:
    def __init__(self, k, es, npieces=6, nst=2):
        self.k = k
        self.st = [k.sb(es, f"wst{i}", [128, 8, 512], F32) for i in range(nst)]
        self.pc = [k.sb(es, f"wpc{i}", [128, 8, 512], BF16) for i in range(npieces)]
        self.si = self.pi = self.ci = 0

    def load(self, wbuf, wv, kc0, nk, c0, ncols, castengs):
        k = self.k
        st = self.st[self.si % len(self.st)]; self.si += 1
        pc = self.pc[self.pi % len(self.pc)]; self.pi += 1
        k.dma("sp", st[:, 0:nk, 0:ncols], wv[:, kc0:kc0 + nk, c0:c0 + ncols], [wbuf], [st], st)
        eng = castengs[self.ci % len(castengs)]; self.ci += 1
        if eng == "act":
            k.op("act", lambda e: e.copy(out=pc[:, 0:nk, 0:ncols], in_=st[:, 0:nk, 0:ncols]), [st], [pc])
        else:
            k.op(eng, lambda e: e.tensor_copy(pc[:, 0:nk, 0:ncols], st[:, 0:nk, 0:ncols]), [st], [pc])
        return pc


def linear(k, ring, jobs, P, prep, castengs=("dve", "act"), LA=3):
    items = []
    for ji, jb in enumerate(jobs):
        for pi in range((jb["KC"] + 7) // 8):
            items.append((ji, pi))
    pcs = {}

    def issue(idx):
        ji, pi = items[idx]
        jb = jobs[ji]
        kc0 = pi * 8
        nk = min(8, jb["KC"] - kc0)
        pcs[idx] = ring.load(jb["wbuf"], jb["wv"], kc0, nk, jb["c0"], jb["ncols"], castengs)
    for idx in range(min(LA, len(items))):
        issue(idx)
    cur_pass = None
    bank = 0
    for idx, (ji, pi) in enumerate(items):
        jb = jobs[ji]
        if pi == 0:
            if jb["pass_id"] != cur_pass:
                cur_pass = jb["pass_id"]
                prep(cur_pass)
            nsub_ = jb.get("nsub", 4)
            nq_ = nsub_ if jb["mode"] == "N" else (jb["ncols"] // 128) * ((nsub_ * 128 + 511) // 512)
            if nq_ > 4:
                jb["_banks"] = P[0:8]
            else:
                jb["_banks"] = P[bank * 4:(bank + 1) * 4]
                bank = (bank + 1) % (len(P) // 4)
        if idx + LA < len(items):
            issue(idx + LA)
        pc = pcs.pop(idx)
        KC = jb["KC"]; ncols = jb["ncols"]; kc0 = pi * 8; nk = min(8, KC - kc0)
        hT, hTb = jb["hT"]
        banks = jb["_banks"]
        nsub = jb.get("nsub", 4)
        ntok = nsub * 128
        nhalf = (ntok + 511) // 512
        nq = nsub if jb["mode"] == "N" else (ncols // 128) * nhalf
        for q in range(nq):
            ps = banks[q]
            for cc in range(nk):
                c = kc0 + cc
                if jb["mode"] == "N":
                    k.op("pe", lambda e: e.matmul(ps[:, 0:ncols], lhsT=hT[:, c, q * 128:(q + 1) * 128],
                                                  rhs=pc[:, cc, 0:ncols], start=(c == 0), stop=(c == KC - 1)),
                         [hTb, pc], [ps], sig=(cc == nk - 1))
                else:
                    cs_ = q // nhalf; hf_ = q % nhalf
                    w_ = min(512, ntok - hf_ * 512)
                    k.op("pe", lambda e: e.matmul(ps[:, 0:w_], lhsT=pc[:, cc, cs_ * 128:(cs_ + 1) * 128],
                                                  rhs=hT[:, c, hf_ * 512:hf_ * 512 + w_], start=(c == 0), stop=(c == KC - 1)),
                         [hTb, pc], [ps], sig=(cc == nk - 1))
        if kc0 + nk == KC:
            for q in range(nq):
                jb["evac"](q, banks[q], ncols)


def build_hT(k, x_dram, row0, xt, hT, hTb, ssq, rstd, junk, gam, ident, cstb, pst, n_sub=4, col0=0):
    for sub in range(n_sub):
        xb = xt[sub % len(xt)]
        r0 = row0 + sub * 128
        k.dma("sp", xb[:, :], x_dram[r0:r0 + 128, col0:col0 + D], [x_dram], [xb], xb)
        k.op("dve", lambda e: e.memset(ssq[:, 0:1], 0.0), [], [ssq])
        k.op("act", lambda e: e.activation(out=junk[:, :], in_=xb[:, :], func=AF.Square,
                                           accum_out=ssq[:, 0:1]), [xb, ssq], [junk, ssq])
        k.op("act", lambda e: e.activation(out=rstd[:, 0:1], in_=ssq[:, 0:1], func=AF.Sqrt, scale=1.0 / D, bias=EPS),
             [ssq], [rstd])
        k.op("dve", lambda e: e.reciprocal(rstd[:, 0:1], rstd[:, 0:1]), [rstd], [rstd])
        k.op("dve", lambda e: e.tensor_scalar(xb[:, :], xb[:, :], rstd[:, 0:1], None, ALU.mult),
             [xb, rstd], [xb])
        for g in range(8):
            ps = pst[g % len(pst)]
            for j in range(4):
                c = g * 4 + j
                k.op("pe", lambda e: e.transpose(out=ps[:, j * 128:(j + 1) * 128],
                                                 in_=xb[:, c * 128:(c + 1) * 128], identity=ident),
                     [xb, cstb], [ps], sig=(j == 3))
            k.op("dve", lambda e: e.tensor_tensor(
                out=hT[:, g * 4:(g + 1) * 4, sub * 128:(sub + 1) * 128],
                in0=ps[:, :].rearrange("p (a b) -> p a b", a=4),
                in1=gam[:, g * 4:(g + 1) * 4].unsqueeze(2).to_broadcast([128, 4, 128]),
                op=ALU.mult), [ps, gam], [hTb])


def mk_jobs(jobs, pass_id, mode, wbuf, wv, KC, c0, n, hT, evac_factory, nsub=4):
    off = 0
    while off < n:
        nc_ = min(512, n - off)
        jobs.append(dict(pass_id=pass_id, mode=mode, wbuf=wbuf, wv=wv, KC=KC, c0=c0 + off, ncols=nc_,
                         hT=hT, evac=evac_factory(off), nsub=nsub))
        off += nc_


def phase1(k, X, P, C):
    with ExitStack() as es:
        xt = [k.sb(es, f"xt{i}", [128, D], F32) for i in range(2)]
        junk = k.sb(es, "junk", [128, D], BF16)
        hTt = es.enter_context(k.nc.sbuf_tensor("hT", [128, 32, 1024], BF16))
        hTb = Buf(hTt, "hT")
        ssq = k.sb(es, "ssq", [128, 1], F32)
        rstd = k.sb(es, "rstd", [128, 1], F32)
        gam = k.sb(es, "gam", [128, 32], F32)
        ot = [k.sb(es, f"ot{i}", [128, 512], F32) for i in range(4)]
        oi = [0]
        ring = WRing(k, es, npieces=4)
        k.dma("sp", gam[:, :], X["norm_mix"][:, :], [X["norm_mix"]], [gam], gam)
        wv = X["w_in"].t.rearrange("(c p) n -> p c n", p=128)
        jobs = []
        passes = {}

        def fac(mode, dst, row0, dcol0):
            def f(off):
                def evac(q, ps, ncols):
                    o = ot[oi[0] % 4]; oi[0] += 1
                    w = ncols if mode == "N" else 512
                    k.op("act", lambda e: e.copy(out=o[:, 0:w], in_=ps[:, 0:w]), [ps], [o])
                    if mode == "N":
                        r = row0 + q * 128
                        dc = dcol0 + off
                        k.dma("sp", dst[r:r + 128, dc:dc + ncols], o[:, 0:ncols], [o], [dst], dst)
                    else:
                        r = dcol0 + off + (q // 2) * 128
                        c_ = row0 + (q % 2) * 512
                        k.dma("sp", dst[r:r + 128, c_:c_ + 512], o[:, 0:512], [o], [dst], dst)
                return evac
            return f

        for t in range(C["n_all_tiles"] // 2):
            pid = ("all", t)
            passes[pid] = (X["x_all"], t * 1024)
            r0 = t * 1024
            mk_jobs(jobs, pid, "T", X["w_in"], wv, 32, C_GQ, 6144, (hTt, hTb), fac("T", X["qkvT"], r0, 0), nsub=8)
            mk_jobs(jobs, pid, "N", X["w_in"], wv, 32, C_GA, 32, (hTt, hTb), fac("N", X["gab"], r0, 0), nsub=8)
            mk_jobs(jobs, pid, "N", X["w_in"], wv, 32, C_AK, 1024, (hTt, hTb), fac("N", X["akv"], r0, 0), nsub=8)
            mk_jobs(jobs, pid, "N", X["w_in"], wv, 32, C_IK, 128, (hTt, hTb), fac("N", X["ikr"], r0, 0), nsub=8)
        for t in range(C["n_own_tiles"] // 2):
            pid = ("own", t)
            passes[pid] = (X["x_own"], t * 1024)
            r0 = t * 1024
            mk_jobs(jobs, pid, "N", X["w_in"], wv, 32, C_GZ, 2048, (hTt, hTb), fac("N", X["gz"], r0, 0), nsub=8)
            mk_jobs(jobs, pid, "N", X["w_in"], wv, 32, C_AQ, 2048, (hTt, hTb), fac("N", X["aq"], r0, 0), nsub=8)
            mk_jobs(jobs, pid, "N", X["w_in"], wv, 32, C_IQ, 4096, (hTt, hTb), fac("N", X["iq"], r0, 0), nsub=8)
            mk_jobs(jobs, pid, "N", X["w_in"], wv, 32, C_IW, 32, (hTt, hTb), fac("N", X["iw"], r0, 0), nsub=8)

        def prep(pid):
            xd, r0 = passes[pid]
            build_hT(k, xd, r0, xt, hTt, hTb, ssq, rstd, junk, gam, C["ident"], C["cst"], P[6:8], n_sub=8)

        linear(k, ring, jobs, P, prep)
    k.barrier()


def phase2(k, X, P, C):
    cst = C["cst"]
    ident = C["ident"]
    tri = cst.t[:, 128:256]
    tris = cst.t[:, 256:384]
    ones = cst.t[:, 384:512]
    NCH = C["n_chunks"]
    NH = C["n_gdn_heads"]
    with ExitStack() as es:
        cw = k.sb(es, "cw", [128, 48, 4], F32)
        oh = k.sb(es, "oh", [16, 2048], F32)
        alog = k.sb(es, "alog", [128, 16], F32)
        dtb = k.sb(es, "dtb", [128, 16], F32)
        nea = k.sb(es, "nea", [128, 16], F32)
        G = {n: k.sb(es, "g_" + n, [128, 32, 16], F32) for n in ("gcum", "eg_unused", "edec", "egl", "nbeta", "beta")}
        gcumT = k.sb(es, "gcumT", [16, 32, 128], F32)
        gtmp = [k.sb(es, f"gtmp{i}", [128, 32], F32) for i in range(4)]
        k.dma("sp", cw[:, :, :], X["cw"][:, :, :], [X["cw"]], [cw], cw)
        k.dma("sp", oh[:, :], X["consts2"][:, :], [X["consts2"]], [oh], oh)
        k.dma("sp", alog[:, :], X["a_log"].t.partition_broadcast(128), [X["a_log"]], [alog], alog)
        k.dma("sp", dtb[:, :], X["dt_bias"].t.partition_broadcast(128), [X["dt_bias"]], [dtb], dtb)
        k.op("act", lambda e: e.activation(out=nea[:, :], in_=alog[:, :], func=AF.Exp), [alog], [nea])
        k.op("dve", lambda e: e.tensor_scalar(nea[:, :], nea[:, :], -1.0, None, ALU.mult), [nea], [nea])

        for ch in range(NCH):
            t0 = ch * 128
            ga = gtmp[0]; t1 = gtmp[1]; t2 = gtmp[2]; g = gtmp[3]
            k.dma("sp", ga[:, :], X["gab"][t0:t0 + 128, :], [X["gab"]], [ga], ga)
            k.op("dve", lambda e: e.tensor_tensor(out=t1[:, 0:16], in0=ga[:, 0:16], in1=dtb[:, :], op=ALU.add),
                 [ga, dtb], [t1])
            k.op("act", lambda e: e.activation(out=t1[:, 0:16], in_=t1[:, 0:16], func=AF.Exp), [t1], [t1])
            k.op("act", lambda e: e.activation(out=t1[:, 0:16], in_=t1[:, 0:16], func=AF.Ln, bias=1.0), [t1], [t1])
            k.op("dve", lambda e: e.tensor_tensor(out=g[:, 0:16], in0=t1[:, 0:16], in1=nea[:, :], op=ALU.mult),
                 [t1, nea], [g])
            k.op("act", lambda e: e.activation(out=t2[:, 0:16], in_=ga[:, 16:32], func=AF.Exp, scale=-1.0), [ga], [t2])
            k.op("dve", lambda e: e.tensor_scalar(t2[:, 0:16], t2[:, 0:16], 1.0, None, ALU.add), [t2], [t2])
            k.op("dve", lambda e: e.reciprocal(G["beta"][:, ch, :], t2[:, 0:16]), [t2], [G["beta"]])
            k.op("dve", lambda e: e.tensor_scalar(G["nbeta"][:, ch, :], G["beta"][:, ch, :], -1.0, None, ALU.mult),
                 [G["beta"]], [G["nbeta"]])
            ps = P[ch % 2]
            k.op("pe", lambda e: e.matmul(ps[:, 0:16], lhsT=tri, rhs=g[:, 0:16], start=True, stop=True), [cst, g], [ps], sig=False)
            k.op("pe", lambda e: e.matmul(ps[:, 16:32], lhsT=ones, rhs=g[:, 0:16], start=True, stop=True), [cst, g], [ps])
            k.op("act", lambda e: e.copy(out=G["gcum"][:, ch, :], in_=ps[:, 0:16]), [ps], [G["gcum"]])
            k.op("act", lambda e: e.activation(out=G["egl"][:, ch, :], in_=ps[:, 16:32], func=AF.Exp), [ps], [G["egl"]])
            k.op("dve", lambda e: e.tensor_tensor(out=t2[:, 16:32], in0=ps[:, 16:32], in1=G["gcum"][:, ch, :], op=ALU.subtract),
                 [ps, G["gcum"]], [t2])
            k.op("act", lambda e: e.activation(out=G["edec"][:, ch, :], in_=t2[:, 16:32], func=AF.Exp), [t2], [G["edec"]])
            ps2 = P[2 + ch % 2]
            k.op("pe", lambda e: e.transpose(out=ps2[0:16, 0:128], in_=G["gcum"][:, ch, :], identity=ident),
                 [G["gcum"], cst], [ps2])
            k.op("act", lambda e: e.copy(out=gcumT[:, ch, :], in_=ps2[0:16, 0:128]), [ps2], [gcumT])


        def head_gen(h, B):
            raw = B['raw']
            y = B['y']
            sq = B['sq']
            rn = B['rn']
            St = B['St']
            W = B['W']
            OSB = B['OSB']
            NY = B['NY']
            Mb = B['Mb']

            def nps():
                p = B['banks'][B['pi'] % 2]
                B['pi'] += 1
                return p
            k.op('dve', lambda e: e.memset(St[:, :], 0.0), [], [St])
            yield
            for tl in range((NCH + 3) // 4):
                tok0 = tl * 512
                nch_t = min(4, NCH - tl * 4)
                ntk = nch_t * 128
                for gi in range(3):
                    row0 = gi * 2048 + h * 128
                    if tl == 0:
                        k.op('dve', lambda e: e.memset(raw[gi][:, 0:3], 0.0), [], [raw[gi]])
                        yield
                        k.dma('sp', raw[gi][:, 3:3 + ntk], X['qkvT'][row0:row0 + 128, 0:ntk], [X['qkvT']], [raw[gi]], raw[gi])
                        yield
                    else:
                        k.dma('sp', raw[gi][:, 0:3 + ntk], X['qkvT'][row0:row0 + 128, tok0 - 3:tok0 + ntk], [X['qkvT']], [raw[gi]], raw[gi])
                        yield
                    ce = 'dve'
                    wi = gi * 16 + h
                    k.op(ce, lambda e: e.tensor_scalar(y[gi][:, 0:ntk], raw[gi][:, 0:ntk], cw[:, wi, 0:1], None, ALU.mult), [raw[gi], cw], [y[gi]])
                    yield
                    for kk in range(1, 4):
                        k.op(ce, lambda e: e.scalar_tensor_tensor(out=y[gi][:, 0:ntk], in0=raw[gi][:, kk:kk + ntk], scalar=cw[:, wi, kk:kk + 1], in1=y[gi][:, 0:ntk], op0=ALU.mult, op1=ALU.add), [raw[gi], cw, y[gi]], [y[gi]])
                        yield
                    k.op('act', lambda e: e.activation(out=y[gi][:, 0:ntk], in_=y[gi][:, 0:ntk], func=AF.Silu), [y[gi]], [y[gi]])
                    yield
                for gi in range(2):
                    k.op('dve', lambda e: e.tensor_tensor(out=sq[:, 0:ntk], in0=y[gi][:, 0:ntk], in1=y[gi][:, 0:ntk], op=ALU.mult), [y[gi]], [sq])
                    yield
                    for hf in range((ntk + 511) // 512):
                        w_ = min(512, ntk - hf * 512)
                        ps = nps()
                        k.op('pe', lambda e: e.matmul(ps[:, 0:w_], lhsT=ones, rhs=sq[:, hf * 512:hf * 512 + w_], start=True, stop=True), [cst, sq], [ps])
                        yield
                        sc_ = 128.0 if gi == 0 else 1.0
                        k.op('act', lambda e: e.activation(out=rn[:, hf * 512:hf * 512 + w_], in_=ps[:, 0:w_], func=AF.Sqrt, scale=sc_, bias=sc_ * EPS), [ps], [rn])
                        yield
                    k.op('dve', lambda e: e.reciprocal(rn[:, 0:ntk], rn[:, 0:ntk]), [rn], [rn])
                    yield
                    k.op('dve', lambda e: e.tensor_tensor(out=y[gi][:, 0:ntk], in0=y[gi][:, 0:ntk], in1=rn[:, 0:ntk], op=ALU.mult), [y[gi], rn], [y[gi]])
                    yield
                for cl in range(nch_t):
                    ch = tl * 4 + cl
                    cs_ = slice(cl * 128, (cl + 1) * 128)
                    qn = y[0].t[:, cs_]
                    kn = y[1].t[:, cs_]
                    vv = y[2].t[:, cs_]
                    gc = G['gcum'].t[:, ch, h:h + 1]
                    psB = nps()
                    k.op('pe', lambda e: e.matmul(psB[:, 0:128], lhsT=oh[0:16, h * 128:(h + 1) * 128], rhs=gcumT[0:16, ch, :], start=True, stop=True), [oh, gcumT], [psB])
                    yield
                    k.op('dve', lambda e: e.tensor_scalar(W['Dm'][:, :], psB[:, 0:128], gc, 0.0, ALU.subtract, ALU.min), [psB, G['gcum']], [W['Dm']])
                    yield
                    k.op('act', lambda e: e.activation(out=W['DT'][:, :], in_=W['Dm'][:, :], func=AF.Exp), [W['Dm']], [W['DT']])
                    yield
                    k.op('act', lambda e: e.activation(out=W['Eg'][:, :], in_=psB[:, 0:128], func=AF.Exp), [psB], [W['Eg']])
                    yield
                    if GDN_STAGE < 1:
                        continue
                    psK = nps()
                    k.op('pe', lambda e: e.matmul(psK[:, 0:128], lhsT=kn, rhs=kn, start=True, stop=True), [y[1]], [psK], sig=False)
                    yield
                    k.op('pe', lambda e: e.matmul(psK[:, 128:256], lhsT=kn, rhs=qn, start=True, stop=True), [y[1], y[0]], [psK])
                    yield
                    k.op(CH_POOL, lambda e: e.tensor_tensor(out=W['DTs'][:, :], in0=W['DT'][:, :], in1=tris, op=ALU.mult), [W['DT'], cst], [W['DTs']])
                    yield
                    k.op(CH_POOL, lambda e: e.tensor_tensor(out=W['DTi'][:, :], in0=W['DT'][:, :], in1=tri, op=ALU.mult), [W['DT'], cst], [W['DTi']])
                    yield
                    if GDN_STAGE < 2:
                        continue
                    N0 = NY[0]
                    k.op('dve', lambda e: e.scalar_tensor_tensor(out=N0[:, 0:128], in0=psK[:, 0:128], scalar=G['nbeta'].t[:, ch, h:h + 1], in1=W['DTs'][:, :], op0=ALU.mult, op1=ALU.mult), [psK, G['nbeta'], W['DTs']], [N0])
                    yield
                    k.op('dve', lambda e: e.tensor_tensor(out=W['QKm'][:, :], in0=psK[:, 128:256], in1=W['DTi'][:, :], op=ALU.mult), [psK, W['DTi']], [W['QKm']])
                    yield
                    if GDN_STAGE < 3:
                        continue
                    psT = nps()
                    k.op('pe', lambda e: e.matmul(psT[:, 0:128], lhsT=N0[:, 0:128], rhs=ident, start=True, stop=True), [N0, cst], [psT])
                    yield
                    if not _os.environ.get('SKIP_M0COPY'):
                        k.op(_os.environ.get('M0ENG', 'act'), (lambda e: e.copy(out=Mb[0][:, :], in_=psT[:, 0:128])) if _os.environ.get('M0ENG', 'act') == 'act' else lambda e: e.tensor_copy(Mb[0][:, :], psT[:, 0:128]), [psT], [Mb[0]])
                        yield
                    if not _os.environ.get('GDN_SKIPY1'):
                        k.op(CH_POOL, lambda e: e.tensor_tensor(out=NY[1][:, 128:256], in0=N0[:, 0:128], in1=ident, op=ALU.add), [N0, cst], [NY[1]])
                        yield
                    if GDN_STAGE < 4:
                        continue
                    ps1 = nps()
                    LV0 = int(_os.environ.get('LV0', '9'))
                    k.op('pe', lambda e: e.matmul(ps1[:, 0:128], lhsT=Mb[0][:, :], rhs=N0[:, 0:128], start=True, stop=True), [Mb[0], N0], [ps1], sig=LV0 < 2)
                    yield
                    if LV0 >= 2:
                        k.op('pe', lambda e: e.matmul(ps1[:, 128:256], lhsT=N0[:, 0:128], rhs=Mb[0][:, :], start=True, stop=True), [Mb[0], N0], [ps1])
                        yield
                    if LV0 >= 3:
                        k.op('act', lambda e: e.copy(out=NY[1][:, 0:128], in_=ps1[:, 0:128]), [ps1], [NY[1]])
                        yield
                    if LV0 >= 4:
                        k.op('act', lambda e: e.copy(out=Mb[1][:, :], in_=ps1[:, 128:256]), [ps1], [Mb[1]])
                        yield
                    if GDN_STAGE < 5:
                        continue
                    cur = 1
                    for lv in range(1, 6):
                        nyc = NY[cur]
                        nyn = NY[1 - cur]
                        mc = Mb[cur]
                        mn = Mb[1 - cur]
                        ps2 = nps()
                        k.op('pe', lambda e: e.matmul(ps2[:, 0:256], lhsT=mc[:, :], rhs=nyc[:, 0:256], start=True, stop=True), [mc, nyc], [ps2], sig=False)
                        yield
                        k.op('pe', lambda e: e.matmul(ps2[:, 256:384], lhsT=nyc[:, 0:128], rhs=mc[:, :], start=True, stop=True), [mc, nyc], [ps2])
                        yield
                        k.op('act', lambda e: e.copy(out=nyn[:, 0:128], in_=ps2[:, 0:128]), [ps2], [nyn])
                        yield
                        k.op('dve', lambda e: e.tensor_tensor(out=nyn[:, 128:256], in0=ps2[:, 128:256], in1=nyc[:, 128:256], op=ALU.add), [ps2, nyc], [nyn])
                        yield
                        k.op('act', lambda e: e.copy(out=mn[:, :], in_=ps2[:, 256:384]), [ps2], [mn])
                        yield
                        cur = 1 - cur
                    if GDN_STAGE < 6:
                        continue
                    ps3 = nps()
                    k.op('pe', lambda e: e.matmul(ps3[:, 0:128], lhsT=Mb[cur][:, :], rhs=NY[cur][:, 128:256], start=True, stop=True), [Mb[cur], NY[cur]], [ps3])
                    yield
                    k.op('dve', lambda e: e.tensor_tensor(out=W['ZT'][:, :], in0=ps3[:, 0:128], in1=NY[cur][:, 128:256], op=ALU.add), [ps3, NY[cur]], [W['ZT']])
                    yield
                    if GDN_STAGE < 7:
                        continue
                    ps4 = nps()
                    k.op('pe', lambda e: e.transpose(out=ps4[:, 0:128], in_=kn, identity=ident), [y[1], cst], [ps4], sig=False)
                    yield
                    k.op('pe', lambda e: e.transpose(out=ps4[:, 128:256], in_=vv, identity=ident), [y[2], cst], [ps4])
                    yield
                    k.op('act', lambda e: e.activation(out=W['kdec'][:, :], in_=ps4[:, 0:128], func=AF.Copy, scale=G['edec'].t[:, ch, h:h + 1]), [ps4, G['edec']], [W['kdec']])
                    yield
                    k.op('act', lambda e: e.copy(out=W['vtok'][:, :], in_=ps4[:, 128:256]), [ps4], [W['vtok']])
                    yield
                    k.op(CH_POOL, lambda e: e.tensor_tensor(out=W['kegT'][:, :], in0=kn, in1=W['Eg'][:, :], op=ALU.mult), [y[1], W['Eg']], [W['kegT']])
                    yield
                    k.op(CH_POOL, lambda e: e.tensor_tensor(out=W['qdT'][:, :], in0=qn, in1=W['Eg'][:, :], op=ALU.mult), [y[0], W['Eg']], [W['qdT']])
                    yield
                    if GDN_STAGE < 8:
                        continue
                    ps5 = nps()
                    k.op('pe', lambda e: e.matmul(ps5[:, 0:128], lhsT=W['kegT'][:, :], rhs=St[:, :], start=True, stop=True), [W['kegT'], St], [ps5])
                    yield
                    k.op('dve', lambda e: e.scalar_tensor_tensor(out=W['r'][:, :], in0=ps5[:, 0:128], scalar=-1.0, in1=W['vtok'][:, :], op0=ALU.mult, op1=ALU.add), [W['vtok'], ps5], [W['r']])
                    yield
                    k.op('pe', lambda e: e.matmul(ps5[:, 128:256], lhsT=W['ZT'][:, :], rhs=W['r'][:, :], start=True, stop=True), [W['ZT'], W['r']], [ps5])
                    yield
                    k.op('act', lambda e: e.activation(out=W['vnew'][:, :], in_=ps5[:, 128:256], func=AF.Copy, scale=G['beta'].t[:, ch, h:h + 1]), [ps5, G['beta']], [W['vnew']])
                    yield
                    ps6 = nps()
                    k.op('pe', lambda e: e.matmul(ps6[:, 0:128], lhsT=W['qdT'][:, :], rhs=St[:, :], start=True, stop=False), [W['qdT'], St], [ps6], sig=False)
                    yield
                    k.op('pe', lambda e: e.matmul(ps6[:, 0:128], lhsT=W['QKm'][:, :], rhs=W['vnew'][:, :], start=False, stop=True), [W['QKm'], W['vnew']], [ps6], sig=False)
                    yield
                    k.op('pe', lambda e: e.matmul(ps6[:, 128:256], lhsT=W['kdec'][:, :], rhs=W['vnew'][:, :], start=True, stop=True), [W['kdec'], W['vnew']], [ps6])
                    yield
                    osb_ = OSB[ch % 2]
                    k.op('act', lambda e: e.copy(out=osb_[:, :], in_=ps6[:, 0:128]), [ps6], [osb_])
                    yield
                    k.dma(_os.environ.get('OQ', 'sp'), X['oscr'][ch * 128:(ch + 1) * 128, h * 128:(h + 1) * 128], osb_[:, :], [osb_], [X['oscr']], X['oscr'])
                    yield
                    k.op('dve', lambda e: e.tensor_scalar(St[:, :], St[:, :], G['egl'].t[:, ch, h:h + 1], None, ALU.mult), [St, G['egl']], [St])
                    yield
                    k.op('dve', lambda e: e.tensor_tensor(out=St[:, :], in0=ps6[:, 128:256], in1=St[:, :], op=ALU.add), [St, ps6], [St])
                    yield

        HG = C.get("gdn_interleave", 4)
        ctxs = []
        for ci in range(HG):
            Bc = dict(raw=[k.sb(es, f"raw{ci}_{i}", [128, 515], F32) for i in range(3)],
                      y=[k.sb(es, f"y{ci}_{i}", [128, 512], F32) for i in range(3)],
                      sq=k.sb(es, f"sq{ci}", [128, 512], F32), rn=k.sb(es, f"rn{ci}", [128, 512], F32),
                      St=k.sb(es, f"S{ci}", [128, 128], F32), W={},
                      OSB=[k.sb(es, f"osb{ci}_{i}", [128, 128], F32) for i in range(2)],
                      NY=[k.sb(es, f"NY{ci}_{i}", [128, 256], F32) for i in range(2)],
                      Mb=[k.sb(es, f"Mb{ci}_{i}", [128, 128], F32) for i in range(2)],
                      banks=[P[(2 * ci) % 8], P[(2 * ci + 1) % 8]], pi=0)
            for n in ("Dm", "DT", "Eg", "DTs", "DTi", "QKm", "M0", "kdec", "vtok", "kegT", "qdT", "r", "vnew", "ZT"):
                Bc["W"][n] = k.sb(es, f"w{ci}_" + n, [128, 128], F32)
            ctxs.append(Bc)
        for h0 in range(0, NH, HG):
            gens = []
            for ci, h in enumerate(range(h0, min(h0 + HG, NH))):
                ctxs[ci]["pi"] = 0
                gens.append(head_gen(h, ctxs[ci]))
            while gens:
                for g_ in list(gens):
                    try:
                        next(g_)
                    except StopIteration:
                        gens.remove(g_)
    k.barrier()


def rot_norm(k, src, dst, nh, ssq, rstd, tmp, gain, cs, scale, do_norm, eng="dve", rot=True):
    s3 = src.t[:, 0:nh * 128].rearrange("p (h d) -> p h d", h=nh)
    t3 = tmp.t[:, 0:nh * 128].rearrange("p (h d) -> p h d", h=nh)
    d3 = dst.t[:, 0:nh * 128].rearrange("p (h d) -> p h d", h=nh)
    if do_norm:
        k.op(eng, lambda e: e.tensor_tensor(out=t3, in0=s3, in1=s3, op=ALU.mult), [src], [tmp])
        k.op("dve", lambda e: e.tensor_reduce(out=ssq[:, 0:nh], in_=t3, axis=AX.X, op=ALU.add), [tmp], [ssq])
        k.op("act", lambda e: e.activation(out=rstd[:, 0:nh], in_=ssq[:, 0:nh], func=AF.Sqrt, scale=1.0 / 128, bias=EPS),
             [ssq], [rstd])
        k.op("dve", lambda e: e.reciprocal(rstd[:, 0:nh], rstd[:, 0:nh]), [rstd], [rstd])
        k.op(eng, lambda e: e.tensor_tensor(out=t3, in0=s3, in1=rstd.t[:, 0:nh].unsqueeze(2).to_broadcast([128, nh, 128]),
                                            op=ALU.mult), [src, rstd], [tmp])
        k.op(eng, lambda e: e.scalar_tensor_tensor(out=t3, in0=t3, scalar=scale,
                                                   in1=gain.t[:, 0:128].unsqueeze(1).to_broadcast([128, nh, 128]),
                                                   op0=ALU.mult, op1=ALU.mult), [tmp, gain], [tmp])
        base = tmp
        b3 = t3
    else:
        k.op(eng, lambda e: e.tensor_scalar(t3, s3, scale, None, ALU.mult), [src], [tmp])
        base = tmp
        b3 = t3
    if not rot:
        k.op("act", lambda e: e.copy(out=d3, in_=b3), [base], [dst])
        return
    cosb = cs.t[:, 0:16].unsqueeze(1).to_broadcast([128, nh, 16])
    sinb = cs.t[:, 16:32].unsqueeze(1).to_broadcast([128, nh, 16])
    k.op("act", lambda e: e.copy(out=d3[:, :, 32:128], in_=b3[:, :, 32:128]), [base], [dst])
    ra = src.t[:, 0:nh * 128].rearrange("p (h d) -> p h d", h=nh)
    k.op(eng, lambda e: e.tensor_tensor(out=ra[:, :, 32:48], in0=b3[:, :, 0:16], in1=cosb, op=ALU.mult), [base, cs], [src])
    k.op(eng, lambda e: e.tensor_tensor(out=ra[:, :, 48:64], in0=b3[:, :, 16:32], in1=sinb, op=ALU.mult), [base, cs], [src])
    k.op(eng, lambda e: e.tensor_tensor(out=ra[:, :, 64:80], in0=b3[:, :, 16:32], in1=cosb, op=ALU.mult), [base, cs], [src])
    k.op(eng, lambda e: e.tensor_tensor(out=ra[:, :, 80:96], in0=b3[:, :, 0:16], in1=sinb, op=ALU.mult), [base, cs], [src])
    k.op(eng, lambda e: e.tensor_tensor(out=d3[:, :, 0:16], in0=ra[:, :, 32:48], in1=ra[:, :, 48:64], op=ALU.subtract), [src], [dst])
    k.op(eng, lambda e: e.tensor_tensor(out=d3[:, :, 16:32], in0=ra[:, :, 64:80], in1=ra[:, :, 80:96], op=ALU.add), [src], [dst])


def phase3(k, X, P, C):
    cst = C["cst"]
    ident = C["ident"]
    NB = C["n_own_blocks"]
    NKB = C["n_key_blocks"]
    with ExitStack() as es:
        identb = k.sb(es, "identb", [128, 128], BF16)
        onesb = k.sb(es, "onesb", [128, 128], BF16)
        k.op("dve", lambda e: e.tensor_copy(identb[:, :], ident), [cst], [identb])
        k.op("dve", lambda e: e.tensor_copy(onesb[:, :], cst.t[:, 384:512]), [cst], [onesb])
        KT = k.sb(es, "KT", [128, 4, S], BF16)
        IKT = k.sb(es, "IKT", [128, S], BF16)
        V = k.sb(es, "V", [128, 32, 512], BF16)
        kpos = k.sb(es, "kpos", [128, 256], F32)
        posr = k.sb(es, "posr", [128, 1], F32)
        gq_n = k.sb(es, "gq_n", [128, 128], F32)
        gk_n = k.sb(es, "gk_n", [128, 128], F32)
        gd_n = k.sb(es, "gd_n", [128, 128], F32)
        jsel = k.sb(es, "jsel", [128, 1], F32)
        for b_, nm in ((gq_n, "att_q_norm"), (gk_n, "att_k_norm"), (gd_n, "gdn_norm")):
            k.dma("sp", b_[:, :], X[nm].t.partition_broadcast(128), [X[nm]], [b_], b_)
        k.dma("sp", kpos[:, :], X["kpos"].t.partition_broadcast(128), [X["kpos"]], [kpos], kpos)
        k.dma("sp", jsel[:, :], X["jsel"][:, :], [X["jsel"]], [jsel], jsel)
        med = [k.sb(es, f"med{i}", [128, 2048], F32) for i in range(3)]
        bfb = k.sb(es, "bfb", [128, 2048], BF16)
        cs = k.sb(es, "cs", [128, 32], F32)
        pos = k.sb(es, "pos", [128, 1], F32)
        iw = k.sb(es, "iw", [128, 32], F32)
        ssq = k.sb(es, "ssq3", [128, 16], F32)
        rstd = k.sb(es, "rstd3", [128, 16], F32)
        QT = k.sb(es, "QT", [128, 16, 128], BF16)
        IQT = k.sb(es, "IQT", [128, 32, 128], BF16)
        sc = k.sb(es, "sc", [128, S], F32)
        wk = k.sb(es, "wk", [128, S], F32)
        m8 = k.sb(es, "m8", [128, 8], F32)
        tau = k.sb(es, "tau", [128, 1], F32)
        mk = k.sb(es, "mk", [128, S], BF16)
        mkT = k.sb(es, "mkT", [128, 32, 128], BF16)
        rl = [k.sb(es, f"rl{i}", [128, 512], F32) for i in range(3)]
        pe_ = [k.sb(es, f"pe{i}", [128, 512], BF16) for i in range(3)]
        pm = [k.sb(es, f"pm{i}", [128, 512], BF16) for i in range(3)]
        rinv = k.sb(es, "rinv", [128, 512], F32)
        obT = [k.sb(es, f"obT{i}", [128, 512], BF16) for i in range(2)]
        mT = [k.sb(es, f"mT{i}", [128, 512], BF16) for i in range(2)]
        pi = [0]

        def nps():
            p = P[pi[0] % 8]; pi[0] += 1
            return p

        for sb_ in range(NKB):
            t0 = sb_ * 128
            a = med[0]; tm = med[1]
            k.dma("sp", a[:, 0:1024], X["akv"][t0:t0 + 128, :], [X["akv"]], [a], a)
            k.dma("sp", a[:, 1024:1152], X["ikr"][t0:t0 + 128, :], [X["ikr"]], [a], a)
            k.dma("sp", cs[:, :], X["cs_all"][t0:t0 + 128, :], [X["cs_all"]], [cs], cs)
            k.op("act", lambda e: e.copy(out=V[:, sb_, :], in_=a[:, 512:1024]), [a], [V])
            rot_norm(k, _view(a, 1024, 128), _view(bfb, 1024, 128), 1, ssq, rstd, _view(tm, 1024, 128), gk_n, cs, 1.0, False)
            rot_norm(k, _view(a, 0, 512), _view(bfb, 0, 512), 4, ssq, rstd, _view(tm, 0, 512), gk_n, cs, 1.0, True)
            ps = nps()
            psb = ps.t[:, :].bitcast(BF16)
            for g in range(4):
                k.op("pe", lambda e: e.transpose(out=psb[:, g * 128:(g + 1) * 128], in_=bfb[:, g * 128:(g + 1) * 128], identity=identb[:, :]),
                     [bfb, identb], [ps], sig=False)
            k.op("pe", lambda e: e.transpose(out=psb[:, 512:640], in_=bfb[:, 1024:1152], identity=identb[:, :]),
                 [bfb, identb], [ps])
            k.op("act", lambda e: e.copy(out=KT[:, :, t0:t0 + 128], in_=psb[:, 0:512].rearrange("p (g s) -> p g s", g=4)),
                 [ps], [KT])
            k.op("act", lambda e: e.copy(out=IKT[:, t0:t0 + 128], in_=psb[:, 512:640]), [ps], [IKT])

        for n in range(NB):
            r0 = n * 128
            NK = min(2 * n + 2, NKB)
            NKc = NK * 128
            aq = med[0]; tm = med[1]
            k.dma("sp", aq[:, :], X["aq"][r0:r0 + 128, :], [X["aq"]], [aq], aq)
            k.dma("sp", iw[:, :], X["iw"][r0:r0 + 128, :], [X["iw"]], [iw], iw)
            k.dma("sp", cs[:, :], X["cs_own"][r0:r0 + 128, :], [X["cs_own"]], [cs], cs)
            k.dma("sp", pos[:, :], X["pos_own"][r0:r0 + 128, :], [X["pos_own"]], [pos], pos)
            k.op("dve", lambda e: e.tensor_scalar(iw[:, :], iw[:, :], 32 ** -0.5, None, ALU.mult), [iw], [iw])
            rot_norm(k, aq, bfb, 16, ssq, rstd, tm, gq_n, cs, 128 ** -0.5, True)
            for g4 in range(4):
                ps = nps(); psb = ps.t[:, :].bitcast(BF16)
                for j in range(4):
                    hq = g4 * 4 + j
                    k.op("pe", lambda e: e.transpose(out=psb[:, j * 128:(j + 1) * 128], in_=bfb[:, hq * 128:(hq + 1) * 128],
                                                     identity=identb[:, :]), [bfb, identb], [ps], sig=(j == 3))
                k.op("act", lambda e: e.copy(out=QT[:, g4 * 4:(g4 + 1) * 4, :], in_=psb[:, 0:512].rearrange("p (g s) -> p g s", g=4)),
                     [ps], [QT])
            for half in range(2):
                iqr = med[0]; itm = med[1]
                k.dma("sp", iqr[:, :], X["iq"][r0:r0 + 128, half * 2048:(half + 1) * 2048], [X["iq"]], [iqr], iqr)
                rot_norm(k, iqr, bfb, 16, ssq, rstd, itm, gq_n, cs, 128 ** -0.5, False, eng="dve")
                for g4 in range(4):
                    ps = nps(); psb = ps.t[:, :].bitcast(BF16)
                    for j in range(4):
                        hq = g4 * 4 + j
                        k.op("pe", lambda e: e.transpose(out=psb[:, j * 128:(j + 1) * 128], in_=bfb[:, hq * 128:(hq + 1) * 128],
                                                         identity=identb[:, :]), [bfb, identb], [ps], sig=(j == 3))
                    k.op("act", lambda e: e.copy(out=IQT[:, half * 16 + g4 * 4:half * 16 + (g4 + 1) * 4, :],
                                                 in_=psb[:, 0:512].rearrange("p (g s) -> p g s", g=4)), [ps], [IQT])
            nkt = (NKc + 511) // 512
            scv = [Buf(sc.t[:, kt * 512:kt * 512 + min(512, NKc - kt * 512)], f"scv{kt}") for kt in range(nkt)]
            ri = 0
            for hi in range(32):
                for kt in range(nkt):
                    w_ = min(512, NKc - kt * 512)
                    acc_eng = "dve"
                    ps = nps()
                    k.op("pe", lambda e: e.matmul(ps[:, 0:w_], lhsT=IQT[:, hi, :], rhs=IKT[:, kt * 512:kt * 512 + w_],
                                                  start=True, stop=True), [IQT, IKT], [ps])
                    r_ = rl[ri % 3]; ri += 1
                    k.op("act", lambda e: e.activation(out=r_[:, 0:w_], in_=ps[:, 0:w_], func=AF.Relu), [ps], [r_])
                    if hi == 0:
                        k.op(acc_eng, lambda e: e.tensor_scalar(sc[:, kt * 512:kt * 512 + w_], r_[:, 0:w_], iw[:, 0:1], None, ALU.mult),
                             [r_, iw], [scv[kt], sc])
                    else:
                        k.op(acc_eng, lambda e: e.scalar_tensor_tensor(out=sc[:, kt * 512:kt * 512 + w_], in0=r_[:, 0:w_],
                                                                       scalar=iw[:, hi:hi + 1], in1=sc[:, kt * 512:kt * 512 + w_],
                                                                       op0=ALU.mult, op1=ALU.add), [r_, iw, scv[kt]], [scv[kt]])
            c0 = NKc - 256
            k.op("dve", lambda e: e.tensor_scalar(posr[:, :], pos[:, :], float(-c0), None, ALU.add), [pos], [posr])
            k.op("dve", lambda e: e.tensor_scalar(wk[:, c0:NKc], kpos[:, 0:256], posr[:, 0:1], -1e30, ALU.is_gt, ALU.mult),
                 [kpos, posr], [wk])
            k.op("dve", lambda e: e.tensor_tensor(out=sc[:, c0:NKc], in0=sc[:, c0:NKc], in1=wk[:, c0:NKc], op=ALU.add),
                 [wk] + scv, [sc] + scv)
            if NKc > 256:
                src = sc
                for rnd in range(32):
                    k.op("dve", lambda e: e.max(out=m8[:, :], in_=src[:, 0:NKc]), [src], [m8])
                    if rnd < 31:
                        k.op("dve", lambda e: e.match_replace(out=wk[:, 0:NKc], in_to_replace=m8[:, :], in_values=src[:, 0:NKc],
                                                              imm_value=-3e38), [src, m8], [wk])
                        src = wk
                k.op("dve", lambda e: e.tensor_scalar(tau[:, :], m8[:, 7:8], -1e29, None, ALU.max), [m8], [tau])
            else:
                k.op("dve", lambda e: e.memset(tau[:, :], -1e29), [], [tau])
            k.op("dve", lambda e: e.tensor_scalar(mk[:, 0:NKc], sc[:, 0:NKc], tau[:, 0:1], None, ALU.is_ge), [sc, tau], [mk])
            for kb4 in range((NK + 3) // 4):
                ps = nps(); psb = ps.t[:, :].bitcast(BF16)
                nb_ = min(4, NK - kb4 * 4)
                for j in range(nb_):
                    kb = kb4 * 4 + j
                    k.op("pe", lambda e: e.transpose(out=psb[:, j * 128:(j + 1) * 128], in_=mk[:, kb * 128:(kb + 1) * 128],
                                                     identity=identb[:, :]), [mk, identb], [ps], sig=(j == nb_ - 1))
                k.op("act", lambda e: e.copy(out=mkT[:, kb4 * 4:kb4 * 4 + nb_, :],
                                             in_=psb[:, 0:nb_ * 128].rearrange("p (g s) -> p g s", g=nb_)), [ps], [mkT])
            for g in range(4):
                psO = P[0]; psR = P[1]
                xi = 0
                for kb in range(NK):
                    psS = P[2 + (pi[0] % 6)]; pi[0] += 1
                    k.op("pe", lambda e: e.matmul(psS[:, 0:512], lhsT=KT[:, g, kb * 128:(kb + 1) * 128],
                                                  rhs=QT[:, g * 4:(g + 1) * 4, :].rearrange("p g s -> p (g s)"), start=True, stop=True),
                         [KT, QT], [psS])
                    pe1 = pe_[xi % 3]; pm1 = pm[xi % 3]; xi += 1
                    k.op("act", lambda e: e.activation(out=pe1[:, :], in_=psS[:, 0:512], func=AF.Exp), [psS], [pe1])
                    me = "dve"
                    k.op(me, lambda e: e.tensor_tensor(out=pm1[:, :].rearrange("p (g s) -> p g s", g=4),
                                                       in0=pe1[:, :].rearrange("p (g s) -> p g s", g=4),
                                                       in1=mkT.t[:, kb, :].unsqueeze(1).to_broadcast([128, 4, 128]), op=ALU.mult),
                         [pe1, mkT], [pm1])
                    k.op("pe", lambda e: e.matmul(psO[:, 0:512], lhsT=V[:, kb, g * 128:(g + 1) * 128], rhs=pm1[:, :],
                                                  start=(kb == 0), stop=(kb == NK - 1)), [V, pm1], [psO], sig=False)
                    k.op("pe", lambda e: e.matmul(psR[:, 0:512], lhsT=onesb[:, :], rhs=pm1[:, :],
                                                  start=(kb == 0), stop=(kb == NK - 1)), [onesb, pm1], [psR])
                k.op("act", lambda e: e.copy(out=rinv[:, :], in_=psR[:, 0:512]), [psR], [rinv])
                k.op("dve", lambda e: e.reciprocal(rinv[:, :], rinv[:, :]), [rinv], [rinv])
                ob = obT[g % 2]
                k.op("dve", lambda e: e.tensor_tensor(out=ob[:, :], in0=psO[:, 0:512], in1=rinv[:, :], op=ALU.mult), [psO, rinv], [ob])
                for hh in range(4):
                    rr = 2048 + (g * 4 + hh) * 128
                    k.dma("sp", X["mixT"][rr:rr + 128, r0:r0 + 128], ob[:, hh * 128:(hh + 1) * 128], [ob], [X["mixT"]], X["mixT"])
            A = med[0]; Bt = med[1]; gzt = med[2]
            k.dma("sp", A[:, :], X["oscr"][(2 * n) * 128:(2 * n + 1) * 128, :], [X["oscr"]], [A], A)
            k.dma("sp", Bt[:, :], X["oscr"][(2 * n + 1) * 128:(2 * n + 2) * 128, :], [X["oscr"]], [Bt], Bt)
            k.dma("sp", gzt[:, 0:2048], X["gz"][r0:r0 + 128, :], [X["gz"]], [gzt], gzt)
            k.op("dve", lambda e: e.tensor_tensor(out=Bt[:, :], in0=Bt[:, :], in1=A[:, :], op=ALU.subtract), [A, Bt], [Bt])
            k.op("dve", lambda e: e.scalar_tensor_tensor(out=A[:, :], in0=Bt[:, :], scalar=jsel[:, 0:1], in1=A[:, :],
                                                         op0=ALU.mult, op1=ALU.add), [A, Bt, jsel], [A])
            A3 = A.t[:, :].rearrange("p (h d) -> p h d", h=16)
            B3 = Bt.t[:, :].rearrange("p (h d) -> p h d", h=16)
            k.op("dve", lambda e: e.tensor_tensor(out=B3, in0=A3, in1=A3, op=ALU.mult), [A], [Bt])
            k.op("dve", lambda e: e.tensor_reduce(out=ssq[:, 0:16], in_=B3, axis=AX.X, op=ALU.add), [Bt], [ssq])
            k.op("act", lambda e: e.activation(out=rstd[:, 0:16], in_=ssq[:, 0:16], func=AF.Sqrt, scale=1.0 / 128, bias=EPS), [ssq], [rstd])
            k.op("dve", lambda e: e.reciprocal(rstd[:, 0:16], rstd[:, 0:16]), [rstd], [rstd])
            k.op("dve", lambda e: e.tensor_tensor(out=A3, in0=A3, in1=rstd.t[:, 0:16].unsqueeze(2).to_broadcast([128, 16, 128]), op=ALU.mult),
                 [A, rstd], [A])
            k.op("dve", lambda e: e.tensor_tensor(out=A3, in0=A3, in1=gd_n.t[:, :].unsqueeze(1).to_broadcast([128, 16, 128]), op=ALU.mult),
                 [A, gd_n], [A])
            k.op("act", lambda e: e.activation(out=gzt[:, 0:2048], in_=gzt[:, 0:2048], func=AF.Silu), [gzt], [gzt])
            k.op("dve", lambda e: e.tensor_tensor(out=A[:, :], in0=A[:, :], in1=gzt[:, 0:2048], op=ALU.mult), [A, gzt], [A])
            for g4 in range(4):
                ps = nps()
                for j in range(4):
                    hh = g4 * 4 + j
                    k.op("pe", lambda e: e.transpose(out=ps[:, j * 128:(j + 1) * 128], in_=A[:, hh * 128:(hh + 1) * 128], identity=ident),
                         [A, cst], [ps], sig=(j == 3))
                m_ = mT[g4 % 2]
                k.op("act", lambda e: e.copy(out=m_[:, :], in_=ps[:, 0:512]), [ps], [m_])
                for j in range(4):
                    rr = (g4 * 4 + j) * 128
                    k.dma("sp", X["mixT"][rr:rr + 128, r0:r0 + 128], m_[:, j * 128:(j + 1) * 128], [m_], [X["mixT"]], X["mixT"])
    k.barrier()


class _View:
    def __init__(self, parent, c0, n):
        self.p = parent
        self.t = parent.t[:, c0:c0 + n]
        self.name = parent.name + "_v"
        self.psum = False
        self.dram = False

    def __getitem__(self, key):
        return self.t[key]
    w = property(lambda self: self.p.w, lambda self, v: setattr(self.p, "w", v))
    r = property(lambda self: self.p.r, lambda self, v: setattr(self.p, "r", v))
    dsem = property(lambda self: self.p.dsem, lambda self, v: setattr(self.p, "dsem", v))
    dcnt = property(lambda self: self.p.dcnt, lambda self, v: setattr(self.p, "dcnt", v))


def _view(parent, c0, n):
    return _View(parent, c0, n)


def resid_evac_factory(k, X, src, row0, xr, ot, ctr):
    def f(off):
        def evac(q, ps, ncols):
            i = ctr[0] % len(xr); ctr[0] += 1
            xb = xr[i]; o = ot[i]
            r = row0 + q * 128
            k.dma("sp", xb[:, 0:ncols], src[r:r + 128, off:off + ncols], [src], [xb], xb)
            k.op("dve", lambda e: e.tensor_tensor(out=o[:, 0:ncols], in0=ps[:, 0:ncols], in1=xb[:, 0:ncols], op=ALU.add),
                 [ps, xb], [o])
            k.dma("sp", X["out"][r:r + 128, off:off + ncols], o[:, 0:ncols], [o], [X["out"]], X["out"])
        return evac
    return f


def phase4(k, X, P, C):
    with ExitStack() as es:
        hTt = es.enter_context(k.nc.sbuf_tensor("hT4", [128, 32, 1024], BF16))
        hTb = Buf(hTt, "hT4")
        xr = [k.sb(es, f"xr{i}", [128, 512], F32) for i in range(4)]
        ot = [k.sb(es, f"ot4{i}", [128, 512], F32) for i in range(4)]
        ring = WRing(k, es)
        wv = X["w_out"].t.rearrange("(c p) n -> p c n", p=128)
        mv = X["mixT"].t.rearrange("(c p) t -> p c t", p=128)
        jobs = []
        ctr = [0]
        for t in range(C["n_own_tiles"] // 2):
            mk_jobs(jobs, t, "N", X["w_out"], wv, 32, 0, D, (hTt, hTb),
                    resid_evac_factory(k, X, X["x_own"], t * 1024, xr, ot, ctr), nsub=8)

        def prep(t):
            for c4 in range(4):
                k.dma("sp", hTt[:, c4 * 8:(c4 + 1) * 8, :], mv[:, c4 * 8:(c4 + 1) * 8, t * 1024:(t + 1) * 1024],
                      [X["mixT"]], [hTb], hTb)
        linear(k, ring, jobs, P, prep)
    k.barrier()


def phase5(k, X, P, C):
    cst = C["cst"]; ident = C["ident"]
    with ExitStack() as es:
        xt = [k.sb(es, f"xt5{i}", [128, D], F32) for i in range(2)]
        junk = k.sb(es, "junk5", [128, D], BF16)
        hTt = es.enter_context(k.nc.sbuf_tensor("hT5", [128, 32, 512], BF16))
        hTb = Buf(hTt, "hT5")
        ssq = k.sb(es, "ssq5", [128, 16], F32)
        rstd = k.sb(es, "rstd5", [128, 16], F32)
        gamq = k.sb(es, "gamq", [128, 32], F32)
        gamkv = k.sb(es, "gamkv", [128, 32], F32)
        qn_g = k.sb(es, "qn_g", [128, 128], F32)
        kn_g = k.sb(es, "kn_g", [128, 128], F32)
        identb = k.sb(es, "identb5", [128, 128], BF16)
        onesb = k.sb(es, "onesb5", [128, 128], BF16)
        k.op("dve", lambda e: e.tensor_copy(identb[:, :], ident), [cst], [identb])
        k.op("dve", lambda e: e.tensor_copy(onesb[:, :], cst.t[:, 384:512]), [cst], [onesb])
        k.dma("sp", gamq[:, :], X["norm_mem_q"][:, :], [X["norm_mem_q"]], [gamq], gamq)
        k.dma("sp", gamkv[:, :], X["norm_mem_kv"][:, :], [X["norm_mem_kv"]], [gamkv], gamkv)
        k.dma("sp", qn_g[:, :], X["mem_q_norm"].t.partition_broadcast(128), [X["mem_q_norm"]], [qn_g], qn_g)
        k.dma("sp", kn_g[:, :], X["mem_k_norm"].t.partition_broadcast(128), [X["mem_k_norm"]], [kn_g], kn_g)
        MKT = k.sb(es, "MKT", [128, 4, 256], BF16)
        MV = k.sb(es, "MV", [128, 2, 512], BF16)
        MQT = k.sb(es, "MQT", [128, 4, 512], BF16)
        moTt = es.enter_context(k.nc.sbuf_tensor("moT", [128, 4, 512], BF16))
        moTb = Buf(moTt, "moT")
        o32 = [k.sb(es, f"o32_{i}", [128, 512], F32) for i in range(2)]
        t32 = k.sb(es, "t32", [128, 512], F32)
        obf = k.sb(es, "obf5", [128, 512], BF16)
        pe_ = [k.sb(es, f"pe5{i}", [128, 512], BF16) for i in range(2)]
        rinv = k.sb(es, "rinv5", [128, 512], F32)
        xr = [k.sb(es, f"xr5{i}", [128, 512], F32) for i in range(4)]
        ot = [k.sb(es, f"ot5{i}", [128, 512], F32) for i in range(4)]
        ring = WRing(k, es, npieces=4)
        ctr = [0]
        oc = [0]
        wkv = X["w_mem_kv"].t.rearrange("(c p) n -> p c n", p=128)

        def kv_fac(off):
            def evac(q, ps, ncols):
                o = o32[oc[0] % 2]; oc[0] += 1
                k.op("act", lambda e: e.copy(out=o[:, :], in_=ps[:, 0:512]), [ps], [o])
                if off == 0:
                    rot_norm(k, o, obf, 4, ssq, rstd, t32, kn_g, None, 1.0, True, rot=False)
                    p2 = P[6 + q % 2]; pb = p2.t[:, :].bitcast(BF16)
                    for j in range(4):
                        k.op("pe", lambda e: e.transpose(out=pb[:, j * 128:(j + 1) * 128], in_=obf[:, j * 128:(j + 1) * 128],
                                                         identity=identb[:, :]), [obf, identb], [p2], sig=(j == 3))
                    k.op("act", lambda e: e.copy(out=MKT[:, :, q * 128:(q + 1) * 128],
                                                 in_=pb[:, 0:512].rearrange("p (g s) -> p g s", g=4)), [p2], [MKT])
                else:
                    k.op("dve", lambda e: e.tensor_copy(MV[:, q, :], o[:, :]), [o], [MV])
            return evac
        jobs = []
        mk_jobs(jobs, "kv", "N", X["w_mem_kv"], wkv, 32, 0, 1024, (hTt, hTb), kv_fac, nsub=2)
        linear(k, ring, jobs, P[0:4], lambda pid: build_hT(k, X["mem"], 0, xt, hTt, hTb, ssq, rstd, junk, gamkv, ident, cst, P[6:8], n_sub=2))
        wq = X["w_mem_q"].t.rearrange("(c p) n -> p c n", p=128)
        wo = X["w_mem_o"].t.rearrange("(c p) n -> p c n", p=128)
        for t in range(C["n_own_tiles"]):
            def q_fac(off):
                def evac(q, ps, ncols):
                    o = o32[oc[0] % 2]; oc[0] += 1
                    k.op("act", lambda e: e.copy(out=o[:, :], in_=ps[:, 0:512]), [ps], [o])
                    rot_norm(k, o, obf, 4, ssq, rstd, t32, qn_g, None, 128 ** -0.5, True, rot=False)
                    p2 = P[6 + q % 2]; pb = p2.t[:, :].bitcast(BF16)
                    for j in range(4):
                        k.op("pe", lambda e: e.transpose(out=pb[:, j * 128:(j + 1) * 128], in_=obf[:, j * 128:(j + 1) * 128],
                                                         identity=identb[:, :]), [obf, identb], [p2], sig=(j == 3))
                    k.op("act", lambda e: e.copy(out=MQT[:, :, q * 128:(q + 1) * 128],
                                                 in_=pb[:, 0:512].rearrange("p (g s) -> p g s", g=4)), [p2], [MQT])
                return evac
            jobs = []
            mk_jobs(jobs, t, "N", X["w_mem_q"], wq, 32, 0, 512, (hTt, hTb), q_fac)
            linear(k, ring, jobs, P[0:4], lambda pid: build_hT(k, X["out"], pid * 512, xt, hTt, hTb, ssq, rstd, junk, gamq, ident, cst, P[6:8]))
            for hd in range(4):
                psO = P[0]; psR = P[1]
                for mb in range(2):
                    psS = P[2 + mb]
                    k.op("pe", lambda e: e.matmul(psS[:, 0:512], lhsT=MKT[:, hd, mb * 128:(mb + 1) * 128], rhs=MQT[:, hd, :],
                                                  start=True, stop=True), [MKT, MQT], [psS])
                    pe1 = pe_[mb]
                    k.op("act", lambda e: e.activation(out=pe1[:, :], in_=psS[:, 0:512], func=AF.Exp), [psS], [pe1])
                    k.op("pe", lambda e: e.matmul(psO[:, 0:512], lhsT=MV[:, mb, hd * 128:(hd + 1) * 128], rhs=pe1[:, :],
                                                  start=(mb == 0), stop=(mb == 1)), [MV, pe1], [psO], sig=False)
                    k.op("pe", lambda e: e.matmul(psR[:, 0:512], lhsT=onesb[:, :], rhs=pe1[:, :],
                                                  start=(mb == 0), stop=(mb == 1)), [onesb, pe1], [psR])
                k.op("act", lambda e: e.copy(out=rinv[:, :], in_=psR[:, 0:512]), [psR], [rinv])
                k.op("dve", lambda e: e.reciprocal(rinv[:, :], rinv[:, :]), [rinv], [rinv])
                k.op("dve", lambda e: e.tensor_tensor(out=moTt[:, hd, :], in0=psO[:, 0:512], in1=rinv[:, :], op=ALU.mult),
                     [psO, rinv], [moTb])
            jobs = []
            mk_jobs(jobs, t, "N", X["w_mem_o"], wo, 4, 0, D, (moTt, moTb),
                    resid_evac_factory(k, X, X["out"], t * 512, xr, ot, ctr))
            linear(k, ring, jobs, P[0:4], lambda pid: None)
    k.barrier()


def phase6a(k, X, P, C):
    cst = C["cst"]; ident = C["ident"]
    with ExitStack() as es:
        xt = [k.sb(es, f"xt6{i}", [128, D], F32) for i in range(2)]
        junk = k.sb(es, "junk6", [128, D], BF16)
        hTt = es.enter_context(k.nc.sbuf_tensor("hT6", [128, 32, 1024], BF16))
        hTb = Buf(hTt, "hT6")
        ssq = k.sb(es, "ssq6", [128, 1], F32)
        rstd = k.sb(es, "rstd6", [128, 1], F32)
        gam = k.sb(es, "gam6", [128, 32], F32)
        k.dma("sp", gam[:, :], X["norm_ffn"][:, :], [X["norm_ffn"]], [gam], gam)
        sg = [k.sb(es, f"sg{i}", [128, 512], BF16) for i in range(8)]
        ab = [k.sb(es, f"ab{i}", [128, 512], BF16) for i in range(4)]
        ring = WRing(k, es, npieces=4)
        wg = X["w_gate"].t.rearrange("(c p) n -> p c n", p=128)
        wu = X["w_up"].t.rearrange("(c p) n -> p c n", p=128)
        jobs = []
        st = dict(g=0, a=0, sgmap={})
        for t in range(C["n_own_tiles"] // 2):
            off = 0
            while off < FFN:
                n_ = min(512, FFN - off)

                def gfac(o_, t=t):
                    def evac(q, ps, ncols):
                        s_ = sg[st["g"] % 8]; st["g"] += 1
                        st["sgmap"][(t, o_, q)] = s_
                        k.op("act", lambda e: e.activation(out=s_[:, :], in_=ps[:, 0:512], func=AF.Silu), [ps], [s_])
                    return evac

                def ufac(o_, t=t, off=off):
                    def evac(q, ps, ncols):
                        s_ = st["sgmap"].pop((t, o_, q))
                        a_ = ab[st["a"] % 4]; st["a"] += 1
                        k.op("dve", lambda e: e.tensor_tensor(out=a_[:, :], in0=ps[:, 0:512], in1=s_[:, :], op=ALU.mult),
                             [ps, s_], [a_])
                        r = off + (q // 2) * 128
                        c_ = t * 1024 + (q % 2) * 512
                        k.dma("sp", X["aT"][r:r + 128, c_:c_ + 512], a_[:, :], [a_], [X["aT"]], X["aT"])
                    return evac
                jobs.append(dict(pass_id=t, mode="T", wbuf=X["w_gate"], wv=wg, KC=32, c0=off, ncols=n_, hT=(hTt, hTb),
                                 evac=gfac(0), nsub=8))
                jobs.append(dict(pass_id=t, mode="T", wbuf=X["w_up"], wv=wu, KC=32, c0=off, ncols=n_, hT=(hTt, hTb),
                                 evac=ufac(0), nsub=8))
                off += n_
        linear(k, ring, jobs, P, lambda pid: build_hT(k, X["out"], pid * 1024, xt, hTt, hTb, ssq, rstd, junk, gam, ident, cst, P[6:8], n_sub=8))
    k.barrier()


def phase6b(k, X, P, C):
    with ExitStack() as es:
        aTt = es.enter_context(k.nc.sbuf_tensor("aT6", [128, 86, 512], BF16))
        aTb = Buf(aTt, "aT6")
        xr = [k.sb(es, f"xr6{i}", [128, 512], F32) for i in range(4)]
        ot = [k.sb(es, f"ot6{i}", [128, 512], F32) for i in range(4)]
        ring = WRing(k, es, npieces=4)
        wd = X["w_down"].t.rearrange("(c p) n -> p c n", p=128)
        av = X["aT"].t.rearrange("(c p) t -> p c t", p=128)
        jobs = []
        ctr = [0]
        for t in range(C["n_own_tiles"]):
            mk_jobs(jobs, t, "N", X["w_down"], wd, 86, 0, D, (aTt, aTb),
                    resid_evac_factory(k, X, X["out"], t * 512, xr, ot, ctr))

        def prep(t):
            for c0 in range(0, 86, 16):
                n_ = min(16, 86 - c0)
                k.dma("sp", aTt[:, c0:c0 + n_, :], av[:, c0:c0 + n_, t * 512:(t + 1) * 512], [X["aT"]], [aTb], aTb)
        linear(k, ring, jobs, P, prep)
    k.barrier()


def build(dbg=False, stop=99, n_all_tiles=8, n_own_tiles=4, n_chunks=32, n_gdn_heads=16, n_own_blocks=16, n_key_blocks=32,
          phases=None, ext_in=(), ext_out=()):
    nc = bass.Bass("TRN2", target_bir_lowering=False)
    k = K(nc)
    shapes = {}

    class LazyX(dict):
        def __missing__(self, name):
            b = Buf(nc.dram_tensor(name, list(shapes[name]), F32, kind="ExternalInput").ap(), name)
            b.dram = True
            self[name] = b
            return b
    X = LazyX()

    def inp(name, shape):
        shapes[name] = shape
        if phases is None:
            X[name]
    inp("x_all", [S, D]); inp("x_own", [NOWN, D]); inp("mem", [256, D])
    inp("norm_mix", [128, 32]); inp("w_in", [D, DIN]); inp("consts", [128, 512]); inp("consts2", [16, 2048])
    inp("cw", [128, 48, 4]); inp("a_log", [16]); inp("dt_bias", [16])
    inp("gdn_norm", [128]); inp("att_q_norm", [128]); inp("att_k_norm", [128])
    inp("w_out", [D, D]); inp("norm_mem_q", [128, 32]); inp("norm_mem_kv", [128, 32])
    inp("w_mem_q", [D, 512]); inp("w_mem_kv", [D, 1024]); inp("mem_q_norm", [128]); inp("mem_k_norm", [128])
    inp("w_mem_o", [512, D]); inp("norm_ffn", [128, 32]); inp("w_gate", [D, FFN]); inp("w_up", [D, FFN]); inp("w_down", [FFN, D])
    inp("cs_all", [S, 32]); inp("cs_own", [NOWN, 32]); inp("pos_own", [NOWN, 1]); inp("kpos", [256]); inp("jsel", [128, 1])

    def scr(name, shape, dtype):
        kind = "ExternalInput" if name in ext_in else ("ExternalOutput" if name in ext_out else "Internal")
        X[name] = k.dram(name, shape, dtype, kind)
    scr("qkvT", [6144, S], F32)
    scr("gab", [S, 32], F32)
    scr("akv", [S, 1024], F32)
    scr("ikr", [S, 128], F32)
    scr("gz", [NOWN, 2048], F32)
    scr("aq", [NOWN, 2048], F32)
    scr("iq", [NOWN, 4096], F32)
    scr("iw", [NOWN, 32], F32)
    scr("oscr", [S, 2048], F32)
    scr("mixT", [D, NOWN], BF16)
    scr("aT", [FFN, NOWN], BF16)
    if "out" in ext_in:
        X["out_in"] = k.dram("out_in", [NOWN, D], F32, "ExternalInput")
    X["out"] = k.dram("out", [NOWN, D], F32, "ExternalOutput")
    with ExitStack() as es:
        P = [k.ps(es, f"P{i}") for i in range(8)]
        cst = k.sb(es, "cst", [128, 512], F32)
        k.dma("sp", cst[:, :], X["consts"][:, :], [X["consts"]], [cst], cst)
        C = dict(n_all_tiles=n_all_tiles, n_own_tiles=n_own_tiles, n_chunks=n_chunks, n_gdn_heads=n_gdn_heads,
                 n_own_blocks=n_own_blocks, n_key_blocks=n_key_blocks)
        C["ident"] = cst.t[:, 0:128]
        C["cst"] = cst
        allph = [phase1, phase2, phase3, phase4, phase5, phase6a, phase6b]
        for i, ph in enumerate(allph):
            if (phases is None and i < stop) or (phases is not None and (i + 1) in phases):
                ph(k, X, P, C)
                print("phase", i + 1, "instructions", k.nins, "waits", k.nwait, flush=True)
        k.finish([X["out"]])
    print("instructions", k.nins, "waits", k.nwait, "dma sems", len(k.dsems))
    return nc


def _consts():
    c = np.zeros((128, 512), np.float32)
    c[:, 0:128] = np.eye(128)
    p = np.arange(128)[:, None]; f = np.arange(128)[None, :]
    c[:, 128:256] = (p <= f)
    c[:, 256:384] = (p < f)
    c[:, 384:512] = 1.0
    c2 = np.zeros((16, 2048), np.float32)
    for h in range(16):
        c2[h, h * 128:(h + 1) * 128] = 1.0
    return c, c2


def _rot_table(pos):
    half = 16
    inv_freq = (np.float32(500000.0) ** (-np.arange(half, dtype=np.float32) * np.float32(2.0) / np.float32(32))).astype(np.float32)
    ang = pos.astype(np.float32)[:, None] * inv_freq[None, :]
    return np.concatenate([np.cos(ang), np.sin(ang)], axis=1).astype(np.float32)


def make_inputs(inp, core):
    b = core // 2
    j = core % 2
    f = lambda a: np.ascontiguousarray(a, dtype=np.float32)
    x = inp["x"][b]
    own_blocks = [2 * n + j for n in range(16)]
    own_rows = np.concatenate([np.arange(g * 128, (g + 1) * 128) for g in own_blocks])
    c, c2 = _consts()
    g32 = lambda v: f(v.reshape(32, 128).T)
    cwv = inp["conv_w"][0]
    cw = f(cwv.reshape(4, 3, 16, 128).transpose(3, 1, 2, 0).reshape(128, 48, 4))
    cs_all = _rot_table(np.arange(S))
    m = dict(
        x_all=f(x), x_own=f(x[own_rows]), mem=f(inp["mem"][b]),
        norm_mix=g32(inp["norm_mix"][0]), w_in=f(inp["w_in"][0]), consts=c, consts2=c2, cw=cw,
        a_log=f(inp["a_log"][0]), dt_bias=f(inp["dt_bias"][0]),
        gdn_norm=f(inp["gdn_norm"][0]), att_q_norm=f(inp["att_q_norm"][0]), att_k_norm=f(inp["att_k_norm"][0]),
        w_out=f(inp["w_out"][0]), norm_mem_q=g32(inp["norm_mem_q"][0]), norm_mem_kv=g32(inp["norm_mem_kv"][0]),
        w_mem_q=f(inp["w_mem_q"][0]), w_mem_kv=f(inp["w_mem_kv"][0]),
        mem_q_norm=f(inp["mem_q_norm"][0]), mem_k_norm=f(inp["mem_k_norm"][0]),
        w_mem_o=f(inp["w_mem_o"][0]), norm_ffn=g32(inp["norm_ffn"][0]),
        w_gate=f(inp["w_gate"][0]), w_up=f(inp["w_up"][0]), w_down=f(inp["w_down"][0]),
        cs_all=cs_all, cs_own=f(cs_all[own_rows]), pos_own=f(own_rows.astype(np.float32)[:, None]),
        kpos=f(np.arange(256)), jsel=np.full((128, 1), float(j), np.float32),
    )
    return m, own_rows


def kernel(**inputs):
    inp = {k_: np.asarray(v) for k_, v in inputs.items()}
    nc = build()
    maps = []
    rows = []
    for core in range(8):
        m, r = make_inputs(inp, core)
        maps.append(m)
        rows.append(r)
    res = run_bass_kernel_spmd(nc, maps, core_ids=list(range(8)))
    out = np.zeros((4, S, D), np.float32)
    for core in range(8):
        out[core // 2][rows[core]] = res.results[core]["out"]
    return out
```
